# Optimizing a Trainium2 kernel written in Bass

```python
import math
import jax, jax.numpy as jnp
from jax import lax
import numpy as np

D_MODEL = 1024
BATCH = 8
SEQ = 2048
DEPTH = 4
DEC_BATCH = 32
DEC_SEQ = 16
PAST_LEN = 4096

CHUNK = 64
N_A_LAYERS = DEPTH // 2
N_B_LAYERS = DEPTH - N_A_LAYERS
SSM_EXPAND = 2
SSM_INNER = SSM_EXPAND * D_MODEL
SSM_HEAD_DIM = 64
SSM_HEADS = SSM_INNER // SSM_HEAD_DIM
SSM_STATE = 128
SSM_GROUPS = 8
SSM_HEADS_PER_GROUP = SSM_HEADS // SSM_GROUPS
CONV_WIDTH = 4
CONV_DIM = SSM_INNER + 2 * SSM_GROUPS * SSM_STATE
SSM_IN_DIM = SSM_INNER + CONV_DIM + SSM_HEADS
WINDOW = 128
WIN_CHUNKS = WINDOW // CHUNK
ATTN_HEADS = 16
KV_HEADS = 4
ATTN_HEAD_DIM = 64
Q_PER_KV = ATTN_HEADS // KV_HEADS
KV_DIM = KV_HEADS * ATTN_HEAD_DIM
PEER_HEADS = 8
PEER_NKEYS = 128
PEER_EXPERTS = PEER_NKEYS * PEER_NKEYS
PEER_QDIM = 256
PEER_HALF = PEER_QDIM // 2
PEER_TOPK = 16
PEER_TOKEN_BLOCK = 128
EPS = 1e-6

kernel_name = 'yoco_ssd_swa_sink_peer_stream_step'


def rmsnorm(x, g):
    xf = x.astype(jnp.float32)
    y = xf * lax.rsqrt(jnp.mean(xf * xf, axis=-1, keepdims=True) + EPS)
    return (y * g.astype(jnp.float32)).astype(x.dtype)


def causal_conv(xbc, prev, w, b):
    L = xbc.shape[1]
    xp = jnp.concatenate([prev.astype(xbc.dtype), xbc], axis=1)
    out = b
    for k in range(CONV_WIDTH):
        out = out + xp[:, k:k + L] * w[k]
    return jax.nn.silu(out), xp[:, -(CONV_WIDTH - 1):]


def ssd_scan(x, dt, A, B, C, s0, block):
    f32 = jnp.float32
    bsz, L = x.shape[:2]
    nc, Q, G, R = L // block, block, SSM_GROUPS, SSM_HEADS_PER_GROUP
    x = x.astype(f32).reshape(bsz, nc, Q, G, R, SSM_HEAD_DIM)
    dt = dt.astype(f32).reshape(bsz, nc, Q, G, R)
    B = B.astype(f32).reshape(bsz, nc, Q, G, SSM_STATE)
    C = C.astype(f32).reshape(bsz, nc, Q, G, SSM_STATE)
    s0 = s0.astype(f32).reshape(bsz, G, R, SSM_HEAD_DIM, SSM_STATE)
    acum = jnp.cumsum(dt * A.reshape(G, R), axis=2)
    seg = acum[:, :, :, None] - acum[:, :, None, :]
    causal = jnp.tril(jnp.ones((Q, Q), bool))[:, :, None, None]
    lmat = jnp.exp(jnp.where(causal, seg, -jnp.inf))
    xdt = x * dt[..., None]
    cb = jnp.einsum('bcign,bcjgn->bcijg', C, B)
    y_diag = jnp.einsum('bcijgr,bcjgrp->bcigrp', cb[..., None] * lmat, xdt)
    decay_end = jnp.exp(acum[:, :, -1:] - acum)
    states = jnp.einsum('bcjgn,bcjgrp->bcgrpn', B, xdt * decay_end[..., None])
    chunk_decay = jnp.exp(acum[:, :, -1])

    def step(s, inp):
        dec, st = inp
        return dec[..., None, None] * s + st, s

    s_fin, s_prev = lax.scan(step, s0, (jnp.moveaxis(chunk_decay, 1, 0), jnp.moveaxis(states, 1, 0)))
    s_prev = jnp.moveaxis(s_prev, 0, 1)
    y_off = jnp.einsum('bcign,bcgrpn->bcigrp', C, s_prev) * jnp.exp(acum)[..., None]
    y = (y_diag + y_off).reshape(bsz, L, SSM_HEADS, SSM_HEAD_DIM)
    return y, s_fin.reshape(bsz, SSM_HEADS, SSM_HEAD_DIM, SSM_STATE)


def mamba_mixer(xn, ssm_prev, conv_prev, w_in, conv_w, conv_b, dt_bias, a_log, d_skip, norm_g, w_out):
    bsz, L, _ = xn.shape
    f32 = jnp.float32
    proj = xn @ w_in
    z = proj[..., :SSM_INNER]
    xbc = proj[..., SSM_INNER:SSM_INNER + CONV_DIM]
    dt_raw = proj[..., SSM_INNER + CONV_DIM:]
    xbc, conv_new = causal_conv(xbc, conv_prev, conv_w, conv_b)
    gn = SSM_GROUPS * SSM_STATE
    xs = xbc[..., :SSM_INNER].reshape(bsz, L, SSM_HEADS, SSM_HEAD_DIM)
    Bm = xbc[..., SSM_INNER:SSM_INNER + gn].reshape(bsz, L, SSM_GROUPS, SSM_STATE)
    Cm = xbc[..., SSM_INNER + gn:].reshape(bsz, L, SSM_GROUPS, SSM_STATE)
    dt = jax.nn.softplus(dt_raw.astype(f32) + dt_bias.astype(f32))
    A = -jnp.exp(a_log.astype(f32))
    y, s_new = ssd_scan(xs, dt, A, Bm, Cm, ssm_prev, min(CHUNK, L))
    y = y + xs.astype(f32) * d_skip.astype(f32)[:, None]
    y = y.reshape(bsz, L, SSM_INNER) * jax.nn.silu(z.astype(f32))
    y = rmsnorm(y, norm_g).astype(xn.dtype)
    return y @ w_out, s_new.astype(ssm_prev.dtype), conv_new


def sink_softmax(scores, sinks):
    s = scores.astype(jnp.float32)
    sink = jnp.broadcast_to(sinks.astype(jnp.float32).reshape(KV_HEADS, Q_PER_KV, 1, 1), s.shape[:-1] + (1,))
    p = jax.nn.softmax(jnp.concatenate([s, sink], axis=-1), axis=-1)
    return p[..., :-1]


def swa_prompt(q, k, v, sinks):
    bsz, L = q.shape[:2]
    nc = L // CHUNK
    nk = (WIN_CHUNKS + 1) * CHUNK
    qb = q.reshape(bsz, nc, CHUNK, KV_HEADS, Q_PER_KV, ATTN_HEAD_DIM)
    pad = ((0, 0), (WINDOW, 0), (0, 0), (0, 0))
    kp = jnp.pad(k, pad).reshape(bsz, nc + WIN_CHUNKS, CHUNK, KV_HEADS, ATTN_HEAD_DIM)
    vp = jnp.pad(v, pad).reshape(bsz, nc + WIN_CHUNKS, CHUNK, KV_HEADS, ATTN_HEAD_DIM)
    kb = jnp.concatenate([kp[:, o:o + nc] for o in range(WIN_CHUNKS + 1)], axis=2)
    vb = jnp.concatenate([vp[:, o:o + nc] for o in range(WIN_CHUNKS + 1)], axis=2)
    scale = ATTN_HEAD_DIM ** -0.5
    scores = jnp.einsum('bcqkgd,bcskd->bckgqs', qb, kb).astype(jnp.float32) * scale
    key_chunk = jnp.arange(nc)[:, None] - WIN_CHUNKS + (jnp.arange(nk) // CHUNK)[None, :]
    valid = (key_chunk >= 0)[None, :, None, None, None, :]
    probs = sink_softmax(jnp.where(valid, scores, -jnp.inf), sinks)
    out = jnp.einsum('bckgqs,bcskd->bcqkgd', probs.astype(v.dtype), vb)
    return out.reshape(bsz, L, ATTN_HEADS * ATTN_HEAD_DIM)


def swa_sample(q, k_all, v_all, sinks):
    bsz, T = q.shape[:2]
    qh = q.reshape(bsz, T, KV_HEADS, Q_PER_KV, ATTN_HEAD_DIM)
    scale = ATTN_HEAD_DIM ** -0.5
    scores = jnp.einsum('btkgd,bskd->bkgts', qh, k_all).astype(jnp.float32) * scale
    probs = sink_softmax(scores, sinks)
    out = jnp.einsum('bkgts,bskd->btkgd', probs.astype(v_all.dtype), v_all)
    return out.reshape(bsz, T, ATTN_HEADS * ATTN_HEAD_DIM)


def peer(xn, w_q, sub_k1, sub_k2, u_tab, v_tab):
    bsz, L, D = xn.shape
    T = bsz * L
    nblk = -(-T // PEER_TOKEN_BLOCK)
    xt = jnp.pad(xn.reshape(T, D), ((0, nblk * PEER_TOKEN_BLOCK - T), (0, 0)))
    xt = xt.reshape(nblk, PEER_TOKEN_BLOCK, D)

    def block_fn(xb):
        q = (xb @ w_q).reshape(PEER_TOKEN_BLOCK, PEER_HEADS, PEER_QDIM)
        s1 = jnp.einsum('thd,nd->thn', q[..., :PEER_HALF], sub_k1)
        s2 = jnp.einsum('thd,nd->thn', q[..., PEER_HALF:], sub_k2)
        v1, i1 = lax.top_k(s1, PEER_TOPK)
        v2, i2 = lax.top_k(s2, PEER_TOPK)
        cand = (v1[..., :, None] + v2[..., None, :]).reshape(PEER_TOKEN_BLOCK, PEER_HEADS, PEER_TOPK * PEER_TOPK)
        vs, ci = lax.top_k(cand, PEER_TOPK)
        e1 = jnp.take_along_axis(i1, ci // PEER_TOPK, axis=-1)
        e2 = jnp.take_along_axis(i2, ci % PEER_TOPK, axis=-1)
        eid = (e1 * PEER_NKEYS + e2).reshape(PEER_TOKEN_BLOCK, PEER_HEADS * PEER_TOPK)
        gate = jax.nn.softmax(vs.astype(jnp.float32), axis=-1).reshape(PEER_TOKEN_BLOCK, PEER_HEADS * PEER_TOPK)
        ue = jnp.take(u_tab, eid, axis=0)
        act = jax.nn.gelu(jnp.einsum('tkd,td->tk', ue, xb).astype(jnp.float32))
        ve = jnp.take(v_tab, eid, axis=0)
        return jnp.einsum('tk,tkd->td', (gate * act).astype(xb.dtype), ve)

    out = lax.map(block_fn, xt)
    return out.reshape(nblk * PEER_TOKEN_BLOCK, D)[:T].reshape(bsz, L, D)


def setup_inputs(seed: int = 0) -> dict:
    key = jax.random.key(seed)
    ks = iter(jax.random.split(key, 48))

    def nrm(shape, scale):
        return jax.random.normal(next(ks), shape, jnp.float32) * scale

    D = D_MODEL
    HQD = ATTN_HEADS * ATTN_HEAD_DIM
    u_dt = jax.random.uniform(next(ks), (N_A_LAYERS, SSM_HEADS), jnp.float32)
    dt0 = jnp.exp(u_dt * (math.log(0.1) - math.log(0.001)) + math.log(0.001))
    return {
        'x_prompt': nrm((BATCH, SEQ, D), 1.0),
        'x_sample': nrm((DEC_BATCH, DEC_SEQ, D), 1.0),
        'state_ssm': nrm((N_A_LAYERS, DEC_BATCH, SSM_HEADS, SSM_HEAD_DIM, SSM_STATE), 0.1),
        'state_conv': nrm((N_A_LAYERS, DEC_BATCH, CONV_WIDTH - 1, CONV_DIM), 1.0),
        'cache_k_win': nrm((DEC_BATCH, WINDOW, KV_HEADS, ATTN_HEAD_DIM), 1.0),
        'cache_v_win': nrm((DEC_BATCH, WINDOW, KV_HEADS, ATTN_HEAD_DIM), 1.0),
        'norm_mix': 1.0 + nrm((DEPTH, D), 0.05),
        'norm_ffn': 1.0 + nrm((DEPTH, D), 0.05),
        'norm_kv': 1.0 + nrm((D,), 0.05),
        'norm_final': 1.0 + nrm((D,), 0.05),
        'm_w_in': nrm((N_A_LAYERS, D, SSM_IN_DIM), D ** -0.5),
        'm_conv_w': nrm((N_A_LAYERS, CONV_WIDTH, CONV_DIM), CONV_WIDTH ** -0.5),
        'm_conv_b': nrm((N_A_LAYERS, CONV_DIM), 0.01),
        'm_dt_bias': dt0 + jnp.log(-jnp.expm1(-dt0)),
        'm_a_log': jnp.log(jax.random.uniform(next(ks), (N_A_LAYERS, SSM_HEADS), jnp.float32, 1.0, 16.0)),
        'm_d_skip': 1.0 + nrm((N_A_LAYERS, SSM_HEADS), 0.1),
        'm_norm': 1.0 + nrm((N_A_LAYERS, SSM_INNER), 0.05),
        'm_w_out': nrm((N_A_LAYERS, SSM_INNER, D), SSM_INNER ** -0.5),
        'a_w_kv': nrm((D, 2 * KV_DIM), D ** -0.5),
        'a_b_kv': nrm((2 * KV_DIM,), 0.01),
        'a_w_q': nrm((N_B_LAYERS, D, HQD), D ** -0.5),
        'a_b_q': nrm((N_B_LAYERS, HQD), 0.01),
        'a_sinks': nrm((N_B_LAYERS, ATTN_HEADS), 0.5),
        'a_w_o': nrm((N_B_LAYERS, HQD, D), HQD ** -0.5),
        'a_b_o': nrm((N_B_LAYERS, D), 0.01),
        'p_w_q': nrm((DEPTH, D, PEER_HEADS * PEER_QDIM), D ** -0.5),
        'p_sub_k1': nrm((DEPTH, PEER_NKEYS, PEER_HALF), PEER_HALF ** -0.5),
        'p_sub_k2': nrm((DEPTH, PEER_NKEYS, PEER_HALF), PEER_HALF ** -0.5),
        'p_u': nrm((DEPTH, PEER_EXPERTS, D), D ** -0.5),
        'p_v': nrm((DEPTH, PEER_EXPERTS, D), (PEER_HEADS * PEER_TOPK) ** -0.5),
    }


def reference(x_prompt, x_sample, state_ssm, state_conv, cache_k_win, cache_v_win,
              norm_mix, norm_ffn, norm_kv, norm_final,
              m_w_in, m_conv_w, m_conv_b, m_dt_bias, m_a_log, m_d_skip, m_norm, m_w_out,
              a_w_kv, a_b_kv, a_w_q, a_b_q, a_sinks, a_w_o, a_b_o,
              p_w_q, p_sub_k1, p_sub_k2, p_u, p_v):

    def run(h, ssm_init, conv_init, k_prev, v_prev):
        bsz, L, _ = h.shape
        ssm_out, conv_out = [], []
        k_new = v_new = k_all = v_all = None
        for l in range(DEPTH):
            xn = rmsnorm(h, norm_mix[l])
            if l < N_A_LAYERS:
                mix, s_new, c_new = mamba_mixer(xn, ssm_init[l], conv_init[l], m_w_in[l], m_conv_w[l],
                                                m_conv_b[l], m_dt_bias[l], m_a_log[l], m_d_skip[l],
                                                m_norm[l], m_w_out[l])
                ssm_out.append(s_new)
                conv_out.append(c_new)
            else:
                j = l - N_A_LAYERS
                if j == 0:
                    kv = rmsnorm(h, norm_kv) @ a_w_kv + a_b_kv
                    k_new = kv[..., :KV_DIM].reshape(bsz, L, KV_HEADS, ATTN_HEAD_DIM)
                    v_new = kv[..., KV_DIM:].reshape(bsz, L, KV_HEADS, ATTN_HEAD_DIM)
                    if k_prev is not None:
                        k_all = jnp.concatenate([k_prev.astype(k_new.dtype), k_new], axis=1)
                        v_all = jnp.concatenate([v_prev.astype(v_new.dtype), v_new], axis=1)
                q = xn @ a_w_q[j] + a_b_q[j]
                if k_prev is None:
                    o = swa_prompt(q, k_new, v_new, a_sinks[j])
                else:
                    o = swa_sample(q, k_all, v_all, a_sinks[j])
                mix = o @ a_w_o[j] + a_b_o[j]
            h = h + mix
            h = h + peer(rmsnorm(h, norm_ffn[l]), p_w_q[l], p_sub_k1[l], p_sub_k2[l], p_u[l], p_v[l])
        y = rmsnorm(h, norm_final)
        if k_prev is None:
            k_win, v_win = k_new[:, -WINDOW:], v_new[:, -WINDOW:]
        else:
            k_win, v_win = k_all[:, -WINDOW:], v_all[:, -WINDOW:]
        return y, jnp.stack(ssm_out), jnp.stack(conv_out), k_win, v_win

    zeros_ssm = jnp.zeros((N_A_LAYERS, x_prompt.shape[0], SSM_HEADS, SSM_HEAD_DIM, SSM_STATE), x_prompt.dtype)
    zeros_conv = jnp.zeros((N_A_LAYERS, x_prompt.shape[0], CONV_WIDTH - 1, CONV_DIM), x_prompt.dtype)
    y_prompt, pr_ssm, pr_conv, pr_k_win, pr_v_win = run(x_prompt, zeros_ssm, zeros_conv, None, None)
    y_sample, sm_ssm, sm_conv, sm_k_win, sm_v_win = run(x_sample, state_ssm, state_conv, cache_k_win, cache_v_win)
    return (y_prompt, y_sample, pr_ssm, pr_conv, pr_k_win, pr_v_win, sm_ssm, sm_conv, sm_k_win, sm_v_win)
```

```python
import os
import numpy as np
from contextlib import ExitStack
import concourse.bass as bass
import concourse.mybir as mybir
from concourse.bass_utils import run_bass_kernel_spmd

F32 = mybir.dt.float32
U32 = mybir.dt.uint32
ALU = mybir.AluOpType
ACT = mybir.ActivationFunctionType
AX = mybir.AxisListType

D = 1024
SEQ = 2048
NPT = 16
NSS = 4
TS = 16
NEG = -30000.0
ILV_MAXL = int(os.environ.get('K_ILVL', '4'))


class Buf:
    __slots__ = ("name", "wr", "rd", "dsem", "dcnt")

    def __init__(self, name):
        self.name = name
        self.wr = []
        self.rd = []
        self.dsem = None
        self.dcnt = 0


class FW:
    def __init__(self, nc, stack):
        self.nc = nc
        self.stack = stack
        self.engs = {"pe": nc.tensor, "dve": nc.vector, "act": nc.scalar,
                     "pool": nc.gpsimd, "sp": nc.sync}
        self.sems = {}
        self.cnt = {}
        self.seen = {k: {} for k in self.engs}
        for k in self.engs:
            self.sems[k] = stack.enter_context(nc.semaphore("s_" + k))
            self.cnt[k] = 0
        self.nbuf = 0
        self.ninst = 0
        self.nwait = 0
        self.allbufs = []
        self.stage = 0
        self.stop_at = int(os.environ.get('K_STAGE', '0'))

    def buf(self, name=None):
        self.nbuf += 1
        b = Buf(name or "b%d" % self.nbuf)
        self.allbufs.append(b)
        return b

    def sb(self, name, shape, dt=F32):
        t = self.stack.enter_context(self.nc.sbuf_tensor(name, list(shape), dt))
        return t, self.buf(name)

    def _dsem(self, b):
        if b.dsem is None:
            key = "ds%d" % len(self.sems)
            b.dsem = key
            self.sems[key] = self.stack.enter_context(self.nc.semaphore(key))
        return b.dsem

    def _wait(self, e, events):
        need = {}
        for (k, v) in events:
            if need.get(k, 0) < v:
                need[k] = v
        seen = self.seen[e]
        for k, v in need.items():
            if e == "pe" and k == "pe":
                continue
            if seen.get(k, 0) >= v:
                continue
            self.engs[e].wait_ge(self.sems[k], v)
            self.nwait += 1
            seen[k] = v

    def _deps(self, e, reads, writes):
        ev = []
        for b in reads:
            ev.extend(b.wr)
        for b in writes:
            ev.extend(b.wr)
            ev.extend(b.rd)
        self._wait(e, ev)

    def _commit(self, event, reads, writes):
        for b in reads:
            b.rd.append(event)
            if len(b.rd) > 48:
                m = {}
                for (k, v) in b.rd:
                    if m.get(k, 0) < v:
                        m[k] = v
                b.rd = list(m.items())
        for b in writes:
            b.wr = [event]
            b.rd = []

    def op(self, e, fn, reads=(), writes=()):
        self._deps(e, reads, writes)
        ins = fn(self.engs[e])
        self.cnt[e] += 1
        ins.then_inc(self.sems[e], 1)
        self._commit((e, self.cnt[e]), reads, writes)
        self.ninst += 1
        return ins

    def dma(self, q, out, in_, reads=(), writes=(), owner=None, fn=None, slow=False):
        self._deps(q, reads, writes)
        key = self._dsem(owner)
        if fn is not None:
            ins = fn(self.engs[q])
        elif slow:
            ins = self.engs[q].dma_start(out=out, in_=in_, allow_slow_non_contiguous=True)
        else:
            ins = self.engs[q].dma_start(out=out, in_=in_)
        owner.dcnt += 16
        ins.then_inc(self.sems[key], 16)
        self._commit((key, owner.dcnt), reads, writes)
        self.ninst += 1
        return ins

    def mark(self, n):
        if self.stop_at and n >= self.stop_at:
            raise Stop()

    def finish(self, bufs, e="sp"):
        ev = []
        for b in bufs:
            ev.extend(b.wr)
            ev.extend(b.rd)
        self._wait(e, ev)


class Stop(Exception):
    pass


def bc(ap, shape):
    return ap.to_broadcast(list(shape))


def build_program(debug=False, nlayers=4, ntiles_p=NPT, nseq_s=NSS):
    nc = bass.Bass("TRN2", target_bir_lowering=False)

    def din(name, shape, dt=F32):
        return nc.dram_tensor(name, list(shape), dt, kind="ExternalInput").ap()

    def dout(name, shape, dt=F32):
        return nc.dram_tensor(name, list(shape), dt, kind="ExternalOutput").ap()

    x_p = din("x_p", [SEQ, D])
    x_s = din("x_s", [NSS * TS, D])
    st_ssm = din("st_ssm", [2, NSS, 2048, 128])
    st_conv = din("st_conv", [2, NSS, 3, 4096])
    ck = din("ck", [NSS, 128, 256])
    cv = din("cv", [NSS, 128, 256])
    norm_mix = din("norm_mix", [4, D])
    norm_ffn = din("norm_ffn", [4, D])
    norm_kv = din("norm_kv", [1, D])
    norm_final = din("norm_final", [1, D])
    m_w_in = din("m_w_in", [2, D, 6176])
    m_conv_w = din("m_conv_w", [2, 128, 32, 4])
    m_conv_b = din("m_conv_b", [2, 128, 32])
    m_dt_bias = din("m_dt_bias", [2, 32])
    m_a_log = din("m_a_log", [2, 32])
    m_d_skip = din("m_d_skip", [2, 32])
    m_norm = din("m_norm", [2, 2048])
    m_w_out = din("m_w_out", [2, 2048, D])
    a_w_kv = din("a_w_kv", [D, 512])
    a_b_kv = din("a_b_kv", [1, 512])
    a_b_kT = din("a_b_kT", [64, 4])
    a_w_q = din("a_w_q", [2, D, D])
    a_b_qT = din("a_b_qT", [2, 64, 16])
    a_sinks = din("a_sinks", [2, 16])
    a_w_o = din("a_w_o", [2, D, D])
    a_b_o = din("a_b_o", [2, D])
    p_w_q = din("p_w_q", [4, D, 2048])
    p_k1T = din("p_k1T", [4, 128, 128])
    p_k2T = din("p_k2T", [4, 128, 128])
    p_u = din("p_u", [4 * 16384, D])
    p_v = din("p_v", [4 * 16384, D])

    y_p = dout("y_p", [SEQ, D])
    y_s = dout("y_s", [NSS * TS, D])
    o_pssm = dout("o_pssm", [2, 2048, 128])
    o_pconv = dout("o_pconv", [2, 3, 4096])
    o_pk = dout("o_pk", [128, 256])
    o_pv = dout("o_pv", [128, 256])
    o_sssm = dout("o_sssm", [2, NSS, 2048, 128])
    o_sconv = dout("o_sconv", [2, NSS, 3, 4096])
    o_sk = dout("o_sk", [NSS, 128, 256])
    o_sv = dout("o_sv", [NSS, 128, 256])

    NTOK = SEQ + NSS * TS
    hck = []
    for i in range(2 * 4):
        if debug:
            hck.append(dout("hck%d" % i, [NTOK, D]))
        else:
            hck.append(nc.dram_tensor("hck%d" % i, [NTOK, D], F32).ap())
    kT_d = nc.dram_tensor("kT_d", [NPT + NSS, 64, 4, 128], F32).ap()
    v_d = nc.dram_tensor("v_d", [NPT + NSS, 128, 256], F32).ap()

    with ExitStack() as st:
        fw = FW(nc, st)
        op = fw.op

        class Tile:
            pass
        tiles = []
        for i in range(ntiles_p):
            t = Tile()
            t.kind, t.idx, t.nt, t.row0 = "p", i, 128, i * 128
            t.chunks = [(0, 64), (64, 64)]
            tiles.append(t)
        for s in range(nseq_s):
            t = Tile()
            t.kind, t.idx, t.nt, t.row0 = "s", s, TS, SEQ + s * TS
            t.chunks = [(0, TS)]
            tiles.append(t)
        for t in tiles:
            t.hb = [fw.buf("h%d_%s%d" % (i, t.kind, t.idx)) for i in range(9)]
            t.kvslot = NPT + t.idx if t.kind == "s" else t.idx
            t.kTd_b = fw.buf()
            t.vd_b = fw.buf()

        speer = Tile()
        speer.kind, speer.idx, speer.nt, speer.row0 = "sp", 0, nseq_s * TS, SEQ
        stiles = [t for t in tiles if t.kind == "s"]

        def hbl(t, ci):
            if t.kind == "sp":
                return [s_.hb[ci] for s_ in stiles]
            return [t.hb[ci]]

        def h_ap(ci, t):
            if ci == 0:
                return x_p[t.row0:t.row0 + t.nt, :] if t.kind == "p" else x_s[t.idx * TS:(t.idx + 1) * TS, :]
            return hck[ci - 1][t.row0:t.row0 + t.nt, :]

        P = []
        for i in range(8):
            P.append(fw.sb("P%d" % i, [128, 2048]))
        xbc_t, xbc_b = fw.sb("xbc", [128, 32, 67])
        halo_t, halo_b = fw.sb("halo", [128, 32, 3])
        xnT_t, xnT_b = fw.sb("xnT", [128, 8, 128])
        ht_t, ht_b = fw.sb("ht", [128, 1024])
        xn_t, xn_b = P[4][0][:, 0:1024], P[4][1]
        W = [fw.sb("W%d" % i, [128, 4096]) for i in range(2)]
        NG = 8
        G = [fw.sb("G%d" % i, [128, 1024]) for i in range(NG)]
        ST_t, ST_b = fw.sb("ST", [128, 2048])
        kTr = [fw.sb("kTr%d" % i, [64, 4, 128]) for i in range(2)]
        vr = [fw.sb("vr%d" % i, [128, 256]) for i in range(2)]
        gmix_t, gmix_b = fw.sb("gmix", [128, 1024])
        gffn_t, gffn_b = fw.sb("gffn", [128, 1024])
        gX_t, gX_b = fw.sb("gX", [128, 2048])
        ident_t, ident_b = fw.sb("ident", [128, 128])
        tri_t, tri_b = fw.sb("tri", [64, 64])
        negm_t, negm_b = fw.sb("negm", [64, 64])
        ones_t, ones_b = fw.sb("ones", [64, 128])
        iota_t, iota_b = fw.sb("iota16", [128, 16])
        thr_t, thr_b = fw.sb("thr15", [128, 15])
        cst_t, cst_b = fw.sb("cst", [128, 4])
        lc_t, lc_b = fw.sb("lc", [128, 320])
        amask_t, amask_b = fw.sb("amask", [128, 384])
        sk_t, sk_b = fw.sb("subk", [128, 2, 128])
        sm_t, sm_b = fw.sb("small", [128, 1024])
        bo_t, bo_b = fw.sb("bo", [128, 1024])
        HTP = [fw.sb("htp%d" % i, [128, 1024]) for i in range(2)]
        WGT = [fw.sb("wgt%d" % i, [128, 128]) for i in range(2)]
        EID = [fw.sb("eid%d" % i, [128, 128], U32) for i in range(2)]
        peer_cnt = [0]
        xnp_t, xnp_b = fw.sb("xnp", [128, 1024])
        psm_t, psm_b = fw.sb("psm", [128, 1024])
        idx_t, idx_b = fw.sb("idxu", [128, 384], U32)

        PS = []
        for i in range(8):
            t_ = st.enter_context(nc.psum_tensor("ps%d" % i, [128, 512], F32))
            PS.append((t_, fw.buf("ps%d" % i)))
        ps_i = [0]

        def psum():
            r = PS[ps_i[0] % 8]
            ps_i[0] += 1
            return r

        w_i = [0]

        def wslot():
            r = W[w_i[0] % 2]
            w_i[0] += 1
            return r

        cb_ = cst_b
        op("pool", lambda e: e.memset(cst_t[:, 0:1], 1e-6), writes=[cb_])
        op("pool", lambda e: e.memset(cst_t[:, 1:2], 1.0), writes=[cb_])
        op("pool", lambda e: e.memset(cst_t[:, 2:4], 0.0), writes=[cb_])
        EPS = cst_t[:, 0:1]
        ONE = cst_t[:, 1:2]
        op("pool", lambda e: e.memset(ident_t[:], 0.0), writes=[ident_b])
        op("pool", lambda e: e.affine_select(out=ident_t[:], in_=ident_t[:], pattern=[[-1, 128]],
                                             compare_op=ALU.not_equal, fill=1.0, base=0,
                                             channel_multiplier=1), reads=[ident_b], writes=[ident_b])
        op("pool", lambda e: e.memset(tri_t[:], 1.0), writes=[tri_b])
        op("pool", lambda e: e.affine_select(out=tri_t[:], in_=tri_t[:], pattern=[[1, 64]],
                                             compare_op=ALU.is_ge, fill=0.0, base=0,
                                             channel_multiplier=-1), reads=[tri_b], writes=[tri_b])
        op("pool", lambda e: e.memset(negm_t[:], 0.0), writes=[negm_b])
        op("pool", lambda e: e.affine_select(out=negm_t[:], in_=negm_t[:], pattern=[[1, 64]],
                                             compare_op=ALU.is_ge, fill=NEG, base=0,
                                             channel_multiplier=-1), reads=[negm_b], writes=[negm_b])
        op("pool", lambda e: e.memset(ones_t[:], 1.0), writes=[ones_b])
        op("pool", lambda e: e.iota(iota_t[:], pattern=[[1, 16]], base=0, channel_multiplier=0,
                                    allow_small_or_imprecise_dtypes=True), writes=[iota_b])
        op("pool", lambda e: e.iota(thr_t[:], pattern=[[16, 15]], base=16, channel_multiplier=0,
                                    allow_small_or_imprecise_dtypes=True), writes=[thr_b])
        op("pool", lambda e: e.memset(amask_t[:], 0.0), writes=[amask_b])
        op("pool", lambda e: e.memset(amask_t[0:64, 192:256], NEG), reads=[amask_b], writes=[amask_b])
        op("pool", lambda e: e.memset(amask_t[64:128, 0:64], NEG), reads=[amask_b], writes=[amask_b])
        op("pool", lambda e: e.memset(amask_t[0:64, 256 + 64:256 + 128], NEG), reads=[amask_b], writes=[amask_b])

        def rmsnorm(src_t, src_b, nt, g_ap, g_b, dst_t, dst_b, ss_ap, ss_b):
            op("dve", lambda e: e.memset(ss_ap[:, 0:1], 0.0), writes=[ss_b])
            op("act", lambda e: e.activation(out=dst_t[0:nt, :], in_=src_t[0:nt, :], func=ACT.Square,
                                             accum_out=ss_ap[:, 0:1]), reads=[src_b, ss_b], writes=[dst_b, ss_b])
            op("act", lambda e: e.activation(out=ss_ap[:, 1:2], in_=ss_ap[:, 0:1], func=ACT.Sqrt,
                                             scale=1.0 / D, bias=EPS[0:nt, :]), reads=[ss_b, cst_b], writes=[ss_b])
            op("dve", lambda e: e.reciprocal(out=ss_ap[:, 2:3], in_=ss_ap[:, 1:2]), reads=[ss_b], writes=[ss_b])
            op("dve", lambda e: e.scalar_tensor_tensor(out=dst_t[0:nt, :], in0=src_t[0:nt, :], scalar=ss_ap[:, 2:3],
                                                        in1=g_ap[0:nt, :], op0=ALU.mult, op1=ALU.mult),
               reads=[src_b, ss_b, g_b], writes=[dst_b])

        def to_featmajor(src_t, src_b, nt, ncols, dstT, dstT_b, tok0=0, eng_alt=True):
            nb = ncols // 128
            for b0 in range(0, nb, 4):
                pt, pb = psum()
                nn = min(4, nb - b0)
                for c in range(nn):
                    op("pe", lambda e, c=c: e.transpose(out=pt[:, c * 128:c * 128 + nt],
                                                        in_=src_t[0:nt, (b0 + c) * 128:(b0 + c + 1) * 128],
                                                        identity=ident_t[0:nt, 0:nt]),
                       reads=[src_b, ident_b], writes=[pb])
                src_v = pt[:, 0:nn * 128].rearrange("p (c n) -> p c n", c=nn)[:, :, 0:nt]
                eng = "act" if (b0 // 4) % 2 == 0 else "dve"
                if eng == "act":
                    op("act", lambda e: e.copy(out=dstT[:, b0:b0 + nn, tok0:tok0 + nt], in_=src_v),
                       reads=[pb], writes=[dstT_b])
                else:
                    op("dve", lambda e: e.tensor_copy(out=dstT[:, b0:b0 + nn, tok0:tok0 + nt], in_=src_v),
                       reads=[pb], writes=[dstT_b])

        def load_w(w_ap, nk, c0, ncols, prows=128):
            wt, wb = wslot()
            view = wt[0:prows, 0:nk * ncols].rearrange("p (k n) -> p k n", k=nk)
            src = w_ap[:, c0:c0 + ncols].rearrange("(k p) n -> p k n", p=prows)
            fw.dma("sp", view, src, writes=[wb], owner=wb)
            return view, wb

        def load_bc(dst_ap, dst_b, src_row_ap, nparts=128):
            fw.dma("sp", dst_ap, src_row_ap.partition_broadcast(nparts), writes=[dst_b], owner=dst_b)

        sm = sm_t
        smb = sm_b

        def mamba_tile(l, t, ci):
            nt = t.nt
            first_tile = (t.kind == "s") or (t.idx == 0)
            last_tile = (t.kind == "s") or (t.idx == NPT - 1)
            fw.dma("sp", ht_t[0:nt, :], h_ap(ci, t), reads=[t.hb[ci]], writes=[ht_b], owner=ht_b)
            rmsnorm(ht_t, ht_b, nt, gmix_t, gmix_b, xn_t, xn_b, sm[0:nt, 0:4], smb)
            to_featmajor(xn_t, xn_b, nt, D, xnT_t, xnT_b)
            fw.mark(1)
            yield
            w_in = m_w_in[l]
            if first_tile:
                if t.kind == "p":
                    op("pool", lambda e: e.memset(ST_t[:], 0.0), writes=[ST_b])
                    op("pool", lambda e: e.memset(halo_t[:], 0.0), writes=[halo_b])
                else:
                    for c0 in range(32):
                        src = st_conv[l, t.idx][:, c0 * 128:(c0 + 1) * 128].rearrange("r p -> p r")
                        fw.dma("sp", halo_t[:, c0, :], src, writes=[halo_b], owner=halo_b, slow=True)
                    stg_t, stg_b = P[4]
                    fw.dma("sp", stg_t[:].rearrange("p (m n) -> p m n", m=16),
                           st_ssm[l, t.idx].rearrange("(m p) n -> p m n", p=128), writes=[stg_b], owner=stg_b)
                    to_featmajor(stg_t, stg_b, 128, 2048, ST_t[:].rearrange("p (m n) -> p m n", m=16), ST_b)

            ynT_t, ynT_b = P[7]
            ynT = ynT_t[:].rearrange("p (k n) -> p k n", k=16)
            for (q0, Q) in t.chunks:
                xc_t, xc_b = P[0]
                xt_t, xt_b = P[1]
                zs_t, zs_b = P[2]
                ab_t, ab_b = P[3]
                lm_t, lm_b = P[4]
                xd_t, xd_b = P[5]
                bt_t, bt_b = P[6]
                xc = xc_t[:, 0:32 * Q].rearrange("p (c n) -> p c n", c=32)
                abc = ab_t[:, 0:32 * Q].rearrange("p (c n) -> p c n", c=32)
                lm = lm_t[0:Q, 0:32 * Q].rearrange("p (c n) -> p c n", c=32)
                xbe = xbc_t[:, :, 0:Q + 3]
                op("pool", lambda e: e.tensor_copy(out=xbc_t[:, :, 0:3], in_=halo_t[:]), reads=[halo_b], writes=[xbc_b])
                for blk in range(8):
                    wv, wb = load_w(w_in, 8, 2048 + blk * 512, 512)
                    pt, pb = psum()
                    for cc in range(4):
                        for k in range(8):
                            op("pe", lambda e, cc=cc, k=k: e.matmul(pt[:, cc * Q:(cc + 1) * Q], lhsT=wv[:, k, cc * 128:(cc + 1) * 128],
                                                                     rhs=xnT_t[:, k, q0:q0 + Q], start=(k == 0), stop=(k == 7)),
                               reads=[wb, xnT_b], writes=[pb])
                        yield
                    op("act", lambda e: e.copy(out=xbc_t[:, blk * 4:blk * 4 + 4, 3:3 + Q],
                                               in_=pt[:, 0:4 * Q].rearrange("p (c n) -> p c n", c=4)),
                       reads=[pb], writes=[xbc_b])
                fw.mark(2)
                for blk in range(4):
                    wv, wb = load_w(w_in, 8, blk * 512, 512)
                    pt, pb = psum()
                    for k in range(8):
                        op("pe", lambda e, k=k: e.matmul(pt[0:Q, :], lhsT=xnT_t[:, k, q0:q0 + Q], rhs=wv[:, k, :],
                                                         start=(k == 0), stop=(k == 7)), reads=[wb, xnT_b], writes=[pb])
                    op("act", lambda e: e.activation(out=zs_t[0:Q, blk * 512:(blk + 1) * 512], in_=pt[0:Q, :], func=ACT.Silu),
                       reads=[pb], writes=[zs_b])
                    yield
                wv, wb = load_w(w_in, 8, 6144, 32)
                pt, pb = psum()
                for k in range(8):
                    op("pe", lambda e, k=k: e.matmul(pt[0:Q, 0:32], lhsT=xnT_t[:, k, q0:q0 + Q], rhs=wv[:, k, :],
                                                     start=(k == 0), stop=(k == 7)), reads=[wb, xnT_b], writes=[pb])
                s_x = sm[0:Q, 32:64]
                s_ax = sm[0:Q, 64:96]
                s_e = sm[0:Q, 96:128]
                s_dt = sm[0:Q, 128:160]
                s_a = sm[0:Q, 160:192]
                s_ac = sm[0:Q, 192:224]
                dec_bc = sm[:, 224:256]
                R = [smb]
                op("dve", lambda e: e.tensor_tensor(out=s_x, in0=pt[0:Q, 0:32], in1=lc_t[0:Q, 0:32], op=ALU.add),
                   reads=[pb, lc_b], writes=R)
                op("act", lambda e: e.activation(out=s_ax, in_=s_x, func=ACT.Abs), reads=R, writes=R)
                op("act", lambda e: e.activation(out=s_e, in_=s_ax, func=ACT.Exp, scale=-1.0), reads=R, writes=R)
                op("act", lambda e: e.activation(out=s_e, in_=s_e, func=ACT.Ln, bias=ONE[0:Q, :]), reads=R + [cst_b], writes=R)
                op("dve", lambda e: e.tensor_scalar_max(out=s_x, in0=s_x, scalar1=0.0), reads=R, writes=R)
                op("dve", lambda e: e.tensor_tensor(out=s_dt, in0=s_x, in1=s_e, op=ALU.add), reads=R, writes=R)
                op("dve", lambda e: e.tensor_tensor(out=s_a, in0=s_dt, in1=lc_t[0:Q, 32:64], op=ALU.mult), reads=R + [lc_b], writes=R)
                fw.mark(3)
                yield
                tmp = xd_t[:, 0:32 * Q].rearrange("p (c n) -> p c n", c=32)
                cw = lc_t[:, 128:256].rearrange("p (c k) -> p c k", k=4)
                cbias = lc_t[:, 256:288]
                op("dve", lambda e: e.tensor_tensor(out=xc, in0=xbe[:, :, 0:Q], in1=bc(cw[:, :, 0:1], [128, 32, Q]), op=ALU.mult),
                   reads=[xbc_b, lc_b], writes=[xc_b])
                for k in range(1, 4):
                    op("dve", lambda e, k=k: e.tensor_tensor(out=tmp, in0=xbe[:, :, k:k + Q], in1=bc(cw[:, :, k:k + 1], [128, 32, Q]), op=ALU.mult),
                       reads=[xbc_b, lc_b], writes=[xd_b])
                    op("dve", lambda e: e.tensor_tensor(out=xc, in0=xc, in1=tmp, op=ALU.add), reads=[xc_b, xd_b], writes=[xc_b])
                    yield
                op("dve", lambda e: e.tensor_tensor(out=xc, in0=xc, in1=bc(cbias.unsqueeze(2), [128, 32, Q]), op=ALU.add),
                   reads=[xc_b, lc_b], writes=[xc_b])
                op("act", lambda e: e.activation(out=xc, in_=xc, func=ACT.Silu), reads=[xc_b], writes=[xc_b])
                op("pool", lambda e: e.tensor_copy(out=halo_t[:], in_=xbc_t[:, :, Q:Q + 3]), reads=[xbc_b], writes=[halo_b])
                fw.mark(4)
                yield
                for b0 in range(0, 24, 4):
                    pt2, pb2 = psum()
                    for c in range(4):
                        op("pe", lambda e, c=c: e.transpose(out=pt2[0:Q, c * 128:(c + 1) * 128], in_=xc[:, b0 + c, :], identity=ident_t[:]),
                           reads=[xc_b, ident_b], writes=[pb2])
                    if b0 < 16:
                        dst, dstb = xt_t[0:Q, b0 * 128:(b0 + 4) * 128], xt_b
                    else:
                        dst, dstb = bt_t[0:Q, (b0 - 16) * 128:(b0 - 12) * 128], bt_b
                    if (b0 // 4) % 2 == 0:
                        op("act", lambda e: e.copy(out=dst, in_=pt2[0:Q, :]), reads=[pb2], writes=[dstb])
                    else:
                        op("dve", lambda e: e.tensor_copy(out=dst, in_=pt2[0:Q, :]), reads=[pb2], writes=[dstb])
                    yield
                fw.mark(5)
                yield
                pt3, pb3 = psum()
                op("pe", lambda e: e.matmul(pt3[0:Q, 0:32], lhsT=tri_t[0:Q, 0:Q], rhs=s_a, start=True, stop=True),
                   reads=[tri_b] + R, writes=[pb3])
                op("act", lambda e: e.copy(out=s_ac, in_=pt3[0:Q, 0:32]), reads=[pb3], writes=R)
                op("dve", lambda e: e.tensor_tensor(out=lm, in0=bc(s_a.unsqueeze(2), [Q, 32, Q]),
                                                    in1=bc(tri_t[0:Q, 0:Q].unsqueeze(1), [Q, 32, Q]), op=ALU.mult),
                   reads=R + [tri_b], writes=[lm_b])
                ncol = 32 * Q
                for c0 in range(0, ncol, 512):
                    n_ = min(512, ncol - c0)
                    pt4, pb4 = psum()
                    op("pe", lambda e: e.matmul(pt4[:, 0:n_], lhsT=ones_t[0:Q, :], rhs=lm_t[0:Q, c0:c0 + n_], start=True, stop=True),
                       reads=[ones_b, lm_b], writes=[pb4])
                    op("act", lambda e: e.copy(out=ab_t[:, c0:c0 + n_], in_=pt4[:, 0:n_]), reads=[pb4], writes=[ab_b])
                    yield
                fw.mark(6)
                yield
                op("dve", lambda e: e.tensor_tensor(out=lm, in0=abc[0:Q], in1=bc(s_ac.unsqueeze(2), [Q, 32, Q]), op=ALU.subtract),
                   reads=[ab_b] + R, writes=[lm_b])
                op("dve", lambda e: e.tensor_tensor(out=lm, in0=lm, in1=bc(negm_t[0:Q, 0:Q].unsqueeze(1), [Q, 32, Q]), op=ALU.add),
                   reads=[lm_b, negm_b], writes=[lm_b])
                op("act", lambda e: e.activation(out=lm, in_=lm, func=ACT.Exp), reads=[lm_b], writes=[lm_b])
                s_de = sm[0:Q, 288:320]
                op("dve", lambda e: e.tensor_copy(out=s_de, in_=lm[:, :, Q - 1]), reads=[lm_b], writes=R)
                op("act", lambda e: e.activation(out=abc, in_=abc, func=ACT.Exp), reads=[ab_b], writes=[ab_b])
                op("dve", lambda e: e.tensor_copy(out=dec_bc, in_=abc[:, :, Q - 1]), reads=[ab_b], writes=R)
                abc4 = ab_t[:, 0:32 * Q].rearrange("p (g r n) -> p g r n", g=8, r=4)
                op("dve", lambda e: e.tensor_tensor(out=abc4, in0=abc4, in1=bc(xc[:, 24:32, :].unsqueeze(2), [128, 8, 4, Q]), op=ALU.mult),
                   reads=[ab_b, xc_b], writes=[ab_b])
                fw.mark(7)
                yield
                pt5, pb5 = psum()
                for g in range(8):
                    op("pe", lambda e, g=g: e.matmul(pt5[0:Q, g * Q:(g + 1) * Q], lhsT=xc[:, 16 + g, :], rhs=xc[:, 24 + g, :], start=True, stop=True),
                       reads=[xc_b], writes=[pb5])
                lm4 = lm_t[0:Q, 0:32 * Q].rearrange("p (g r n) -> p g r n", g=8, r=4)
                cb4 = pt5[0:Q, 0:8 * Q].rearrange("p (g n) -> p g n", g=8)
                op("dve", lambda e: e.tensor_tensor(out=lm4, in0=lm4, in1=bc(cb4.unsqueeze(2), [Q, 8, 4, Q]), op=ALU.mult),
                   reads=[lm_b, pb5], writes=[lm_b])
                xt3 = xt_t[0:Q, :].rearrange("p (h d) -> p h d", h=32)
                xd3 = xd_t[0:Q, :].rearrange("p (h d) -> p h d", h=32)
                op("dve", lambda e: e.tensor_tensor(out=xd3, in0=xt3, in1=bc(s_dt.unsqueeze(2), [Q, 32, 64]), op=ALU.mult),
                   reads=[xt_b] + R, writes=[xd_b])
                op("dve", lambda e: e.tensor_tensor(out=xt3, in0=xt3, in1=bc(lc_t[0:Q, 64:96].unsqueeze(2), [Q, 32, 64]), op=ALU.mult),
                   reads=[xt_b, lc_b], writes=[xt_b])
                fw.mark(8)
                yield
                for grp in range(4):
                    pt6, pb6 = psum()
                    for hh in range(8):
                        h_ = grp * 8 + hh
                        op("pe", lambda e, h_=h_, hh=hh: e.matmul(pt6[0:Q, hh * 64:(hh + 1) * 64], lhsT=lm[:, h_, :], rhs=xd_t[0:Q, h_ * 64:(h_ + 1) * 64],
                                                                   start=True, stop=False), reads=[lm_b, xd_b], writes=[pb6])
                        op("pe", lambda e, h_=h_, hh=hh: e.matmul(pt6[0:Q, hh * 64:(hh + 1) * 64], lhsT=abc[:, h_, :], rhs=ST_t[:, h_ * 64:(h_ + 1) * 64],
                                                                   start=False, stop=True), reads=[ab_b, ST_b], writes=[pb6])
                    op("dve", lambda e: e.tensor_tensor(out=xt_t[0:Q, grp * 512:(grp + 1) * 512], in0=xt_t[0:Q, grp * 512:(grp + 1) * 512],
                                                        in1=pt6[0:Q, :], op=ALU.add), reads=[xt_b, pb6], writes=[xt_b])
                    yield
                op("dve", lambda e: e.tensor_tensor(out=xd3, in0=xd3, in1=bc(s_de.unsqueeze(2), [Q, 32, 64]), op=ALU.mult),
                   reads=[xd_b] + R, writes=[xd_b])
                fw.mark(9)
                yield
                ST3 = ST_t[:].rearrange("p (h d) -> p h d", h=32)
                for grp in range(4):
                    pt7, pb7 = psum()
                    for hh in range(8):
                        h_ = grp * 8 + hh
                        g = h_ // 4
                        op("pe", lambda e, h_=h_, hh=hh, g=g: e.matmul(pt7[:, hh * 64:(hh + 1) * 64], lhsT=bt_t[0:Q, g * 128:(g + 1) * 128],
                                                                        rhs=xd_t[0:Q, h_ * 64:(h_ + 1) * 64], start=True, stop=True),
                           reads=[bt_b, xd_b], writes=[pb7])
                    sl = ST3[:, grp * 8:(grp + 1) * 8, :]
                    op("dve", lambda e: e.tensor_tensor(out=sl, in0=sl, in1=bc(dec_bc[:, grp * 8:(grp + 1) * 8].unsqueeze(2), [128, 8, 64]), op=ALU.mult),
                       reads=[ST_b] + R, writes=[ST_b])
                    op("dve", lambda e: e.tensor_tensor(out=ST_t[:, grp * 512:(grp + 1) * 512], in0=ST_t[:, grp * 512:(grp + 1) * 512],
                                                        in1=pt7[:, :], op=ALU.add), reads=[ST_b, pb7], writes=[ST_b])
                    yield
                fw.mark(10)
                yield
                y = xt_t[0:Q, :]
                op("dve", lambda e: e.tensor_tensor(out=y, in0=y, in1=zs_t[0:Q, :], op=ALU.mult), reads=[xt_b, zs_b], writes=[xt_b])
                ss = sm[0:Q, 0:4]
                op("dve", lambda e: e.memset(ss[:, 0:1], 0.0), writes=R)
                op("act", lambda e: e.activation(out=zs_t[0:Q, :], in_=y, func=ACT.Square, accum_out=ss[:, 0:1]),
                   reads=[xt_b] + R, writes=[zs_b] + R)
                op("act", lambda e: e.activation(out=ss[:, 1:2], in_=ss[:, 0:1], func=ACT.Sqrt, scale=1.0 / 2048, bias=EPS[0:Q, :]),
                   reads=R + [cst_b], writes=R)
                op("dve", lambda e: e.reciprocal(out=ss[:, 2:3], in_=ss[:, 1:2]), reads=R, writes=R)
                op("dve", lambda e: e.scalar_tensor_tensor(out=y, in0=y, scalar=ss[:, 2:3], in1=gX_t[0:Q, :], op0=ALU.mult, op1=ALU.mult),
                   reads=[xt_b, gX_b] + R, writes=[xt_b])
                to_featmajor(xt_t, xt_b, Q, 2048, ynT, ynT_b, tok0=q0)

            fw.mark(11)
            yield
            for blk in range(4):
                wv, wb = load_w(m_w_out[l], 16, blk * 256, 256)
                pt, pb = psum()
                for k in range(16):
                    op("pe", lambda e, k=k: e.matmul(pt[0:nt, 0:256], lhsT=ynT[:, k, 0:nt], rhs=wv[:, k, :], start=(k == 0), stop=(k == 15)),
                       reads=[wb, ynT_b], writes=[pb])
                op("dve", lambda e: e.tensor_tensor(out=ht_t[0:nt, blk * 256:(blk + 1) * 256], in0=ht_t[0:nt, blk * 256:(blk + 1) * 256],
                                                    in1=pt[0:nt, 0:256], op=ALU.add), reads=[ht_b, pb], writes=[ht_b])
                yield
            fw.dma("sp", h_ap(ci + 1, t), ht_t[0:nt, :], reads=[ht_b], writes=[t.hb[ci + 1]], owner=ht_b)

            fw.mark(12)
            if last_tile:
                stg_t, stg_b = P[4]
                to_featmajor(ST_t, ST_b, 128, 2048, stg_t[:].rearrange("p (m n) -> p m n", m=16), stg_b)
                dst = o_pssm[l] if t.kind == "p" else o_sssm[l, t.idx]
                fw.dma("sp", dst.rearrange("(m p) n -> p m n", p=128), stg_t[:].rearrange("p (m n) -> p m n", m=16),
                       reads=[stg_b], owner=stg_b)
                outs_b.append(stg_b)
                fw.mark(13)
                dstc = o_pconv[l] if t.kind == "p" else o_sconv[l, t.idx]
                for c0 in range(32):
                    dv = dstc[:, c0 * 128:(c0 + 1) * 128].rearrange("r p -> p r")
                    fw.dma("sp", dv, halo_t[:, c0, :], reads=[halo_b], owner=halo_b, slow=True)
                outs_b.append(halo_b)

        def attn_tile(l, t, ci):
            j = l - 2
            nt = t.nt
            fw.dma("sp", ht_t[0:nt, :], h_ap(ci, t), reads=[t.hb[ci]], writes=[ht_b], owner=ht_b)
            xkT_t, xkT_b = P[7]
            xkT = xkT_t[:, 0:1024].rearrange("p (k n) -> p k n", k=8)
            stg_t, stg_b = P[6]
            if t.kind == "p":
                cur = t.idx % 2
                prev = 1 - cur
                blocks = ([(prev, 128)] if t.idx > 0 else []) + [(cur, 128)]
            else:
                prev, cur = 0, 1
                blocks = [(prev, 128), (cur, TS)]
                fw.dma("sp", stg_t[:, 0:256], ck[t.idx], writes=[stg_b], owner=stg_b)
                fw.dma("sp", vr[prev][0][:, :], cv[t.idx], writes=[vr[prev][1]], owner=vr[prev][1])
                if j == 0:
                    fw.dma("sp", o_sk[t.idx][0:128 - TS, :], stg_t[TS:128, 0:256], reads=[stg_b], owner=stg_b)
                    fw.dma("sp", o_sv[t.idx][0:128 - TS, :], vr[prev][0][TS:128, :], reads=[vr[prev][1]], owner=vr[prev][1])
                    outs_b.extend([stg_b, vr[prev][1]])
                pt, pb = psum()
                for kh in range(4):
                    op("pe", lambda e, kh=kh: e.transpose(out=pt[0:64, kh * 128:(kh + 1) * 128], in_=stg_t[:, kh * 64:(kh + 1) * 64], identity=ident_t[:]),
                       reads=[stg_b, ident_b], writes=[pb])
                op("act", lambda e: e.copy(out=kTr[prev][0][:], in_=pt[0:64, :].rearrange("p (k n) -> p k n", k=4)),
                   reads=[pb], writes=[kTr[prev][1]])
            kT_c, kT_cb = kTr[cur]
            yield
            v_c, v_cb = vr[cur]
            if j == 0:
                rmsnorm(ht_t, ht_b, nt, gX_t[:, 0:1024], gX_b, xn_t, xn_b, sm[0:nt, 0:4], smb)
                to_featmajor(xn_t, xn_b, nt, D, xkT, xkT_b)
                wv, wb = load_w(a_w_kv, 8, 0, 512)
                pt, pb = psum()
                for kh in range(4):
                    for k in range(8):
                        op("pe", lambda e, kh=kh, k=k: e.matmul(pt[0:64, kh * nt:(kh + 1) * nt], lhsT=wv[:, k, kh * 64:(kh + 1) * 64], rhs=xkT[:, k, 0:nt],
                                                                 start=(k == 0), stop=(k == 7)), reads=[wb, xkT_b], writes=[pb])
                for kh in range(4):
                    op("act", lambda e, kh=kh: e.activation(out=kT_c[:, kh, 0:nt], in_=pt[0:64, kh * nt:(kh + 1) * nt], func=ACT.Identity,
                                                            bias=lc_t[0:64, 96 + kh:97 + kh]), reads=[pb, lc_b], writes=[kT_cb])
                pt2, pb2 = psum()
                for k in range(8):
                    op("pe", lambda e, k=k: e.matmul(pt2[0:nt, :], lhsT=xkT[:, k, 0:nt], rhs=wv[:, k, :], start=(k == 0), stop=(k == 7)),
                       reads=[wb, xkT_b], writes=[pb2])
                kvt_t, kvt_b = P[5]
                op("dve", lambda e: e.tensor_tensor(out=kvt_t[0:nt, 0:512], in0=pt2[0:nt, :], in1=gX_t[0:nt, 1024:1536], op=ALU.add),
                   reads=[pb2, gX_b], writes=[kvt_b])
                op("pool", lambda e: e.tensor_copy(out=v_c[0:nt, :], in_=kvt_t[0:nt, 256:512]), reads=[kvt_b], writes=[v_cb])
                fw.dma("sp", kT_d[t.kvslot][:, :, 0:nt], kT_c[:, :, 0:nt], reads=[kT_cb], writes=[t.kTd_b], owner=kT_cb)
                fw.dma("sp", v_d[t.kvslot][0:nt, :], v_c[0:nt, :], reads=[v_cb], writes=[t.vd_b], owner=v_cb)
                if t.kind == "p" and t.idx == NPT - 1:
                    fw.dma("sp", o_pk, kvt_t[0:128, 0:256], reads=[kvt_b], owner=kvt_b)
                    fw.dma("sp", o_pv, kvt_t[0:128, 256:512], reads=[kvt_b], owner=kvt_b)
                    outs_b.append(kvt_b)
                if t.kind == "s":
                    fw.dma("sp", o_sk[t.idx][128 - TS:128, :], kvt_t[0:TS, 0:256], reads=[kvt_b], owner=kvt_b)
                    fw.dma("sp", o_sv[t.idx][128 - TS:128, :], kvt_t[0:TS, 256:512], reads=[kvt_b], owner=kvt_b)
                    outs_b.append(kvt_b)
            else:
                fw.dma("sp", kT_c[:, :, 0:nt], kT_d[t.kvslot][:, :, 0:nt], reads=[t.kTd_b], writes=[kT_cb], owner=kT_cb)
                fw.dma("sp", v_c[0:nt, :], v_d[t.kvslot][0:nt, :], reads=[t.vd_b], writes=[v_cb], owner=v_cb)
            yield
            rmsnorm(ht_t, ht_b, nt, gmix_t, gmix_b, xn_t, xn_b, sm[0:nt, 0:4], smb)
            to_featmajor(xn_t, xn_b, nt, D, xnT_t, xnT_b)
            QT_t, QT_b = P[0]
            QT = QT_t[0:64, :].rearrange("p (h n) -> p h n", h=16)
            for blk in range(2):
                wv, wb = load_w(a_w_q[j], 8, blk * 512, 512)
                for h4 in range(2):
                    pt, pb = psum()
                    for hh in range(4):
                        for k in range(8):
                            op("pe", lambda e, hh=hh, k=k: e.matmul(pt[0:64, hh * nt:(hh + 1) * nt], lhsT=wv[:, k, (h4 * 4 + hh) * 64:(h4 * 4 + hh + 1) * 64],
                                                                     rhs=xnT_t[:, k, 0:nt], start=(k == 0), stop=(k == 7)), reads=[wb, xnT_b], writes=[pb])
                    for hh in range(4):
                        hq = blk * 8 + h4 * 4 + hh
                        op("act", lambda e, hh=hh, hq=hq: e.activation(out=QT[:, hq, 0:nt], in_=pt[0:64, hh * nt:(hh + 1) * nt], func=ACT.Identity,
                                                                        scale=0.125, bias=lc_t[0:64, 112 + hq:113 + hq]), reads=[pb, lc_b], writes=[QT_b])
                    yield
            oT_t, oT_b = P[1]
            oT = oT_t[0:64, :].rearrange("p (h n) -> p h n", h=16)
            NK = sum(n for _, n in blocks)
            if t.kind == "p":
                mask = amask_t[0:nt, 0:256] if t.idx > 0 else amask_t[0:nt, 256:384]
            else:
                mask = None
            S_t, S_b = P[2]
            PT_t, PT_b = P[3]
            po = None
            for hq in range(16):
                kvh = hq // 4
                pt, pb = psum()
                off = 0
                for (slot, n) in blocks:
                    op("pe", lambda e, slot=slot, n=n, off=off: e.matmul(pt[0:nt, off:off + n], lhsT=QT[:, hq, 0:nt], rhs=kTr[slot][0][:, kvh, 0:n],
                                                                          start=True, stop=True), reads=[QT_b, kTr[slot][1]], writes=[pb])
                    off += n
                Ss = S_t[0:nt, (hq % 2) * 256:(hq % 2) * 256 + NK]
                st_ = sm[0:nt, 8:16]
                if mask is not None:
                    op("dve", lambda e: e.tensor_tensor(out=Ss, in0=pt[0:nt, 0:NK], in1=mask, op=ALU.add), reads=[pb, amask_b], writes=[S_b])
                else:
                    op("dve", lambda e: e.tensor_copy(out=Ss, in_=pt[0:nt, 0:NK]), reads=[pb], writes=[S_b])
                op("dve", lambda e: e.reduce_max(out=st_[:, 0:1], in_=Ss, axis=AX.X), reads=[S_b], writes=[smb])
                op("dve", lambda e: e.tensor_scalar_mul(out=st_[:, 1:2], in0=st_[:, 0:1], scalar1=-1.0), reads=[smb], writes=[smb])
                op("dve", lambda e: e.memset(st_[:, 2:3], 0.0), writes=[smb])
                op("act", lambda e: e.activation(out=Ss, in_=Ss, func=ACT.Exp, bias=st_[:, 1:2], accum_out=st_[:, 2:3]),
                   reads=[S_b, smb], writes=[S_b, smb])
                op("act", lambda e: e.activation(out=st_[:, 3:4], in_=lc_t[0:nt, 288 + hq:289 + hq], func=ACT.Exp, bias=st_[:, 1:2]),
                   reads=[lc_b, smb], writes=[smb])
                op("dve", lambda e: e.tensor_tensor(out=st_[:, 4:5], in0=st_[:, 2:3], in1=st_[:, 3:4], op=ALU.add), reads=[smb], writes=[smb])
                op("dve", lambda e: e.reciprocal(out=st_[:, 5:6], in_=st_[:, 4:5]), reads=[smb], writes=[smb])
                op("dve", lambda e: e.tensor_scalar_mul(out=Ss, in0=Ss, scalar1=st_[:, 5:6]), reads=[S_b, smb], writes=[S_b])
                yield
                pt2, pb2 = psum()
                off = 0
                for bi, (slot, n) in enumerate(blocks):
                    op("pe", lambda e, n=n, off=off, bi=bi: e.transpose(out=pt2[0:n, bi * 128:bi * 128 + nt],
                                                                        in_=S_t[0:nt, (hq % 2) * 256 + off:(hq % 2) * 256 + off + n],
                                                                        identity=ident_t[0:nt, 0:nt]), reads=[S_b, ident_b], writes=[pb2])
                    off += n
                PTs = PT_t[:, (hq % 2) * 256:(hq % 2) * 256 + 256]
                for bi, (slot, n) in enumerate(blocks):
                    op("act", lambda e, n=n, bi=bi: e.copy(out=PTs[0:n, bi * 128:bi * 128 + nt], in_=pt2[0:n, bi * 128:bi * 128 + nt]),
                       reads=[pb2], writes=[PT_b])
                if hq % 4 == 0:
                    po, pob = psum()
                yield
                for bi, (slot, n) in enumerate(blocks):
                    op("pe", lambda e, n=n, bi=bi, slot=slot: e.matmul(po[0:64, (hq % 4) * nt:(hq % 4 + 1) * nt], lhsT=vr[slot][0][0:n, kvh * 64:(kvh + 1) * 64],
                                                                        rhs=PTs[0:n, bi * 128:bi * 128 + nt], start=(bi == 0), stop=(bi == len(blocks) - 1)),
                       reads=[vr[slot][1], PT_b], writes=[pob])
                if hq % 4 == 3:
                    op("act", lambda e: e.copy(out=oT[:, hq - 3:hq + 1, 0:nt], in_=po[0:64, 0:4 * nt].rearrange("p (h n) -> p h n", h=4)),
                       reads=[pob], writes=[oT_b])
                yield
            op("dve", lambda e: e.tensor_tensor(out=ht_t[0:nt, :], in0=ht_t[0:nt, :], in1=bo_t[0:nt, :], op=ALU.add),
               reads=[ht_b, bo_b], writes=[ht_b])
            wo = a_w_o[j].rearrange("(h d) n -> d h n", d=64)
            for blk in range(4):
                wt, wb = wslot()
                wv = wt[0:64, 0:16 * 256].rearrange("p (h n) -> p h n", h=16)
                fw.dma("sp", wv, wo[:, :, blk * 256:(blk + 1) * 256], writes=[wb], owner=wb)
                pt, pb = psum()
                for hq in range(16):
                    op("pe", lambda e, hq=hq: e.matmul(pt[0:nt, 0:256], lhsT=oT[:, hq, 0:nt], rhs=wv[:, hq, :], start=(hq == 0), stop=(hq == 15)),
                       reads=[wb, oT_b], writes=[pb])
                op("dve", lambda e: e.tensor_tensor(out=ht_t[0:nt, blk * 256:(blk + 1) * 256], in0=ht_t[0:nt, blk * 256:(blk + 1) * 256],
                                                    in1=pt[0:nt, 0:256], op=ALU.add), reads=[ht_b, pb], writes=[ht_b])
                yield
            fw.dma("sp", h_ap(ci + 1, t), ht_t[0:nt, :], reads=[ht_b], writes=[t.hb[ci + 1]], owner=ht_b)

        def peer_a(l, t, ci, pp):
            nt = t.nt
            htp_t, htp_b = HTP[pp]
            eid_t, eid_b = EID[pp]
            fw.dma("sp", htp_t[0:nt, :], h_ap(ci, t), reads=hbl(t, ci), writes=[htp_b], owner=htp_b)
            rmsnorm(htp_t, htp_b, nt, gffn_t, gffn_b, xnp_t, xnp_b, sm[0:nt, 0:4], smb)
            to_featmajor(xnp_t, xnp_b, nt, D, xnT_t, xnT_b)
            fw.mark(17)
            yield
            qT_t, qT_b = P[0]
            qT = qT_t[:].rearrange("p (c n) -> p c n", c=16)
            S_t, S_b = P[1]
            Sw_t, Sw_b = P[2]
            S3 = S_t[0:nt, :].rearrange("p (c n) -> p c n", c=16)
            Sw3 = Sw_t[0:nt, :].rearrange("p (c n) -> p c n", c=16)
            for blk in range(4):
                wv, wb = load_w(p_w_q[l], 8, blk * 512, 512)
                pt, pb = psum()
                for cc in range(4):
                    for k in range(8):
                        op("pe", lambda e, cc=cc, k=k: e.matmul(pt[:, cc * nt:(cc + 1) * nt], lhsT=wv[:, k, cc * 128:(cc + 1) * 128], rhs=xnT_t[:, k, 0:nt],
                                                                 start=(k == 0), stop=(k == 7)), reads=[wb, xnT_b], writes=[pb])
                op("act", lambda e: e.copy(out=qT[:, blk * 4:blk * 4 + 4, 0:nt], in_=pt[:, 0:4 * nt].rearrange("p (c n) -> p c n", c=4)),
                   reads=[pb], writes=[qT_b])
                yield
            fw.mark(18)
            for blk in range(4):
                pt, pb = psum()
                for cc in range(4):
                    c = blk * 4 + cc
                    op("pe", lambda e, cc=cc, c=c: e.matmul(pt[0:nt, cc * 128:(cc + 1) * 128], lhsT=qT[:, c, 0:nt], rhs=sk_t[:, c % 2, :], start=True, stop=True),
                       reads=[qT_b, sk_b], writes=[pb])
                if os.environ.get('K_SKIPC') != '1':
                    op("act", lambda e: e.copy(out=S_t[0:nt, blk * 512:(blk + 1) * 512], in_=pt[0:nt, :]), reads=[pb], writes=[S_b])
                if os.environ.get('K_SKIPC') not in ('1', '2'):
                    op("dve", lambda e: e.tensor_copy(out=Sw_t[0:nt, blk * 512:(blk + 1) * 512], in_=S_t[0:nt, blk * 512:(blk + 1) * 512]), reads=[S_b], writes=[Sw_b])
            fw.mark(20)
            yield
            R = [smb]
            vals = sm[0:nt, 256:512].rearrange("p (c k) -> p c k", c=16)
            idxf = sm[0:nt, 512:768].rearrange("p (c k) -> p c k", c=16)
            vs = sm[0:nt, 768:896].rearrange("p (h k) -> p h k", h=8)
            posf = sm[0:nt, 896:1024].rearrange("p (h k) -> p h k", h=8)
            idxu = idx_t[0:nt, 0:256].rearrange("p (c k) -> p c k", c=16)
            posu = idx_t[0:nt, 256:384].rearrange("p (h k) -> p h k", h=8)
            for c in range(16):
                op("dve", lambda e, c=c: e.max(out=vals[:, c, 0:8], in_=Sw3[:, c, :]), reads=[Sw_b], writes=R)
                op("dve", lambda e, c=c: e.match_replace(out=Sw3[:, c, :], in_to_replace=vals[:, c, 0:8], in_values=Sw3[:, c, :], imm_value=-1e30),
                   reads=[Sw_b] + R, writes=[Sw_b])
                op("dve", lambda e, c=c: e.max(out=vals[:, c, 8:16], in_=Sw3[:, c, :]), reads=[Sw_b], writes=R)
                op("dve", lambda e, c=c: e.max_index(out=idxu[:, c, 0:8], in_max=vals[:, c, 0:8], in_values=S3[:, c, :]), reads=[S_b] + R, writes=[idx_b])
                op("dve", lambda e, c=c: e.max_index(out=idxu[:, c, 8:16], in_max=vals[:, c, 8:16], in_values=S3[:, c, :]), reads=[S_b] + R, writes=[idx_b])
                yield
            op("dve", lambda e: e.tensor_copy(out=idxf, in_=idxu), reads=[idx_b], writes=R)
            vals4 = sm[0:nt, 256:512].rearrange("p (h s k) -> p h s k", h=8, s=2)
            idxf4 = sm[0:nt, 512:768].rearrange("p (h s k) -> p h s k", h=8, s=2)
            cand_t, cand_b = P[3]
            cw_t, cw_b = P[4]
            T1_t, T1_b = P[5]
            cand = cand_t[0:nt, :].rearrange("p (h a b) -> p h a b", h=8, a=16)
            cand3 = cand_t[0:nt, :].rearrange("p (h n) -> p h n", h=8)
            cw3 = cw_t[0:nt, :].rearrange("p (h n) -> p h n", h=8)
            for h in range(8):
                op("dve", lambda e, h=h: e.tensor_tensor(out=cand[:, h], in0=bc(vals4[:, h, 0, :].unsqueeze(2), [nt, 16, 16]),
                                                         in1=bc(vals4[:, h, 1, :].unsqueeze(1), [nt, 16, 16]), op=ALU.add), reads=R, writes=[cand_b])
            op("dve", lambda e: e.tensor_copy(out=cw_t[0:nt, :], in_=cand_t[0:nt, :]), reads=[cand_b], writes=[cw_b])
            for h in range(8):
                op("dve", lambda e, h=h: e.max(out=vs[:, h, 0:8], in_=cw3[:, h, :]), reads=[cw_b], writes=R)
                op("dve", lambda e, h=h: e.match_replace(out=cw3[:, h, :], in_to_replace=vs[:, h, 0:8], in_values=cw3[:, h, :], imm_value=-1e30),
                   reads=[cw_b] + R, writes=[cw_b])
                op("dve", lambda e, h=h: e.max(out=vs[:, h, 8:16], in_=cw3[:, h, :]), reads=[cw_b], writes=R)
                op("dve", lambda e, h=h: e.max_index(out=posu[:, h, 0:8], in_max=vs[:, h, 0:8], in_values=cand3[:, h, :]), reads=[cand_b] + R, writes=[idx_b])
                op("dve", lambda e, h=h: e.max_index(out=posu[:, h, 8:16], in_max=vs[:, h, 8:16], in_values=cand3[:, h, :]), reads=[cand_b] + R, writes=[idx_b])
                yield
            op("dve", lambda e: e.tensor_copy(out=posf, in_=posu), reads=[idx_b], writes=R)
            fw.mark(21)
            yield
            T15 = cw_t[0:nt, 0:128 * 15].rearrange("p (s m) -> p s m", m=15)
            posf2 = sm[0:nt, 896:1024]
            af = psm_t[0:nt, 0:128]
            bf = psm_t[0:nt, 128:256]
            e1 = psm_t[0:nt, 256:384]
            e2 = psm_t[0:nt, 384:512]
            gate = psm_t[0:nt, 512:640]
            actv = psm_t[0:nt, 640:768]
            wgt = psm_t[0:nt, 768:896]
            tmp1 = psm_t[0:nt, 768:896]
            eidf = psm_t[0:nt, 896:1024]
            Rb = [psm_b]
            op("dve", lambda e: e.tensor_tensor(out=T15, in0=bc(posf2.unsqueeze(2), [nt, 128, 15]), in1=bc(thr_t[0:nt, :].unsqueeze(1), [nt, 128, 15]), op=ALU.is_ge),
               reads=R + [thr_b], writes=[cw_b])
            op("dve", lambda e: e.reduce_sum(out=af, in_=T15, axis=AX.X), reads=[cw_b], writes=Rb)
            op("dve", lambda e: e.scalar_tensor_tensor(out=bf, in0=af, scalar=-16.0, in1=posf2, op0=ALU.mult, op1=ALU.add), reads=Rb + R, writes=Rb)
            T1 = T1_t[0:nt, :].rearrange("p (h k a) -> p h k a", h=8, k=16)
            for (src, which, dst) in ((af, 0, e1), (bf, 1, e2)):
                src3 = src.rearrange("p (h k) -> p h k", h=8)
                dst3 = dst.rearrange("p (h k) -> p h k", h=8)
                for h in range(8):
                    op("dve", lambda e, h=h, src3=src3: e.tensor_tensor(out=T1[:, h], in0=bc(src3[:, h, :].unsqueeze(2), [nt, 16, 16]),
                                                                        in1=bc(iota_t[0:nt, :].unsqueeze(1), [nt, 16, 16]), op=ALU.is_equal),
                       reads=Rb + [iota_b], writes=[T1_b])
                    op("dve", lambda e, h=h, which=which: e.tensor_tensor(out=T1[:, h], in0=T1[:, h], in1=bc(idxf4[:, h, which, :].unsqueeze(1), [nt, 16, 16]), op=ALU.mult),
                       reads=[T1_b] + R, writes=[T1_b])
                    yield
                op("dve", lambda e, dst=dst: e.reduce_sum(out=dst, in_=T1_t[0:nt, :].rearrange("p (s a) -> p s a", a=16), axis=AX.X), reads=[T1_b], writes=Rb)
            op("dve", lambda e: e.scalar_tensor_tensor(out=eidf, in0=e1, scalar=128.0, in1=e2, op0=ALU.mult, op1=ALU.add), reads=Rb, writes=Rb)
            op("dve", lambda e: e.tensor_scalar_add(out=eidf, in0=eidf, scalar1=float(l * 16384)), reads=Rb, writes=Rb)
            op("dve", lambda e: e.tensor_copy(out=eid_t[0:nt, :], in_=eidf), reads=Rb, writes=[eid_b])
            fw.mark(22)
            yield
            gate3 = gate.rearrange("p (h k) -> p h k", h=8)
            op("dve", lambda e: e.tensor_tensor(out=gate3, in0=vs, in1=bc(vs[:, :, 0:1], [nt, 8, 16]), op=ALU.subtract), reads=R, writes=Rb)
            op("act", lambda e: e.activation(out=gate, in_=gate, func=ACT.Exp), reads=Rb, writes=Rb)
            op("dve", lambda e: e.reduce_sum(out=tmp1[:, 0:8], in_=gate3, axis=AX.X), reads=Rb, writes=Rb)
            op("dve", lambda e: e.reciprocal(out=tmp1[:, 8:16], in_=tmp1[:, 0:8]), reads=Rb, writes=Rb)
            op("dve", lambda e: e.tensor_tensor(out=gate3, in0=gate3, in1=bc(tmp1[:, 8:16].unsqueeze(2), [nt, 8, 16]), op=ALU.mult), reads=Rb, writes=Rb)
            fw.mark(23)
        def peer_b(l, t, ci, pp):
            nt = t.nt
            htp_t, htp_b = HTP[pp]
            eid_t, eid_b = EID[pp]
            wgt_t, wgt_b = WGT[pp]
            Rb = [psm_b]
            gate = psm_t[0:nt, 512:640]
            actv = psm_t[0:nt, 640:768]
            wgt = wgt_t[0:nt, :]
            tmp1 = psm_t[0:nt, 768:896]
            g_i = [0]

            def gather(tab, jj):
                gt, gb = G[g_i[0] % NG]
                g_i[0] += 1
                fw.dma("pool", None, None, reads=[eid_b], writes=[gb], owner=gb,
                       fn=lambda e: e.indirect_dma_start(out=gt[0:nt, :], out_offset=None, in_=tab,
                                                         in_offset=bass.IndirectOffsetOnAxis(ap=eid_t[0:nt, jj:jj + 1], axis=0)))
                return gt, gb
            op("dve", lambda e: e.memset(actv, 0.0), writes=Rb)
            for jj in range(128):
                gt, gb = gather(p_u, jj)
                op("dve", lambda e, gt=gt, jj=jj: e.scalar_tensor_tensor(out=gt[0:nt, :], in0=gt[0:nt, :], scalar=1.0, in1=xnp_t[0:nt, :],
                                                                           op0=ALU.mult, op1=ALU.mult, accum_out=actv[:, jj:jj + 1]),
                   reads=[gb, xnp_b], writes=[gb] + Rb)
                yield
            fw.mark(24)
            op("dve", lambda e: e.tensor_tensor(out=tmp1, in0=actv, in1=actv, op=ALU.mult), reads=Rb, writes=Rb)
            op("dve", lambda e: e.tensor_scalar(out=tmp1, in0=tmp1, scalar1=0.044715, scalar2=1.0, op0=ALU.mult, op1=ALU.add), reads=Rb, writes=Rb)
            op("dve", lambda e: e.tensor_tensor(out=tmp1, in0=tmp1, in1=actv, op=ALU.mult), reads=Rb, writes=Rb)
            op("act", lambda e: e.activation(out=tmp1, in_=tmp1, func=ACT.Tanh, scale=0.7978845608028654), reads=Rb, writes=Rb)
            op("dve", lambda e: e.tensor_scalar(out=tmp1, in0=tmp1, scalar1=1.0, scalar2=0.5, op0=ALU.add, op1=ALU.mult), reads=Rb, writes=Rb)
            op("dve", lambda e: e.tensor_tensor(out=tmp1, in0=tmp1, in1=actv, op=ALU.mult), reads=Rb, writes=Rb)
            op("dve", lambda e: e.tensor_tensor(out=wgt, in0=tmp1, in1=gate, op=ALU.mult), reads=Rb, writes=[wgt_b])
            for jj in range(128):
                gt, gb = gather(p_v, jj)
                op("dve", lambda e, gt=gt, jj=jj: e.scalar_tensor_tensor(out=htp_t[0:nt, :], in0=gt[0:nt, :], scalar=wgt[:, jj:jj + 1], in1=htp_t[0:nt, :],
                                                                           op0=ALU.mult, op1=ALU.add), reads=[gb, htp_b, wgt_b], writes=[htp_b])
                yield
            fw.dma("sp", h_ap(ci + 1, t), htp_t[0:nt, :], reads=[htp_b], writes=hbl(t, ci + 1), owner=htp_b)
            if l == 3:
                jk_t, jk_b = G[0]
                ssf = wgt_t[0:nt, 0:4]
                op("dve", lambda e: e.memset(ssf[:, 0:1], 0.0), writes=[wgt_b])
                op("act", lambda e: e.activation(out=jk_t[0:nt, :], in_=htp_t[0:nt, :], func=ACT.Square, accum_out=ssf[:, 0:1]),
                   reads=[htp_b, wgt_b], writes=[jk_b, wgt_b])
                op("act", lambda e: e.activation(out=ssf[:, 1:2], in_=ssf[:, 0:1], func=ACT.Sqrt, scale=1.0 / D, bias=EPS[0:nt, :]),
                   reads=[wgt_b, cst_b], writes=[wgt_b])
                op("dve", lambda e: e.reciprocal(out=ssf[:, 2:3], in_=ssf[:, 1:2]), reads=[wgt_b], writes=[wgt_b])
                op("dve", lambda e: e.scalar_tensor_tensor(out=htp_t[0:nt, :], in0=htp_t[0:nt, :], scalar=ssf[:, 2:3], in1=gX_t[0:nt, 0:1024],
                                                            op0=ALU.mult, op1=ALU.mult), reads=[htp_b, wgt_b, gX_b], writes=[htp_b])
                dst = y_p[t.row0:t.row0 + nt, :] if t.kind == "p" else (y_s[0:nt, :] if t.kind == "sp" else y_s[t.idx * TS:(t.idx + 1) * TS, :])
                fw.dma("sp", dst, htp_t[0:nt, :], reads=[htp_b], owner=htp_b)
                outs_b.append(htp_b)

        outs_b = []
        pending = [None]
        NRES = int(os.environ.get('K_NRES', '96'))
        RATIO_A = float(os.environ.get('K_RPA', '1.5'))
        RATIO = {(True, 'p'): float(os.environ.get('K_RM', '1.9')), (True, 's'): float(os.environ.get('K_RMS', '3.8')),
                 (False, 'p'): float(os.environ.get('K_RA', '4.4')), (False, 's'): float(os.environ.get('K_RAS', '4.4'))}

        class Gen:
            def __init__(self, g, total):
                self.g, self.total, self.n = g, total, 0

            def step(self):
                if self.g is None:
                    return False
                try:
                    next(self.g)
                    self.n += 1
                    return True
                except StopIteration:
                    self.g = None
                    return False

        def drain2(g1, g2, r=1.0, reserve=0):
            acc = 0.0
            while g1 is not None and g1.g is not None:
                if not g1.step():
                    break
                acc += r
                while g2 is not None and g2.g is not None and acc >= 1.0 and g2.n < g2.total - reserve:
                    acc -= 1.0
                    g2.step()
                if acc > 4.0:
                    acc = 4.0
            if g2 is not None and reserve == 0:
                while g2.step():
                    pass

        for l in range(nlayers):
            load_bc(gmix_t[:], gmix_b, norm_mix[l])
            load_bc(gffn_t[:], gffn_b, norm_ffn[l])
            fw.dma("sp", sk_t[:, 0, :], p_k1T[l], writes=[sk_b], owner=sk_b)
            fw.dma("sp", sk_t[:, 1, :], p_k2T[l], writes=[sk_b], owner=sk_b)
            if l < 2:
                load_bc(gX_t[:], gX_b, m_norm[l])
                load_bc(lc_t[:, 0:32], lc_b, m_dt_bias[l])
                load_bc(lc_t[:, 32:64], lc_b, m_a_log[l])
                load_bc(lc_t[:, 64:96], lc_b, m_d_skip[l])
                fw.dma("sp", lc_t[:, 128:256], m_conv_w[l].rearrange("p c k -> p (c k)"), writes=[lc_b], owner=lc_b)
                fw.dma("sp", lc_t[:, 256:288], m_conv_b[l], writes=[lc_b], owner=lc_b)
                op("act", lambda e: e.activation(out=lc_t[:, 32:64], in_=lc_t[:, 32:64], func=ACT.Exp), reads=[lc_b], writes=[lc_b])
                op("dve", lambda e: e.tensor_scalar_mul(out=lc_t[:, 32:64], in0=lc_t[:, 32:64], scalar1=-1.0), reads=[lc_b], writes=[lc_b])
            else:
                j = l - 2
                if j == 0:
                    load_bc(gX_t[:, 0:1024], gX_b, norm_kv[0])
                    load_bc(gX_t[:, 1024:1536], gX_b, a_b_kv[0])
                    fw.dma("sp", lc_t[0:64, 96:100], a_b_kT, writes=[lc_b], owner=lc_b)
                load_bc(bo_t[:], bo_b, a_b_o[j])
                fw.dma("sp", lc_t[0:64, 112:128], a_b_qT[j], writes=[lc_b], owner=lc_b)
                op("dve", lambda e: e.tensor_scalar_mul(out=lc_t[0:64, 112:128], in0=lc_t[0:64, 112:128], scalar1=0.125), reads=[lc_b], writes=[lc_b])
                load_bc(lc_t[:, 288:304], lc_b, a_sinks[j])
            if l == 3:
                load_bc(gX_t[:, 0:1024], gX_b, norm_final[0])
            try:
                for t in tiles:
                    ci = 2 * l
                    mix = Gen(mamba_tile(l, t, ci) if l < 2 else attn_tile(l, t, ci), 0)
                    pt_ = t
                    if t.kind == "s":
                        pt_ = speer if t is stiles[-1] else None
                    will_peer = (pt_ is not None) and not os.environ.get('K_NOPEER')
                    drain2(mix, pending[0], RATIO[(l < 2, t.kind)], reserve=(NRES if will_peer else 0) if pending[0] is not None else 0)
                    if not will_peer:
                        if pending[0] is not None and pending[0].g is None:
                            pending[0] = None
                        continue
                    pp = peer_cnt[0] % 2
                    peer_cnt[0] += 1
                    pa_g = Gen(peer_a(l, pt_, ci + 1, pp), 0)
                    drain2(pa_g, pending[0], RATIO_A, reserve=0)
                    pending[0] = Gen(peer_b(l, pt_, ci + 1, pp), 256)
                    if l >= ILV_MAXL:
                        drain2(None, pending[0])
                        pending[0] = None
            except Stop:
                break
        if pending[0] is not None:
            drain2(None, pending[0])
        fw.finish(fw.allbufs, "sp")
        fw.finish(fw.allbufs, "act")
        print("program: ninst=%d nwait=%d sems=%d" % (fw.ninst, fw.nwait, len(fw.sems)))
    return nc


_CACHE = {}


def make_in_maps(inp):
    f = lambda a: np.ascontiguousarray(np.asarray(a, dtype=np.float32))
    shared = {
        "norm_mix": f(inp["norm_mix"]), "norm_ffn": f(inp["norm_ffn"]),
        "norm_kv": f(inp["norm_kv"]).reshape(1, D), "norm_final": f(inp["norm_final"]).reshape(1, D),
        "m_w_in": f(inp["m_w_in"]),
        "m_conv_w": f(np.asarray(inp["m_conv_w"]).reshape(2, 4, 32, 128).transpose(0, 3, 2, 1)),
        "m_conv_b": f(np.asarray(inp["m_conv_b"]).reshape(2, 32, 128).transpose(0, 2, 1)),
        "m_dt_bias": f(inp["m_dt_bias"]), "m_a_log": f(inp["m_a_log"]), "m_d_skip": f(inp["m_d_skip"]),
        "m_norm": f(inp["m_norm"]), "m_w_out": f(inp["m_w_out"]),
        "a_w_kv": f(inp["a_w_kv"]), "a_b_kv": f(inp["a_b_kv"]).reshape(1, 512),
        "a_b_kT": f(np.asarray(inp["a_b_kv"])[:256].reshape(4, 64).T),
        "a_w_q": f(inp["a_w_q"]),
        "a_b_qT": f(np.asarray(inp["a_b_q"]).reshape(2, 16, 64).transpose(0, 2, 1)),
        "a_sinks": f(inp["a_sinks"]), "a_w_o": f(inp["a_w_o"]), "a_b_o": f(inp["a_b_o"]),
        "p_w_q": f(inp["p_w_q"]),
        "p_k1T": f(np.asarray(inp["p_sub_k1"]).transpose(0, 2, 1)),
        "p_k2T": f(np.asarray(inp["p_sub_k2"]).transpose(0, 2, 1)),
        "p_u": f(inp["p_u"]).reshape(4 * 16384, D), "p_v": f(inp["p_v"]).reshape(4 * 16384, D),
    }
    xp = np.asarray(inp["x_prompt"], dtype=np.float32)
    xs = np.asarray(inp["x_sample"], dtype=np.float32)
    ssm = np.asarray(inp["state_ssm"], dtype=np.float32)
    conv = np.asarray(inp["state_conv"], dtype=np.float32)
    ckw = np.asarray(inp["cache_k_win"], dtype=np.float32)
    cvw = np.asarray(inp["cache_v_win"], dtype=np.float32)
    maps = []
    for c in range(8):
        m = dict(shared)
        m["x_p"] = f(xp[c])
        m["x_s"] = f(xs[4 * c:4 * c + 4].reshape(NSS * TS, D))
        m["st_ssm"] = f(ssm[:, 4 * c:4 * c + 4].reshape(2, NSS, 2048, 128))
        m["st_conv"] = f(conv[:, 4 * c:4 * c + 4])
        m["ck"] = f(ckw[4 * c:4 * c + 4].reshape(NSS, 128, 256))
        m["cv"] = f(cvw[4 * c:4 * c + 4].reshape(NSS, 128, 256))
        maps.append(m)
    return maps


def assemble(results):
    r = results
    y_prompt = np.stack([r[c]["y_p"] for c in range(8)], 0)
    y_sample = np.concatenate([r[c]["y_s"].reshape(NSS, TS, D) for c in range(8)], 0)
    pr_ssm = np.stack([r[c]["o_pssm"].reshape(2, 32, 64, 128) for c in range(8)], 1)
    pr_conv = np.stack([r[c]["o_pconv"] for c in range(8)], 1)
    pr_k = np.stack([r[c]["o_pk"].reshape(128, 4, 64) for c in range(8)], 0)
    pr_v = np.stack([r[c]["o_pv"].reshape(128, 4, 64) for c in range(8)], 0)
    sm_ssm = np.concatenate([r[c]["o_sssm"].reshape(2, NSS, 32, 64, 128) for c in range(8)], 1)
    sm_conv = np.concatenate([r[c]["o_sconv"] for c in range(8)], 1)
    sm_k = np.concatenate([r[c]["o_sk"].reshape(NSS, 128, 4, 64) for c in range(8)], 0)
    sm_v = np.concatenate([r[c]["o_sv"].reshape(NSS, 128, 4, 64) for c in range(8)], 0)
    outs = (y_prompt, y_sample, pr_ssm, pr_conv, pr_k, pr_v, sm_ssm, sm_conv, sm_k, sm_v)
    return tuple(np.ascontiguousarray(o, dtype=np.float32) for o in outs)


def kernel(**inputs):
    if "nc" not in _CACHE:
        _CACHE["nc"] = build_program()
    nc = _CACHE["nc"]
    maps = make_in_maps(inputs)
    res = run_bass_kernel_spmd(nc, maps, core_ids=list(range(8)))
    return assemble(res.results)
```

```python
import os
import numpy as np
from contextlib import ExitStack
import concourse.bass as bass
import concourse.mybir as mybir
from concourse.bass_utils import run_bass_kernel_spmd

F32 = mybir.dt.float32
U32 = mybir.dt.uint32
ALU = mybir.AluOpType
ACT = mybir.ActivationFunctionType
AX = mybir.AxisListType

D = 1024
SEQ = 2048
NPT = 16
NSS = 4
TS = 16
NEG = -30000.0
ILV_MAXL = int(os.environ.get('K_ILVL', '4'))


class Buf:
    __slots__ = ("name", "wr", "rd", "dsem", "dcnt")

    def __init__(self, name):
        self.name = name
        self.wr = []
        self.rd = []
        self.dsem = None
        self.dcnt = 0


class FW:
    def __init__(self, nc, stack):
        self.nc = nc
        self.stack = stack
        self.engs = {"pe": nc.tensor, "dve": nc.vector, "act": nc.scalar,
                     "pool": nc.gpsimd, "sp": nc.sync}
        self.sems = {}
        self.cnt = {}
        self.seen = {k: {} for k in self.engs}
        for k in self.engs:
            self.sems[k] = stack.enter_context(nc.semaphore("s_" + k))
            self.cnt[k] = 0
        self.nbuf = 0
        self.ninst = 0
        self.nwait = 0
        self.allbufs = []
        self.stage = 0
        self.stop_at = int(os.environ.get('K_STAGE', '0'))

    def buf(self, name=None):
        self.nbuf += 1
        b = Buf(name or "b%d" % self.nbuf)
        self.allbufs.append(b)
        return b

    def sb(self, name, shape, dt=F32):
        t = self.stack.enter_context(self.nc.sbuf_tensor(name, list(shape), dt))
        return t, self.buf(name)

    def _dsem(self, b):
        if b.dsem is None:
            key = "ds%d" % len(self.sems)
            b.dsem = key
            self.sems[key] = self.stack.enter_context(self.nc.semaphore(key))
        return b.dsem

    def _wait(self, e, events):
        need = {}
        for (k, v) in events:
            if need.get(k, 0) < v:
                need[k] = v
        seen = self.seen[e]
        for k, v in need.items():
            if e == "pe" and k == "pe":
                continue
            if seen.get(k, 0) >= v:
                continue
            self.engs[e].wait_ge(self.sems[k], v)
            self.nwait += 1
            seen[k] = v

    def _deps(self, e, reads, writes):
        ev = []
        for b in reads:
            ev.extend(b.wr)
        for b in writes:
            ev.extend(b.wr)
            ev.extend(b.rd)
        self._wait(e, ev)

    def _commit(self, event, reads, writes):
        for b in reads:
            b.rd.append(event)
            if len(b.rd) > 48:
                m = {}
                for (k, v) in b.rd:
                    if m.get(k, 0) < v:
                        m[k] = v
                b.rd = list(m.items())
        for b in writes:
            b.wr = [event]
            b.rd = []

    def op(self, e, fn, reads=(), writes=()):
        self._deps(e, reads, writes)
        ins = fn(self.engs[e])
        self.cnt[e] += 1
        ins.then_inc(self.sems[e], 1)
        self._commit((e, self.cnt[e]), reads, writes)
        self.ninst += 1
        return ins

    def dma(self, q, out, in_, reads=(), writes=(), owner=None, fn=None, slow=False):
        self._deps(q, reads, writes)
        key = self._dsem(owner)
        if fn is not None:
            ins = fn(self.engs[q])
        elif slow:
            ins = self.engs[q].dma_start(out=out, in_=in_, allow_slow_non_contiguous=True)
        else:
            ins = self.engs[q].dma_start(out=out, in_=in_)
        owner.dcnt += 16
        ins.then_inc(self.sems[key], 16)
        self._commit((key, owner.dcnt), reads, writes)
        self.ninst += 1
        return ins

    def mark(self, n):
        if self.stop_at and n >= self.stop_at:
            raise Stop()

    def finish(self, bufs, e="sp"):
        ev = []
        for b in bufs:
            ev.extend(b.wr)
            ev.extend(b.rd)
        self._wait(e, ev)


class Stop(Exception):
    pass


def bc(ap, shape):
    return ap.to_broadcast(list(shape))


def build_program(debug=False, nlayers=4, ntiles_p=NPT, nseq_s=NSS):
    nc = bass.Bass("TRN2", target_bir_lowering=False)

    def din(name, shape, dt=F32):
        return nc.dram_tensor(name, list(shape), dt, kind="ExternalInput").ap()

    def dout(name, shape, dt=F32):
        return nc.dram_tensor(name, list(shape), dt, kind="ExternalOutput").ap()

    x_p = din("x_p", [SEQ, D])
    x_s = din("x_s", [NSS * TS, D])
    st_ssm = din("st_ssm", [2, NSS, 2048, 128])
    st_conv = din("st_conv", [2, NSS, 3, 4096])
    ck = din("ck", [NSS, 128, 256])
    cv = din("cv", [NSS, 128, 256])
    norm_mix = din("norm_mix", [4, D])
    norm_ffn = din("norm_ffn", [4, D])
    norm_kv = din("norm_kv", [1, D])
    norm_final = din("norm_final", [1, D])
    m_w_in = din("m_w_in", [2, D, 6176])
    m_conv_w = din("m_conv_w", [2, 128, 32, 4])
    m_conv_b = din("m_conv_b", [2, 128, 32])
    m_dt_bias = din("m_dt_bias", [2, 32])
    m_a_log = din("m_a_log", [2, 32])
    m_d_skip = din("m_d_skip", [2, 32])
    m_norm = din("m_norm", [2, 2048])
    m_w_out = din("m_w_out", [2, 2048, D])
    a_w_kv = din("a_w_kv", [D, 512])
    a_b_kv = din("a_b_kv", [1, 512])
    a_b_kT = din("a_b_kT", [64, 4])
    a_w_q = din("a_w_q", [2, D, D])
    a_b_qT = din("a_b_qT", [2, 64, 16])
    a_sinks = din("a_sinks", [2, 16])
    a_w_o = din("a_w_o", [2, D, D])
    a_b_o = din("a_b_o", [2, D])
    p_w_q = din("p_w_q", [4, D, 2048])
    p_k1T = din("p_k1T", [4, 128, 128])
    p_k2T = din("p_k2T", [4, 128, 128])
    p_u = din("p_u", [4 * 16384, D])
    p_v = din("p_v", [4 * 16384, D])

    y_p = dout("y_p", [SEQ, D])
    y_s = dout("y_s", [NSS * TS, D])
    o_pssm = dout("o_pssm", [2, 2048, 128])
    o_pconv = dout("o_pconv", [2, 3, 4096])
    o_pk = dout("o_pk", [128, 256])
    o_pv = dout("o_pv", [128, 256])
    o_sssm = dout("o_sssm", [2, NSS, 2048, 128])
    o_sconv = dout("o_sconv", [2, NSS, 3, 4096])
    o_sk = dout("o_sk", [NSS, 128, 256])
    o_sv = dout("o_sv", [NSS, 128, 256])

    NTOK = SEQ + NSS * TS
    hck = []
    for i in range(2 * 4):
        if debug:
            hck.append(dout("hck%d" % i, [NTOK, D]))
        else:
            hck.append(nc.dram_tensor("hck%d" % i, [NTOK, D], F32).ap())
    kT_d = nc.dram_tensor("kT_d", [NPT + NSS, 64, 4, 128], F32).ap()
    v_d = nc.dram_tensor("v_d", [NPT + NSS, 128, 256], F32).ap()

    with ExitStack() as st:
        fw = FW(nc, st)
        op = fw.op

        class Tile:
            pass
        tiles = []
        for i in range(ntiles_p):
            t = Tile()
            t.kind, t.idx, t.nt, t.row0 = "p", i, 128, i * 128
            t.chunks = [(0, 64), (64, 64)]
            tiles.append(t)
        for s in range(nseq_s):
            t = Tile()
            t.kind, t.idx, t.nt, t.row0 = "s", s, TS, SEQ + s * TS
            t.chunks = [(0, TS)]
            tiles.append(t)
        for t in tiles:
            t.hb = [fw.buf("h%d_%s%d" % (i, t.kind, t.idx)) for i in range(9)]
            t.kvslot = NPT + t.idx if t.kind == "s" else t.idx
            t.kTd_b = fw.buf()
            t.vd_b = fw.buf()

        speer = Tile()
        speer.kind, speer.idx, speer.nt, speer.row0 = "sp", 0, nseq_s * TS, SEQ
        stiles = [t for t in tiles if t.kind == "s"]

        def hbl(t, ci):
            if t.kind == "sp":
                return [s_.hb[ci] for s_ in stiles]
            return [t.hb[ci]]

        def h_ap(ci, t):
            if ci == 0:
                return x_p[t.row0:t.row0 + t.nt, :] if t.kind == "p" else x_s[t.idx * TS:(t.idx + 1) * TS, :]
            return hck[ci - 1][t.row0:t.row0 + t.nt, :]

        P = []
        for i in range(8):
            P.append(fw.sb("P%d" % i, [128, 2048]))
        xbc_t, xbc_b = fw.sb("xbc", [128, 32, 67])
        halo_t, halo_b = fw.sb("halo", [128, 32, 3])
        xnT_t, xnT_b = fw.sb("xnT", [128, 8, 128])
        ht_t, ht_b = fw.sb("ht", [128, 1024])
        xn_t, xn_b = fw.sb("xn", [128, 1024])
        W = [fw.sb("W%d" % i, [128, 4096]) for i in range(2)]
        NG = 8
        G = [fw.sb("G%d" % i, [128, 1024]) for i in range(NG)]
        ST_t, ST_b = fw.sb("ST", [128, 2048])
        kTr = [fw.sb("kTr%d" % i, [64, 4, 128]) for i in range(2)]
        vr = [fw.sb("vr%d" % i, [128, 256]) for i in range(2)]
        gmix_t, gmix_b = fw.sb("gmix", [128, 1024])
        gffn_t, gffn_b = fw.sb("gffn", [128, 1024])
        gX_t, gX_b = fw.sb("gX", [128, 2048])
        ident_t, ident_b = fw.sb("ident", [128, 128])
        tri_t, tri_b = fw.sb("tri", [64, 64])
        negm_t, negm_b = fw.sb("negm", [64, 64])
        ones_t, ones_b = fw.sb("ones", [64, 128])
        iota_t, iota_b = fw.sb("iota16", [128, 16])
        thr_t, thr_b = fw.sb("thr15", [128, 15])
        cst_t, cst_b = fw.sb("cst", [128, 4])
        lc_t, lc_b = fw.sb("lc", [128, 320])
        amask_t, amask_b = fw.sb("amask", [128, 384])
        sk_t, sk_b = fw.sb("subk", [128, 2, 128])
        sm_t, sm_b = fw.sb("small", [128, 1024])
        bo_t, bo_b = fw.sb("bo", [128, 1024])
        htp_t, htp_b = fw.sb("htp", [128, 1024])
        xnp_t, xnp_b = fw.sb("xnp", [128, 1024])
        psm_t, psm_b = fw.sb("psm", [128, 1288])
        eid_t, eid_b = fw.sb("eid", [128, 128], U32)
        idx_t, idx_b = fw.sb("idxu", [128, 384], U32)

        PS = []
        for i in range(8):
            t_ = st.enter_context(nc.psum_tensor("ps%d" % i, [128, 512], F32))
            PS.append((t_, fw.buf("ps%d" % i)))
        ps_i = [0]

        def psum():
            r = PS[ps_i[0] % 8]
            ps_i[0] += 1
            return r

        w_i = [0]

        def wslot():
            r = W[w_i[0] % 2]
            w_i[0] += 1
            return r

        cb_ = cst_b
        op("pool", lambda e: e.memset(cst_t[:, 0:1], 1e-6), writes=[cb_])
        op("pool", lambda e: e.memset(cst_t[:, 1:2], 1.0), writes=[cb_])
        op("pool", lambda e: e.memset(cst_t[:, 2:4], 0.0), writes=[cb_])
        EPS = cst_t[:, 0:1]
        ONE = cst_t[:, 1:2]
        op("pool", lambda e: e.memset(ident_t[:], 0.0), writes=[ident_b])
        op("pool", lambda e: e.affine_select(out=ident_t[:], in_=ident_t[:], pattern=[[-1, 128]],
                                             compare_op=ALU.not_equal, fill=1.0, base=0,
                                             channel_multiplier=1), reads=[ident_b], writes=[ident_b])
        op("pool", lambda e: e.memset(tri_t[:], 1.0), writes=[tri_b])
        op("pool", lambda e: e.affine_select(out=tri_t[:], in_=tri_t[:], pattern=[[1, 64]],
                                             compare_op=ALU.is_ge, fill=0.0, base=0,
                                             channel_multiplier=-1), reads=[tri_b], writes=[tri_b])
        op("pool", lambda e: e.memset(negm_t[:], 0.0), writes=[negm_b])
        op("pool", lambda e: e.affine_select(out=negm_t[:], in_=negm_t[:], pattern=[[1, 64]],
                                             compare_op=ALU.is_ge, fill=NEG, base=0,
                                             channel_multiplier=-1), reads=[negm_b], writes=[negm_b])
        op("pool", lambda e: e.memset(ones_t[:], 1.0), writes=[ones_b])
        op("pool", lambda e: e.iota(iota_t[:], pattern=[[1, 16]], base=0, channel_multiplier=0,
                                    allow_small_or_imprecise_dtypes=True), writes=[iota_b])
        op("pool", lambda e: e.iota(thr_t[:], pattern=[[16, 15]], base=16, channel_multiplier=0,
                                    allow_small_or_imprecise_dtypes=True), writes=[thr_b])
        op("pool", lambda e: e.memset(amask_t[:], 0.0), writes=[amask_b])
        op("pool", lambda e: e.memset(amask_t[0:64, 192:256], NEG), reads=[amask_b], writes=[amask_b])
        op("pool", lambda e: e.memset(amask_t[64:128, 0:64], NEG), reads=[amask_b], writes=[amask_b])
        op("pool", lambda e: e.memset(amask_t[0:64, 256 + 64:256 + 128], NEG), reads=[amask_b], writes=[amask_b])

        def rmsnorm(src_t, src_b, nt, g_ap, g_b, dst_t, dst_b, ss_ap, ss_b):
            op("dve", lambda e: e.memset(ss_ap[:, 0:1], 0.0), writes=[ss_b])
            op("act", lambda e: e.activation(out=dst_t[0:nt, :], in_=src_t[0:nt, :], func=ACT.Square,
                                             accum_out=ss_ap[:, 0:1]), reads=[src_b, ss_b], writes=[dst_b, ss_b])
            op("act", lambda e: e.activation(out=ss_ap[:, 1:2], in_=ss_ap[:, 0:1], func=ACT.Sqrt,
                                             scale=1.0 / D, bias=EPS[0:nt, :]), reads=[ss_b, cst_b], writes=[ss_b])
            op("dve", lambda e: e.reciprocal(out=ss_ap[:, 2:3], in_=ss_ap[:, 1:2]), reads=[ss_b], writes=[ss_b])
            op("dve", lambda e: e.scalar_tensor_tensor(out=dst_t[0:nt, :], in0=src_t[0:nt, :], scalar=ss_ap[:, 2:3],
                                                        in1=g_ap[0:nt, :], op0=ALU.mult, op1=ALU.mult),
               reads=[src_b, ss_b, g_b], writes=[dst_b])

        def to_featmajor(src_t, src_b, nt, ncols, dstT, dstT_b, tok0=0, eng_alt=True):
            nb = ncols // 128
            for b0 in range(0, nb, 4):
                pt, pb = psum()
                nn = min(4, nb - b0)
                for c in range(nn):
                    op("pe", lambda e, c=c: e.transpose(out=pt[:, c * 128:c * 128 + nt],
                                                        in_=src_t[0:nt, (b0 + c) * 128:(b0 + c + 1) * 128],
                                                        identity=ident_t[0:nt, 0:nt]),
                       reads=[src_b, ident_b], writes=[pb])
                src_v = pt[:, 0:nn * 128].rearrange("p (c n) -> p c n", c=nn)[:, :, 0:nt]
                eng = "act" if (b0 // 4) % 2 == 0 else "dve"
                if eng == "act":
                    op("act", lambda e: e.copy(out=dstT[:, b0:b0 + nn, tok0:tok0 + nt], in_=src_v),
                       reads=[pb], writes=[dstT_b])
                else:
                    op("dve", lambda e: e.tensor_copy(out=dstT[:, b0:b0 + nn, tok0:tok0 + nt], in_=src_v),
                       reads=[pb], writes=[dstT_b])

        def load_w(w_ap, nk, c0, ncols, prows=128):
            wt, wb = wslot()
            view = wt[0:prows, 0:nk * ncols].rearrange("p (k n) -> p k n", k=nk)
            src = w_ap[:, c0:c0 + ncols].rearrange("(k p) n -> p k n", p=prows)
            fw.dma("sp", view, src, writes=[wb], owner=wb)
            return view, wb

        def load_bc(dst_ap, dst_b, src_row_ap, nparts=128):
            fw.dma("sp", dst_ap, src_row_ap.partition_broadcast(nparts), writes=[dst_b], owner=dst_b)

        sm = sm_t
        smb = sm_b

        def mamba_tile(l, t, ci):
            nt = t.nt
            first_tile = (t.kind == "s") or (t.idx == 0)
            last_tile = (t.kind == "s") or (t.idx == NPT - 1)
            fw.dma("sp", ht_t[0:nt, :], h_ap(ci, t), reads=[t.hb[ci]], writes=[ht_b], owner=ht_b)
            rmsnorm(ht_t, ht_b, nt, gmix_t, gmix_b, xn_t, xn_b, sm[0:nt, 0:4], smb)
            to_featmajor(xn_t, xn_b, nt, D, xnT_t, xnT_b)
            fw.mark(1)
            yield
            w_in = m_w_in[l]
            if first_tile:
                if t.kind == "p":
                    op("pool", lambda e: e.memset(ST_t[:], 0.0), writes=[ST_b])
                    op("pool", lambda e: e.memset(halo_t[:], 0.0), writes=[halo_b])
                else:
                    for c0 in range(32):
                        src = st_conv[l, t.idx][:, c0 * 128:(c0 + 1) * 128].rearrange("r p -> p r")
                        fw.dma("sp", halo_t[:, c0, :], src, writes=[halo_b], owner=halo_b, slow=True)
                    stg_t, stg_b = P[4]
                    fw.dma("sp", stg_t[:].rearrange("p (m n) -> p m n", m=16),
                           st_ssm[l, t.idx].rearrange("(m p) n -> p m n", p=128), writes=[stg_b], owner=stg_b)
                    to_featmajor(stg_t, stg_b, 128, 2048, ST_t[:].rearrange("p (m n) -> p m n", m=16), ST_b)

            ynT_t, ynT_b = P[7]
            ynT = ynT_t[:].rearrange("p (k n) -> p k n", k=16)
            for (q0, Q) in t.chunks:
                xc_t, xc_b = P[0]
                xt_t, xt_b = P[1]
                zs_t, zs_b = P[2]
                ab_t, ab_b = P[3]
                lm_t, lm_b = P[4]
                xd_t, xd_b = P[5]
                bt_t, bt_b = P[6]
                xc = xc_t[:, 0:32 * Q].rearrange("p (c n) -> p c n", c=32)
                abc = ab_t[:, 0:32 * Q].rearrange("p (c n) -> p c n", c=32)
                lm = lm_t[0:Q, 0:32 * Q].rearrange("p (c n) -> p c n", c=32)
                xbe = xbc_t[:, :, 0:Q + 3]
                op("pool", lambda e: e.tensor_copy(out=xbc_t[:, :, 0:3], in_=halo_t[:]), reads=[halo_b], writes=[xbc_b])
                for blk in range(8):
                    wv, wb = load_w(w_in, 8, 2048 + blk * 512, 512)
                    pt, pb = psum()
                    for cc in range(4):
                        for k in range(8):
                            op("pe", lambda e, cc=cc, k=k: e.matmul(pt[:, cc * Q:(cc + 1) * Q], lhsT=wv[:, k, cc * 128:(cc + 1) * 128],
                                                                     rhs=xnT_t[:, k, q0:q0 + Q], start=(k == 0), stop=(k == 7)),
                               reads=[wb, xnT_b], writes=[pb])
                        yield
                    op("act", lambda e: e.copy(out=xbc_t[:, blk * 4:blk * 4 + 4, 3:3 + Q],
                                               in_=pt[:, 0:4 * Q].rearrange("p (c n) -> p c n", c=4)),
                       reads=[pb], writes=[xbc_b])
                fw.mark(2)
                for blk in range(4):
                    wv, wb = load_w(w_in, 8, blk * 512, 512)
                    pt, pb = psum()
                    for k in range(8):
                        op("pe", lambda e, k=k: e.matmul(pt[0:Q, :], lhsT=xnT_t[:, k, q0:q0 + Q], rhs=wv[:, k, :],
                                                         start=(k == 0), stop=(k == 7)), reads=[wb, xnT_b], writes=[pb])
                    op("act", lambda e: e.activation(out=zs_t[0:Q, blk * 512:(blk + 1) * 512], in_=pt[0:Q, :], func=ACT.Silu),
                       reads=[pb], writes=[zs_b])
                    yield
                wv, wb = load_w(w_in, 8, 6144, 32)
                pt, pb = psum()
                for k in range(8):
                    op("pe", lambda e, k=k: e.matmul(pt[0:Q, 0:32], lhsT=xnT_t[:, k, q0:q0 + Q], rhs=wv[:, k, :],
                                                     start=(k == 0), stop=(k == 7)), reads=[wb, xnT_b], writes=[pb])
                s_x = sm[0:Q, 32:64]
                s_ax = sm[0:Q, 64:96]
                s_e = sm[0:Q, 96:128]
                s_dt = sm[0:Q, 128:160]
                s_a = sm[0:Q, 160:192]
                s_ac = sm[0:Q, 192:224]
                dec_bc = sm[:, 224:256]
                R = [smb]
                op("dve", lambda e: e.tensor_tensor(out=s_x, in0=pt[0:Q, 0:32], in1=lc_t[0:Q, 0:32], op=ALU.add),
                   reads=[pb, lc_b], writes=R)
                op("act", lambda e: e.activation(out=s_ax, in_=s_x, func=ACT.Abs), reads=R, writes=R)
                op("act", lambda e: e.activation(out=s_e, in_=s_ax, func=ACT.Exp, scale=-1.0), reads=R, writes=R)
                op("act", lambda e: e.activation(out=s_e, in_=s_e, func=ACT.Ln, bias=ONE[0:Q, :]), reads=R + [cst_b], writes=R)
                op("dve", lambda e: e.tensor_scalar_max(out=s_x, in0=s_x, scalar1=0.0), reads=R, writes=R)
                op("dve", lambda e: e.tensor_tensor(out=s_dt, in0=s_x, in1=s_e, op=ALU.add), reads=R, writes=R)
                op("dve", lambda e: e.tensor_tensor(out=s_a, in0=s_dt, in1=lc_t[0:Q, 32:64], op=ALU.mult), reads=R + [lc_b], writes=R)
                fw.mark(3)
                yield
                tmp = xd_t[:, 0:32 * Q].rearrange("p (c n) -> p c n", c=32)
                cw = lc_t[:, 128:256].rearrange("p (c k) -> p c k", k=4)
                cbias = lc_t[:, 256:288]
                op("dve", lambda e: e.tensor_tensor(out=xc, in0=xbe[:, :, 0:Q], in1=bc(cw[:, :, 0:1], [128, 32, Q]), op=ALU.mult),
                   reads=[xbc_b, lc_b], writes=[xc_b])
                for k in range(1, 4):
                    op("dve", lambda e, k=k: e.tensor_tensor(out=tmp, in0=xbe[:, :, k:k + Q], in1=bc(cw[:, :, k:k + 1], [128, 32, Q]), op=ALU.mult),
                       reads=[xbc_b, lc_b], writes=[xd_b])
                    op("dve", lambda e: e.tensor_tensor(out=xc, in0=xc, in1=tmp, op=ALU.add), reads=[xc_b, xd_b], writes=[xc_b])
                    yield
                op("dve", lambda e: e.tensor_tensor(out=xc, in0=xc, in1=bc(cbias.unsqueeze(2), [128, 32, Q]), op=ALU.add),
                   reads=[xc_b, lc_b], writes=[xc_b])
                op("act", lambda e: e.activation(out=xc, in_=xc, func=ACT.Silu), reads=[xc_b], writes=[xc_b])
                op("pool", lambda e: e.tensor_copy(out=halo_t[:], in_=xbc_t[:, :, Q:Q + 3]), reads=[xbc_b], writes=[halo_b])
                fw.mark(4)
                yield
                for b0 in range(0, 24, 4):
                    pt2, pb2 = psum()
                    for c in range(4):
                        op("pe", lambda e, c=c: e.transpose(out=pt2[0:Q, c * 128:(c + 1) * 128], in_=xc[:, b0 + c, :], identity=ident_t[:]),
                           reads=[xc_b, ident_b], writes=[pb2])
                    if b0 < 16:
                        dst, dstb = xt_t[0:Q, b0 * 128:(b0 + 4) * 128], xt_b
                    else:
                        dst, dstb = bt_t[0:Q, (b0 - 16) * 128:(b0 - 12) * 128], bt_b
                    if (b0 // 4) % 2 == 0:
                        op("act", lambda e: e.copy(out=dst, in_=pt2[0:Q, :]), reads=[pb2], writes=[dstb])
                    else:
                        op("dve", lambda e: e.tensor_copy(out=dst, in_=pt2[0:Q, :]), reads=[pb2], writes=[dstb])
                    yield
                fw.mark(5)
                yield
                pt3, pb3 = psum()
                op("pe", lambda e: e.matmul(pt3[0:Q, 0:32], lhsT=tri_t[0:Q, 0:Q], rhs=s_a, start=True, stop=True),
                   reads=[tri_b] + R, writes=[pb3])
                op("act", lambda e: e.copy(out=s_ac, in_=pt3[0:Q, 0:32]), reads=[pb3], writes=R)
                op("dve", lambda e: e.tensor_tensor(out=lm, in0=bc(s_a.unsqueeze(2), [Q, 32, Q]),
                                                    in1=bc(tri_t[0:Q, 0:Q].unsqueeze(1), [Q, 32, Q]), op=ALU.mult),
                   reads=R + [tri_b], writes=[lm_b])
                ncol = 32 * Q
                for c0 in range(0, ncol, 512):
                    n_ = min(512, ncol - c0)
                    pt4, pb4 = psum()
                    op("pe", lambda e: e.matmul(pt4[:, 0:n_], lhsT=ones_t[0:Q, :], rhs=lm_t[0:Q, c0:c0 + n_], start=True, stop=True),
                       reads=[ones_b, lm_b], writes=[pb4])
                    op("act", lambda e: e.copy(out=ab_t[:, c0:c0 + n_], in_=pt4[:, 0:n_]), reads=[pb4], writes=[ab_b])
                    yield
                fw.mark(6)
                yield
                op("dve", lambda e: e.tensor_tensor(out=lm, in0=abc[0:Q], in1=bc(s_ac.unsqueeze(2), [Q, 32, Q]), op=ALU.subtract),
                   reads=[ab_b] + R, writes=[lm_b])
                op("dve", lambda e: e.tensor_tensor(out=lm, in0=lm, in1=bc(negm_t[0:Q, 0:Q].unsqueeze(1), [Q, 32, Q]), op=ALU.add),
                   reads=[lm_b, negm_b], writes=[lm_b])
                op("act", lambda e: e.activation(out=lm, in_=lm, func=ACT.Exp), reads=[lm_b], writes=[lm_b])
                s_de = sm[0:Q, 288:320]
                op("dve", lambda e: e.tensor_copy(out=s_de, in_=lm[:, :, Q - 1]), reads=[lm_b], writes=R)
                op("act", lambda e: e.activation(out=abc, in_=abc, func=ACT.Exp), reads=[ab_b], writes=[ab_b])
                op("dve", lambda e: e.tensor_copy(out=dec_bc, in_=abc[:, :, Q - 1]), reads=[ab_b], writes=R)
                abc4 = ab_t[:, 0:32 * Q].rearrange("p (g r n) -> p g r n", g=8, r=4)
                op("dve", lambda e: e.tensor_tensor(out=abc4, in0=abc4, in1=bc(xc[:, 24:32, :].unsqueeze(2), [128, 8, 4, Q]), op=ALU.mult),
                   reads=[ab_b, xc_b], writes=[ab_b])
                fw.mark(7)
                yield
                pt5, pb5 = psum()
                for g in range(8):
                    op("pe", lambda e, g=g: e.matmul(pt5[0:Q, g * Q:(g + 1) * Q], lhsT=xc[:, 16 + g, :], rhs=xc[:, 24 + g, :], start=True, stop=True),
                       reads=[xc_b], writes=[pb5])
                lm4 = lm_t[0:Q, 0:32 * Q].rearrange("p (g r n) -> p g r n", g=8, r=4)
                cb4 = pt5[0:Q, 0:8 * Q].rearrange("p (g n) -> p g n", g=8)
                op("dve", lambda e: e.tensor_tensor(out=lm4, in0=lm4, in1=bc(cb4.unsqueeze(2), [Q, 8, 4, Q]), op=ALU.mult),
                   reads=[lm_b, pb5], writes=[lm_b])
                xt3 = xt_t[0:Q, :].rearrange("p (h d) -> p h d", h=32)
                xd3 = xd_t[0:Q, :].rearrange("p (h d) -> p h d", h=32)
                op("dve", lambda e: e.tensor_tensor(out=xd3, in0=xt3, in1=bc(s_dt.unsqueeze(2), [Q, 32, 64]), op=ALU.mult),
                   reads=[xt_b] + R, writes=[xd_b])
                op("dve", lambda e: e.tensor_tensor(out=xt3, in0=xt3, in1=bc(lc_t[0:Q, 64:96].unsqueeze(2), [Q, 32, 64]), op=ALU.mult),
                   reads=[xt_b, lc_b], writes=[xt_b])
                fw.mark(8)
                yield
                for grp in range(4):
                    pt6, pb6 = psum()
                    for hh in range(8):
                        h_ = grp * 8 + hh
                        op("pe", lambda e, h_=h_, hh=hh: e.matmul(pt6[0:Q, hh * 64:(hh + 1) * 64], lhsT=lm[:, h_, :], rhs=xd_t[0:Q, h_ * 64:(h_ + 1) * 64],
                                                                   start=True, stop=False), reads=[lm_b, xd_b], writes=[pb6])
                        op("pe", lambda e, h_=h_, hh=hh: e.matmul(pt6[0:Q, hh * 64:(hh + 1) * 64], lhsT=abc[:, h_, :], rhs=ST_t[:, h_ * 64:(h_ + 1) * 64],
                                                                   start=False, stop=True), reads=[ab_b, ST_b], writes=[pb6])
                    op("dve", lambda e: e.tensor_tensor(out=xt_t[0:Q, grp * 512:(grp + 1) * 512], in0=xt_t[0:Q, grp * 512:(grp + 1) * 512],
                                                        in1=pt6[0:Q, :], op=ALU.add), reads=[xt_b, pb6], writes=[xt_b])
                    yield
                op("dve", lambda e: e.tensor_tensor(out=xd3, in0=xd3, in1=bc(s_de.unsqueeze(2), [Q, 32, 64]), op=ALU.mult),
                   reads=[xd_b] + R, writes=[xd_b])
                fw.mark(9)
                yield
                ST3 = ST_t[:].rearrange("p (h d) -> p h d", h=32)
                for grp in range(4):
                    pt7, pb7 = psum()
                    for hh in range(8):
                        h_ = grp * 8 + hh
                        g = h_ // 4
                        op("pe", lambda e, h_=h_, hh=hh, g=g: e.matmul(pt7[:, hh * 64:(hh + 1) * 64], lhsT=bt_t[0:Q, g * 128:(g + 1) * 128],
                                                                        rhs=xd_t[0:Q, h_ * 64:(h_ + 1) * 64], start=True, stop=True),
                           reads=[bt_b, xd_b], writes=[pb7])
                    sl = ST3[:, grp * 8:(grp + 1) * 8, :]
                    op("dve", lambda e: e.tensor_tensor(out=sl, in0=sl, in1=bc(dec_bc[:, grp * 8:(grp + 1) * 8].unsqueeze(2), [128, 8, 64]), op=ALU.mult),
                       reads=[ST_b] + R, writes=[ST_b])
                    op("dve", lambda e: e.tensor_tensor(out=ST_t[:, grp * 512:(grp + 1) * 512], in0=ST_t[:, grp * 512:(grp + 1) * 512],
                                                        in1=pt7[:, :], op=ALU.add), reads=[ST_b, pb7], writes=[ST_b])
                    yield
                fw.mark(10)
                yield
                y = xt_t[0:Q, :]
                op("dve", lambda e: e.tensor_tensor(out=y, in0=y, in1=zs_t[0:Q, :], op=ALU.mult), reads=[xt_b, zs_b], writes=[xt_b])
                ss = sm[0:Q, 0:4]
                op("dve", lambda e: e.memset(ss[:, 0:1], 0.0), writes=R)
                op("act", lambda e: e.activation(out=zs_t[0:Q, :], in_=y, func=ACT.Square, accum_out=ss[:, 0:1]),
                   reads=[xt_b] + R, writes=[zs_b] + R)
                op("act", lambda e: e.activation(out=ss[:, 1:2], in_=ss[:, 0:1], func=ACT.Sqrt, scale=1.0 / 2048, bias=EPS[0:Q, :]),
                   reads=R + [cst_b], writes=R)
                op("dve", lambda e: e.reciprocal(out=ss[:, 2:3], in_=ss[:, 1:2]), reads=R, writes=R)
                op("dve", lambda e: e.scalar_tensor_tensor(out=y, in0=y, scalar=ss[:, 2:3], in1=gX_t[0:Q, :], op0=ALU.mult, op1=ALU.mult),
                   reads=[xt_b, gX_b] + R, writes=[xt_b])
                to_featmajor(xt_t, xt_b, Q, 2048, ynT, ynT_b, tok0=q0)

            fw.mark(11)
            yield
            for blk in range(4):
                wv, wb = load_w(m_w_out[l], 16, blk * 256, 256)
                pt, pb = psum()
                for k in range(16):
                    op("pe", lambda e, k=k: e.matmul(pt[0:nt, 0:256], lhsT=ynT[:, k, 0:nt], rhs=wv[:, k, :], start=(k == 0), stop=(k == 15)),
                       reads=[wb, ynT_b], writes=[pb])
                op("dve", lambda e: e.tensor_tensor(out=ht_t[0:nt, blk * 256:(blk + 1) * 256], in0=ht_t[0:nt, blk * 256:(blk + 1) * 256],
                                                    in1=pt[0:nt, 0:256], op=ALU.add), reads=[ht_b, pb], writes=[ht_b])
                yield
            fw.dma("sp", h_ap(ci + 1, t), ht_t[0:nt, :], reads=[ht_b], writes=[t.hb[ci + 1]], owner=ht_b)

            fw.mark(12)
            if last_tile:
                stg_t, stg_b = P[4]
                to_featmajor(ST_t, ST_b, 128, 2048, stg_t[:].rearrange("p (m n) -> p m n", m=16), stg_b)
                dst = o_pssm[l] if t.kind == "p" else o_sssm[l, t.idx]
                fw.dma("sp", dst.rearrange("(m p) n -> p m n", p=128), stg_t[:].rearrange("p (m n) -> p m n", m=16),
                       reads=[stg_b], owner=stg_b)
                outs_b.append(stg_b)
                fw.mark(13)
                dstc = o_pconv[l] if t.kind == "p" else o_sconv[l, t.idx]
                for c0 in range(32):
                    dv = dstc[:, c0 * 128:(c0 + 1) * 128].rearrange("r p -> p r")
                    fw.dma("sp", dv, halo_t[:, c0, :], reads=[halo_b], owner=halo_b, slow=True)
                outs_b.append(halo_b)

        def attn_tile(l, t, ci):
            j = l - 2
            nt = t.nt
            fw.dma("sp", ht_t[0:nt, :], h_ap(ci, t), reads=[t.hb[ci]], writes=[ht_b], owner=ht_b)
            xkT_t, xkT_b = P[7]
            xkT = xkT_t[:, 0:1024].rearrange("p (k n) -> p k n", k=8)
            stg_t, stg_b = P[6]
            if t.kind == "p":
                cur = t.idx % 2
                prev = 1 - cur
                blocks = ([(prev, 128)] if t.idx > 0 else []) + [(cur, 128)]
            else:
                prev, cur = 0, 1
                blocks = [(prev, 128), (cur, TS)]
                fw.dma("sp", stg_t[:, 0:256], ck[t.idx], writes=[stg_b], owner=stg_b)
                fw.dma("sp", vr[prev][0][:, :], cv[t.idx], writes=[vr[prev][1]], owner=vr[prev][1])
                if j == 0:
                    fw.dma("sp", o_sk[t.idx][0:128 - TS, :], stg_t[TS:128, 0:256], reads=[stg_b], owner=stg_b)
                    fw.dma("sp", o_sv[t.idx][0:128 - TS, :], vr[prev][0][TS:128, :], reads=[vr[prev][1]], owner=vr[prev][1])
                    outs_b.extend([stg_b, vr[prev][1]])
                pt, pb = psum()
                for kh in range(4):
                    op("pe", lambda e, kh=kh: e.transpose(out=pt[0:64, kh * 128:(kh + 1) * 128], in_=stg_t[:, kh * 64:(kh + 1) * 64], identity=ident_t[:]),
                       reads=[stg_b, ident_b], writes=[pb])
                op("act", lambda e: e.copy(out=kTr[prev][0][:], in_=pt[0:64, :].rearrange("p (k n) -> p k n", k=4)),
                   reads=[pb], writes=[kTr[prev][1]])
            kT_c, kT_cb = kTr[cur]
            yield
            v_c, v_cb = vr[cur]
            if j == 0:
                rmsnorm(ht_t, ht_b, nt, gX_t[:, 0:1024], gX_b, xn_t, xn_b, sm[0:nt, 0:4], smb)
                to_featmajor(xn_t, xn_b, nt, D, xkT, xkT_b)
                wv, wb = load_w(a_w_kv, 8, 0, 512)
                pt, pb = psum()
                for kh in range(4):
                    for k in range(8):
                        op("pe", lambda e, kh=kh, k=k: e.matmul(pt[0:64, kh * nt:(kh + 1) * nt], lhsT=wv[:, k, kh * 64:(kh + 1) * 64], rhs=xkT[:, k, 0:nt],
                                                                 start=(k == 0), stop=(k == 7)), reads=[wb, xkT_b], writes=[pb])
                for kh in range(4):
                    op("act", lambda e, kh=kh: e.activation(out=kT_c[:, kh, 0:nt], in_=pt[0:64, kh * nt:(kh + 1) * nt], func=ACT.Identity,
                                                            bias=lc_t[0:64, 96 + kh:97 + kh]), reads=[pb, lc_b], writes=[kT_cb])
                pt2, pb2 = psum()
                for k in range(8):
                    op("pe", lambda e, k=k: e.matmul(pt2[0:nt, :], lhsT=xkT[:, k, 0:nt], rhs=wv[:, k, :], start=(k == 0), stop=(k == 7)),
                       reads=[wb, xkT_b], writes=[pb2])
                kvt_t, kvt_b = P[5]
                op("dve", lambda e: e.tensor_tensor(out=kvt_t[0:nt, 0:512], in0=pt2[0:nt, :], in1=gX_t[0:nt, 1024:1536], op=ALU.add),
                   reads=[pb2, gX_b], writes=[kvt_b])
                op("pool", lambda e: e.tensor_copy(out=v_c[0:nt, :], in_=kvt_t[0:nt, 256:512]), reads=[kvt_b], writes=[v_cb])
                fw.dma("sp", kT_d[t.kvslot][:, :, 0:nt], kT_c[:, :, 0:nt], reads=[kT_cb], writes=[t.kTd_b], owner=kT_cb)
                fw.dma("sp", v_d[t.kvslot][0:nt, :], v_c[0:nt, :], reads=[v_cb], writes=[t.vd_b], owner=v_cb)
                if t.kind == "p" and t.idx == NPT - 1:
                    fw.dma("sp", o_pk, kvt_t[0:128, 0:256], reads=[kvt_b], owner=kvt_b)
                    fw.dma("sp", o_pv, kvt_t[0:128, 256:512], reads=[kvt_b], owner=kvt_b)
                    outs_b.append(kvt_b)
                if t.kind == "s":
                    fw.dma("sp", o_sk[t.idx][128 - TS:128, :], kvt_t[0:TS, 0:256], reads=[kvt_b], owner=kvt_b)
                    fw.dma("sp", o_sv[t.idx][128 - TS:128, :], kvt_t[0:TS, 256:512], reads=[kvt_b], owner=kvt_b)
                    outs_b.append(kvt_b)
            else:
                fw.dma("sp", kT_c[:, :, 0:nt], kT_d[t.kvslot][:, :, 0:nt], reads=[t.kTd_b], writes=[kT_cb], owner=kT_cb)
                fw.dma("sp", v_c[0:nt, :], v_d[t.kvslot][0:nt, :], reads=[t.vd_b], writes=[v_cb], owner=v_cb)
            yield
            rmsnorm(ht_t, ht_b, nt, gmix_t, gmix_b, xn_t, xn_b, sm[0:nt, 0:4], smb)
            to_featmajor(xn_t, xn_b, nt, D, xnT_t, xnT_b)
            QT_t, QT_b = P[0]
            QT = QT_t[0:64, :].rearrange("p (h n) -> p h n", h=16)
            for blk in range(2):
                wv, wb = load_w(a_w_q[j], 8, blk * 512, 512)
                for h4 in range(2):
                    pt, pb = psum()
                    for hh in range(4):
                        for k in range(8):
                            op("pe", lambda e, hh=hh, k=k: e.matmul(pt[0:64, hh * nt:(hh + 1) * nt], lhsT=wv[:, k, (h4 * 4 + hh) * 64:(h4 * 4 + hh + 1) * 64],
                                                                     rhs=xnT_t[:, k, 0:nt], start=(k == 0), stop=(k == 7)), reads=[wb, xnT_b], writes=[pb])
                    for hh in range(4):
                        hq = blk * 8 + h4 * 4 + hh
                        op("act", lambda e, hh=hh, hq=hq: e.activation(out=QT[:, hq, 0:nt], in_=pt[0:64, hh * nt:(hh + 1) * nt], func=ACT.Identity,
                                                                        scale=0.125, bias=lc_t[0:64, 112 + hq:113 + hq]), reads=[pb, lc_b], writes=[QT_b])
                    yield
            oT_t, oT_b = P[1]
            oT = oT_t[0:64, :].rearrange("p (h n) -> p h n", h=16)
            NK = sum(n for _, n in blocks)
            if t.kind == "p":
                mask = amask_t[0:nt, 0:256] if t.idx > 0 else amask_t[0:nt, 256:384]
            else:
                mask = None
            S_par = [P[2], P[4]]
            PT_par = [P[3], P[5]]
            pst = {}

            def stageA(hq):
                kvh = hq // 4
                S_t, S_b = S_par[hq % 2]
                pt, pb = psum()
                off = 0
                for (slot, n) in blocks:
                    op("pe", lambda e, slot=slot, n=n, off=off: e.matmul(pt[0:nt, off:off + n], lhsT=QT[:, hq, 0:nt], rhs=kTr[slot][0][:, kvh, 0:n],
                                                                          start=True, stop=True), reads=[QT_b, kTr[slot][1]], writes=[pb])
                    off += n
                Ss = S_t[0:nt, 0:NK]
                st_ = sm[0:nt, 8 + 8 * (hq % 2):16 + 8 * (hq % 2)]
                if mask is not None:
                    op("dve", lambda e: e.tensor_tensor(out=Ss, in0=pt[0:nt, 0:NK], in1=mask, op=ALU.add), reads=[pb, amask_b], writes=[S_b])
                else:
                    op("dve", lambda e: e.tensor_copy(out=Ss, in_=pt[0:nt, 0:NK]), reads=[pb], writes=[S_b])
                op("dve", lambda e: e.reduce_max(out=st_[:, 0:1], in_=Ss, axis=AX.X), reads=[S_b], writes=[smb])
                op("dve", lambda e: e.tensor_scalar_mul(out=st_[:, 1:2], in0=st_[:, 0:1], scalar1=-1.0), reads=[smb], writes=[smb])
                op("dve", lambda e: e.memset(st_[:, 2:3], 0.0), writes=[smb])
                op("act", lambda e: e.activation(out=Ss, in_=Ss, func=ACT.Exp, bias=st_[:, 1:2], accum_out=st_[:, 2:3]),
                   reads=[S_b, smb], writes=[S_b, smb])
                op("act", lambda e: e.activation(out=st_[:, 3:4], in_=lc_t[0:nt, 288 + hq:289 + hq], func=ACT.Exp, bias=st_[:, 1:2]),
                   reads=[lc_b, smb], writes=[smb])
                op("dve", lambda e: e.tensor_tensor(out=st_[:, 4:5], in0=st_[:, 2:3], in1=st_[:, 3:4], op=ALU.add), reads=[smb], writes=[smb])
                op("dve", lambda e: e.reciprocal(out=st_[:, 5:6], in_=st_[:, 4:5]), reads=[smb], writes=[smb])
                op("dve", lambda e: e.tensor_scalar_mul(out=Ss, in0=Ss, scalar1=st_[:, 5:6]), reads=[S_b, smb], writes=[S_b])

            def stageB(hq):
                kvh = hq // 4
                S_t, S_b = S_par[hq % 2]
                PT_t, PT_b = PT_par[hq % 2]
                pt2, pb2 = psum()
                off = 0
                for bi, (slot, n) in enumerate(blocks):
                    op("pe", lambda e, n=n, off=off, bi=bi: e.transpose(out=pt2[0:n, bi * 128:bi * 128 + nt],
                                                                        in_=S_t[0:nt, off:off + n],
                                                                        identity=ident_t[0:nt, 0:nt]), reads=[S_b, ident_b], writes=[pb2])
                    off += n
                PTs = PT_t[:, 0:256]
                for bi, (slot, n) in enumerate(blocks):
                    op("act", lambda e, n=n, bi=bi: e.copy(out=PTs[0:n, bi * 128:bi * 128 + nt], in_=pt2[0:n, bi * 128:bi * 128 + nt]),
                       reads=[pb2], writes=[PT_b])
                if hq % 4 == 0:
                    pst["po"], pst["pob"] = psum()
                po, pob = pst["po"], pst["pob"]
                for bi, (slot, n) in enumerate(blocks):
                    op("pe", lambda e, n=n, bi=bi, slot=slot: e.matmul(po[0:64, (hq % 4) * nt:(hq % 4 + 1) * nt], lhsT=vr[slot][0][0:n, kvh * 64:(kvh + 1) * 64],
                                                                        rhs=PTs[0:n, bi * 128:bi * 128 + nt], start=(bi == 0), stop=(bi == len(blocks) - 1)),
                       reads=[vr[slot][1], PT_b], writes=[pob])
                if hq % 4 == 3:
                    op("act", lambda e: e.copy(out=oT[:, hq - 3:hq + 1, 0:nt], in_=po[0:64, 0:4 * nt].rearrange("p (h n) -> p h n", h=4)),
                       reads=[pob], writes=[oT_b])

            stageA(0)
            yield
            for hq in range(16):
                if hq + 1 < 16:
                    stageA(hq + 1)
                    yield
                stageB(hq)
                yield
            op("dve", lambda e: e.tensor_tensor(out=ht_t[0:nt, :], in0=ht_t[0:nt, :], in1=bo_t[0:nt, :], op=ALU.add),
               reads=[ht_b, bo_b], writes=[ht_b])
            wo = a_w_o[j].rearrange("(h d) n -> d h n", d=64)
            for blk in range(4):
                wt, wb = wslot()
                wv = wt[0:64, 0:16 * 256].rearrange("p (h n) -> p h n", h=16)
                fw.dma("sp", wv, wo[:, :, blk * 256:(blk + 1) * 256], writes=[wb], owner=wb)
                pt, pb = psum()
                for hq in range(16):
                    op("pe", lambda e, hq=hq: e.matmul(pt[0:nt, 0:256], lhsT=oT[:, hq, 0:nt], rhs=wv[:, hq, :], start=(hq == 0), stop=(hq == 15)),
                       reads=[wb, oT_b], writes=[pb])
                op("dve", lambda e: e.tensor_tensor(out=ht_t[0:nt, blk * 256:(blk + 1) * 256], in0=ht_t[0:nt, blk * 256:(blk + 1) * 256],
                                                    in1=pt[0:nt, 0:256], op=ALU.add), reads=[ht_b, pb], writes=[ht_b])
                yield
            fw.dma("sp", h_ap(ci + 1, t), ht_t[0:nt, :], reads=[ht_b], writes=[t.hb[ci + 1]], owner=ht_b)

        def peer_a(l, t, ci):
            nt = t.nt
            fw.dma("sp", htp_t[0:nt, :], h_ap(ci, t), reads=hbl(t, ci), writes=[htp_b], owner=htp_b)
            rmsnorm(htp_t, htp_b, nt, gffn_t, gffn_b, xnp_t, xnp_b, sm[0:nt, 0:4], smb)
            to_featmajor(xnp_t, xnp_b, nt, D, xnT_t, xnT_b)
            fw.mark(17)
            qT_t, qT_b = P[0]
            qT = qT_t[:].rearrange("p (c n) -> p c n", c=16)
            S_t, S_b = P[1]
            Sw_t, Sw_b = P[2]
            S3 = S_t[0:nt, :].rearrange("p (c n) -> p c n", c=16)
            Sw3 = Sw_t[0:nt, :].rearrange("p (c n) -> p c n", c=16)
            for blk in range(4):
                wv, wb = load_w(p_w_q[l], 8, blk * 512, 512)
                pt, pb = psum()
                for cc in range(4):
                    for k in range(8):
                        op("pe", lambda e, cc=cc, k=k: e.matmul(pt[:, cc * nt:(cc + 1) * nt], lhsT=wv[:, k, cc * 128:(cc + 1) * 128], rhs=xnT_t[:, k, 0:nt],
                                                                 start=(k == 0), stop=(k == 7)), reads=[wb, xnT_b], writes=[pb])
                op("act", lambda e: e.copy(out=qT[:, blk * 4:blk * 4 + 4, 0:nt], in_=pt[:, 0:4 * nt].rearrange("p (c n) -> p c n", c=4)),
                   reads=[pb], writes=[qT_b])
            fw.mark(18)
            for blk in range(4):
                pt, pb = psum()
                for cc in range(4):
                    c = blk * 4 + cc
                    op("pe", lambda e, cc=cc, c=c: e.matmul(pt[0:nt, cc * 128:(cc + 1) * 128], lhsT=qT[:, c, 0:nt], rhs=sk_t[:, c % 2, :], start=True, stop=True),
                       reads=[qT_b, sk_b], writes=[pb])
                if os.environ.get('K_SKIPC') != '1':
                    op("act", lambda e: e.copy(out=S_t[0:nt, blk * 512:(blk + 1) * 512], in_=pt[0:nt, :]), reads=[pb], writes=[S_b])
                if os.environ.get('K_SKIPC') not in ('1', '2'):
                    op("dve", lambda e: e.tensor_copy(out=Sw_t[0:nt, blk * 512:(blk + 1) * 512], in_=S_t[0:nt, blk * 512:(blk + 1) * 512]), reads=[S_b], writes=[Sw_b])
            fw.mark(20)
            R = [smb]
            vals = sm[0:nt, 256:512].rearrange("p (c k) -> p c k", c=16)
            idxf = sm[0:nt, 512:768].rearrange("p (c k) -> p c k", c=16)
            vs = sm[0:nt, 768:896].rearrange("p (h k) -> p h k", h=8)
            posf = sm[0:nt, 896:1024].rearrange("p (h k) -> p h k", h=8)
            idxu = idx_t[0:nt, 0:256].rearrange("p (c k) -> p c k", c=16)
            posu = idx_t[0:nt, 256:384].rearrange("p (h k) -> p h k", h=8)
            for c in range(16):
                op("dve", lambda e, c=c: e.max(out=vals[:, c, 0:8], in_=Sw3[:, c, :]), reads=[Sw_b], writes=R)
                op("dve", lambda e, c=c: e.match_replace(out=Sw3[:, c, :], in_to_replace=vals[:, c, 0:8], in_values=Sw3[:, c, :], imm_value=-1e30),
                   reads=[Sw_b] + R, writes=[Sw_b])
                op("dve", lambda e, c=c: e.max(out=vals[:, c, 8:16], in_=Sw3[:, c, :]), reads=[Sw_b], writes=R)
                op("dve", lambda e, c=c: e.max_index(out=idxu[:, c, 0:8], in_max=vals[:, c, 0:8], in_values=S3[:, c, :]), reads=[S_b] + R, writes=[idx_b])
                op("dve", lambda e, c=c: e.max_index(out=idxu[:, c, 8:16], in_max=vals[:, c, 8:16], in_values=S3[:, c, :]), reads=[S_b] + R, writes=[idx_b])
            op("dve", lambda e: e.tensor_copy(out=idxf, in_=idxu), reads=[idx_b], writes=R)
            vals4 = sm[0:nt, 256:512].rearrange("p (h s k) -> p h s k", h=8, s=2)
            idxf4 = sm[0:nt, 512:768].rearrange("p (h s k) -> p h s k", h=8, s=2)
            cand_t, cand_b = P[3]
            cw_t, cw_b = P[4]
            T1_t, T1_b = P[5]
            cand = cand_t[0:nt, :].rearrange("p (h a b) -> p h a b", h=8, a=16)
            cand3 = cand_t[0:nt, :].rearrange("p (h n) -> p h n", h=8)
            cw3 = cw_t[0:nt, :].rearrange("p (h n) -> p h n", h=8)
            for h in range(8):
                op("dve", lambda e, h=h: e.tensor_tensor(out=cand[:, h], in0=bc(vals4[:, h, 0, :].unsqueeze(2), [nt, 16, 16]),
                                                         in1=bc(vals4[:, h, 1, :].unsqueeze(1), [nt, 16, 16]), op=ALU.add), reads=R, writes=[cand_b])
            op("dve", lambda e: e.tensor_copy(out=cw_t[0:nt, :], in_=cand_t[0:nt, :]), reads=[cand_b], writes=[cw_b])
            for h in range(8):
                op("dve", lambda e, h=h: e.max(out=vs[:, h, 0:8], in_=cw3[:, h, :]), reads=[cw_b], writes=R)
                op("dve", lambda e, h=h: e.match_replace(out=cw3[:, h, :], in_to_replace=vs[:, h, 0:8], in_values=cw3[:, h, :], imm_value=-1e30),
                   reads=[cw_b] + R, writes=[cw_b])
                op("dve", lambda e, h=h: e.max(out=vs[:, h, 8:16], in_=cw3[:, h, :]), reads=[cw_b], writes=R)
                op("dve", lambda e, h=h: e.max_index(out=posu[:, h, 0:8], in_max=vs[:, h, 0:8], in_values=cand3[:, h, :]), reads=[cand_b] + R, writes=[idx_b])
                op("dve", lambda e, h=h: e.max_index(out=posu[:, h, 8:16], in_max=vs[:, h, 8:16], in_values=cand3[:, h, :]), reads=[cand_b] + R, writes=[idx_b])
            op("dve", lambda e: e.tensor_copy(out=posf, in_=posu), reads=[idx_b], writes=R)
            fw.mark(21)
            T15 = cw_t[0:nt, 0:128 * 15].rearrange("p (s m) -> p s m", m=15)
            posf2 = sm[0:nt, 896:1024]
            af = psm_t[0:nt, 0:128]
            bf = psm_t[0:nt, 128:256]
            e1 = psm_t[0:nt, 256:384]
            e2 = psm_t[0:nt, 384:512]
            gate = psm_t[0:nt, 512:640]
            actv = psm_t[0:nt, 640:768]
            wgt = psm_t[0:nt, 768:896]
            tmp1 = psm_t[0:nt, 896:1024]
            tmp2 = psm_t[0:nt, 1024:1152]
            eidf = psm_t[0:nt, 1152:1280]
            Rb = [psm_b]
            op("dve", lambda e: e.tensor_tensor(out=T15, in0=bc(posf2.unsqueeze(2), [nt, 128, 15]), in1=bc(thr_t[0:nt, :].unsqueeze(1), [nt, 128, 15]), op=ALU.is_ge),
               reads=R + [thr_b], writes=[cw_b])
            op("dve", lambda e: e.reduce_sum(out=af, in_=T15, axis=AX.X), reads=[cw_b], writes=Rb)
            op("dve", lambda e: e.scalar_tensor_tensor(out=bf, in0=af, scalar=-16.0, in1=posf2, op0=ALU.mult, op1=ALU.add), reads=Rb + R, writes=Rb)
            T1 = T1_t[0:nt, :].rearrange("p (h k a) -> p h k a", h=8, k=16)
            for (src, which, dst) in ((af, 0, e1), (bf, 1, e2)):
                src3 = src.rearrange("p (h k) -> p h k", h=8)
                dst3 = dst.rearrange("p (h k) -> p h k", h=8)
                for h in range(8):
                    op("dve", lambda e, h=h, src3=src3: e.tensor_tensor(out=T1[:, h], in0=bc(src3[:, h, :].unsqueeze(2), [nt, 16, 16]),
                                                                        in1=bc(iota_t[0:nt, :].unsqueeze(1), [nt, 16, 16]), op=ALU.is_equal),
                       reads=Rb + [iota_b], writes=[T1_b])
                    op("dve", lambda e, h=h, which=which: e.tensor_tensor(out=T1[:, h], in0=T1[:, h], in1=bc(idxf4[:, h, which, :].unsqueeze(1), [nt, 16, 16]), op=ALU.mult),
                       reads=[T1_b] + R, writes=[T1_b])
                op("dve", lambda e, dst=dst: e.reduce_sum(out=dst, in_=T1_t[0:nt, :].rearrange("p (s a) -> p s a", a=16), axis=AX.X), reads=[T1_b], writes=Rb)
            op("dve", lambda e: e.scalar_tensor_tensor(out=eidf, in0=e1, scalar=128.0, in1=e2, op0=ALU.mult, op1=ALU.add), reads=Rb, writes=Rb)
            op("dve", lambda e: e.tensor_scalar_add(out=eidf, in0=eidf, scalar1=float(l * 16384)), reads=Rb, writes=Rb)
            op("dve", lambda e: e.tensor_copy(out=eid_t[0:nt, :], in_=eidf), reads=Rb, writes=[eid_b])
            fw.mark(22)
            gate3 = gate.rearrange("p (h k) -> p h k", h=8)
            op("dve", lambda e: e.tensor_tensor(out=gate3, in0=vs, in1=bc(vs[:, :, 0:1], [nt, 8, 16]), op=ALU.subtract), reads=R, writes=Rb)
            op("act", lambda e: e.activation(out=gate, in_=gate, func=ACT.Exp), reads=Rb, writes=Rb)
            op("dve", lambda e: e.reduce_sum(out=tmp1[:, 0:8], in_=gate3, axis=AX.X), reads=Rb, writes=Rb)
            op("dve", lambda e: e.reciprocal(out=tmp1[:, 8:16], in_=tmp1[:, 0:8]), reads=Rb, writes=Rb)
            op("dve", lambda e: e.tensor_tensor(out=gate3, in0=gate3, in1=bc(tmp1[:, 8:16].unsqueeze(2), [nt, 8, 16]), op=ALU.mult), reads=Rb, writes=Rb)
            fw.mark(23)
        def peer_b(l, t, ci):
            nt = t.nt
            Rb = [psm_b]
            gate = psm_t[0:nt, 512:640]
            actv = psm_t[0:nt, 640:768]
            wgt = psm_t[0:nt, 768:896]
            tmp1 = psm_t[0:nt, 896:1024]
            g_i = [0]

            def gather(tab, jj):
                gt, gb = G[g_i[0] % NG]
                g_i[0] += 1
                fw.dma("pool", None, None, reads=[eid_b], writes=[gb], owner=gb,
                       fn=lambda e: e.indirect_dma_start(out=gt[0:nt, :], out_offset=None, in_=tab,
                                                         in_offset=bass.IndirectOffsetOnAxis(ap=eid_t[0:nt, jj:jj + 1], axis=0)))
                return gt, gb
            op("dve", lambda e: e.memset(actv, 0.0), writes=Rb)
            for jj in range(128):
                gt, gb = gather(p_u, jj)
                op("dve", lambda e, gt=gt, jj=jj: e.scalar_tensor_tensor(out=gt[0:nt, :], in0=gt[0:nt, :], scalar=1.0, in1=xnp_t[0:nt, :],
                                                                           op0=ALU.mult, op1=ALU.mult, accum_out=actv[:, jj:jj + 1]),
                   reads=[gb, xnp_b], writes=[gb] + Rb)
                yield
            fw.mark(24)
            op("dve", lambda e: e.tensor_tensor(out=tmp1, in0=actv, in1=actv, op=ALU.mult), reads=Rb, writes=Rb)
            op("dve", lambda e: e.tensor_scalar(out=tmp1, in0=tmp1, scalar1=0.044715, scalar2=1.0, op0=ALU.mult, op1=ALU.add), reads=Rb, writes=Rb)
            op("dve", lambda e: e.tensor_tensor(out=tmp1, in0=tmp1, in1=actv, op=ALU.mult), reads=Rb, writes=Rb)
            op("act", lambda e: e.activation(out=tmp1, in_=tmp1, func=ACT.Tanh, scale=0.7978845608028654), reads=Rb, writes=Rb)
            op("dve", lambda e: e.tensor_scalar(out=tmp1, in0=tmp1, scalar1=1.0, scalar2=0.5, op0=ALU.add, op1=ALU.mult), reads=Rb, writes=Rb)
            op("dve", lambda e: e.tensor_tensor(out=tmp1, in0=tmp1, in1=actv, op=ALU.mult), reads=Rb, writes=Rb)
            op("dve", lambda e: e.tensor_tensor(out=wgt, in0=tmp1, in1=gate, op=ALU.mult), reads=Rb, writes=Rb)
            for jj in range(128):
                gt, gb = gather(p_v, jj)
                op("dve", lambda e, gt=gt, jj=jj: e.scalar_tensor_tensor(out=htp_t[0:nt, :], in0=gt[0:nt, :], scalar=wgt[:, jj:jj + 1], in1=htp_t[0:nt, :],
                                                                           op0=ALU.mult, op1=ALU.add), reads=[gb, htp_b] + Rb, writes=[htp_b])
                yield
            fw.dma("sp", h_ap(ci + 1, t), htp_t[0:nt, :], reads=[htp_b], writes=hbl(t, ci + 1), owner=htp_b)
            if l == 3:
                rmsnorm(htp_t, htp_b, nt, gX_t[:, 0:1024], gX_b, xnp_t, xnp_b, psm_t[0:nt, 1280:1284], psm_b)
                dst = y_p[t.row0:t.row0 + nt, :] if t.kind == "p" else (y_s[0:nt, :] if t.kind == "sp" else y_s[t.idx * TS:(t.idx + 1) * TS, :])
                fw.dma("sp", dst, xnp_t[0:nt, :], reads=[xnp_b], owner=xnp_b)
                outs_b.append(xnp_b)

        outs_b = []
        pending = [None]
        RATIO = {(True, 'p'): float(os.environ.get('K_RM', '1.9')), (True, 's'): float(os.environ.get('K_RMS', '3.8')),
                 (False, 'p'): float(os.environ.get('K_RA', '6.1')), (False, 's'): float(os.environ.get('K_RAS', '6.1'))}

        def drain2(g1, g2, r=1.0):
            n1 = n2 = 0
            acc = 0.0
            while g1 is not None or g2 is not None:
                if g1 is not None:
                    try:
                        next(g1)
                        n1 += 1
                    except StopIteration:
                        g1 = None
                    acc += r
                else:
                    acc += 1.0
                while g2 is not None and acc >= 1.0:
                    acc -= 1.0
                    try:
                        next(g2)
                        n2 += 1
                    except StopIteration:
                        g2 = None
                if g2 is None:
                    acc = 0.0
            if os.environ.get('K_CNT'):
                print("drain2 yields mixer=%d peer=%d" % (n1, n2))

        for l in range(nlayers):
            load_bc(gmix_t[:], gmix_b, norm_mix[l])
            load_bc(gffn_t[:], gffn_b, norm_ffn[l])
            fw.dma("sp", sk_t[:, 0, :], p_k1T[l], writes=[sk_b], owner=sk_b)
            fw.dma("sp", sk_t[:, 1, :], p_k2T[l], writes=[sk_b], owner=sk_b)
            if l < 2:
                load_bc(gX_t[:], gX_b, m_norm[l])
                load_bc(lc_t[:, 0:32], lc_b, m_dt_bias[l])
                load_bc(lc_t[:, 32:64], lc_b, m_a_log[l])
                load_bc(lc_t[:, 64:96], lc_b, m_d_skip[l])
                fw.dma("sp", lc_t[:, 128:256], m_conv_w[l].rearrange("p c k -> p (c k)"), writes=[lc_b], owner=lc_b)
                fw.dma("sp", lc_t[:, 256:288], m_conv_b[l], writes=[lc_b], owner=lc_b)
                op("act", lambda e: e.activation(out=lc_t[:, 32:64], in_=lc_t[:, 32:64], func=ACT.Exp), reads=[lc_b], writes=[lc_b])
                op("dve", lambda e: e.tensor_scalar_mul(out=lc_t[:, 32:64], in0=lc_t[:, 32:64], scalar1=-1.0), reads=[lc_b], writes=[lc_b])
            else:
                j = l - 2
                if j == 0:
                    load_bc(gX_t[:, 0:1024], gX_b, norm_kv[0])
                    load_bc(gX_t[:, 1024:1536], gX_b, a_b_kv[0])
                    fw.dma("sp", lc_t[0:64, 96:100], a_b_kT, writes=[lc_b], owner=lc_b)
                load_bc(bo_t[:], bo_b, a_b_o[j])
                fw.dma("sp", lc_t[0:64, 112:128], a_b_qT[j], writes=[lc_b], owner=lc_b)
                op("dve", lambda e: e.tensor_scalar_mul(out=lc_t[0:64, 112:128], in0=lc_t[0:64, 112:128], scalar1=0.125), reads=[lc_b], writes=[lc_b])
                load_bc(lc_t[:, 288:304], lc_b, a_sinks[j])
            if l == 3:
                load_bc(gX_t[:, 0:1024], gX_b, norm_final[0])
            try:
                for t in tiles:
                    ci = 2 * l
                    mix = mamba_tile(l, t, ci) if l < 2 else attn_tile(l, t, ci)
                    drain2(mix, pending[0], RATIO[(l < 2, t.kind)])
                    pending[0] = None
                    pt_ = t
                    if t.kind == "s":
                        if t is not stiles[-1]:
                            continue
                        pt_ = speer
                    if not os.environ.get('K_NOPEER'):
                        peer_a(l, pt_, ci + 1)
                        pending[0] = peer_b(l, pt_, ci + 1)
                        if os.environ.get('K_NOILV') or l >= ILV_MAXL:
                            drain2(pending[0], None)
                            pending[0] = None
            except Stop:
                break
        if pending[0] is not None:
            drain2(pending[0], None)
        fw.finish(fw.allbufs, "sp")
        fw.finish(fw.allbufs, "act")
        print("program: ninst=%d nwait=%d sems=%d" % (fw.ninst, fw.nwait, len(fw.sems)))
    return nc


_CACHE = {}


def make_in_maps(inp):
    f = lambda a: np.ascontiguousarray(np.asarray(a, dtype=np.float32))
    shared = {
        "norm_mix": f(inp["norm_mix"]), "norm_ffn": f(inp["norm_ffn"]),
        "norm_kv": f(inp["norm_kv"]).reshape(1, D), "norm_final": f(inp["norm_final"]).reshape(1, D),
        "m_w_in": f(inp["m_w_in"]),
        "m_conv_w": f(np.asarray(inp["m_conv_w"]).reshape(2, 4, 32, 128).transpose(0, 3, 2, 1)),
        "m_conv_b": f(np.asarray(inp["m_conv_b"]).reshape(2, 32, 128).transpose(0, 2, 1)),
        "m_dt_bias": f(inp["m_dt_bias"]), "m_a_log": f(inp["m_a_log"]), "m_d_skip": f(inp["m_d_skip"]),
        "m_norm": f(inp["m_norm"]), "m_w_out": f(inp["m_w_out"]),
        "a_w_kv": f(inp["a_w_kv"]), "a_b_kv": f(inp["a_b_kv"]).reshape(1, 512),
        "a_b_kT": f(np.asarray(inp["a_b_kv"])[:256].reshape(4, 64).T),
        "a_w_q": f(inp["a_w_q"]),
        "a_b_qT": f(np.asarray(inp["a_b_q"]).reshape(2, 16, 64).transpose(0, 2, 1)),
        "a_sinks": f(inp["a_sinks"]), "a_w_o": f(inp["a_w_o"]), "a_b_o": f(inp["a_b_o"]),
        "p_w_q": f(inp["p_w_q"]),
        "p_k1T": f(np.asarray(inp["p_sub_k1"]).transpose(0, 2, 1)),
        "p_k2T": f(np.asarray(inp["p_sub_k2"]).transpose(0, 2, 1)),
        "p_u": f(inp["p_u"]).reshape(4 * 16384, D), "p_v": f(inp["p_v"]).reshape(4 * 16384, D),
    }
    xp = np.asarray(inp["x_prompt"], dtype=np.float32)
    xs = np.asarray(inp["x_sample"], dtype=np.float32)
    ssm = np.asarray(inp["state_ssm"], dtype=np.float32)
    conv = np.asarray(inp["state_conv"], dtype=np.float32)
    ckw = np.asarray(inp["cache_k_win"], dtype=np.float32)
    cvw = np.asarray(inp["cache_v_win"], dtype=np.float32)
    maps = []
    for c in range(8):
        m = dict(shared)
        m["x_p"] = f(xp[c])
        m["x_s"] = f(xs[4 * c:4 * c + 4].reshape(NSS * TS, D))
        m["st_ssm"] = f(ssm[:, 4 * c:4 * c + 4].reshape(2, NSS, 2048, 128))
        m["st_conv"] = f(conv[:, 4 * c:4 * c + 4])
        m["ck"] = f(ckw[4 * c:4 * c + 4].reshape(NSS, 128, 256))
        m["cv"] = f(cvw[4 * c:4 * c + 4].reshape(NSS, 128, 256))
        maps.append(m)
    return maps


def assemble(results):
    r = results
    y_prompt = np.stack([r[c]["y_p"] for c in range(8)], 0)
    y_sample = np.concatenate([r[c]["y_s"].reshape(NSS, TS, D) for c in range(8)], 0)
    pr_ssm = np.stack([r[c]["o_pssm"].reshape(2, 32, 64, 128) for c in range(8)], 1)
    pr_conv = np.stack([r[c]["o_pconv"] for c in range(8)], 1)
    pr_k = np.stack([r[c]["o_pk"].reshape(128, 4, 64) for c in range(8)], 0)
    pr_v = np.stack([r[c]["o_pv"].reshape(128, 4, 64) for c in range(8)], 0)
    sm_ssm = np.concatenate([r[c]["o_sssm"].reshape(2, NSS, 32, 64, 128) for c in range(8)], 1)
    sm_conv = np.concatenate([r[c]["o_sconv"] for c in range(8)], 1)
    sm_k = np.concatenate([r[c]["o_sk"].reshape(NSS, 128, 4, 64) for c in range(8)], 0)
    sm_v = np.concatenate([r[c]["o_sv"].reshape(NSS, 128, 4, 64) for c in range(8)], 0)
    outs = (y_prompt, y_sample, pr_ssm, pr_conv, pr_k, pr_v, sm_ssm, sm_conv, sm_k, sm_v)
    return tuple(np.ascontiguousarray(o, dtype=np.float32) for o in outs)


def kernel(**inputs):
    if "nc" not in _CACHE:
        _CACHE["nc"] = build_program()
    nc = _CACHE["nc"]
    maps = make_in_maps(inputs)
    res = run_bass_kernel_spmd(nc, maps, core_ids=list(range(8)))
    return assemble(res.results)
```

```python
import os
import numpy as np
from contextlib import ExitStack
import concourse.bass as bass
import concourse.mybir as mybir
from concourse.bass_utils import run_bass_kernel_spmd

F32 = mybir.dt.float32
U32 = mybir.dt.uint32
ALU = mybir.AluOpType
ACT = mybir.ActivationFunctionType
AX = mybir.AxisListType

D = 1024
SEQ = 2048
NPT = 16
NSS = 4
TS = 16
NEG = -30000.0
ILV_MAXL = int(os.environ.get('K_ILVL', '4'))


class Buf:
    __slots__ = ("name", "wr", "rd", "dsem", "dcnt")

    def __init__(self, name):
        self.name = name
        self.wr = []
        self.rd = []
        self.dsem = None
        self.dcnt = 0


class FW:
    def __init__(self, nc, stack):
        self.nc = nc
        self.stack = stack
        self.engs = {"pe": nc.tensor, "dve": nc.vector, "act": nc.scalar,
                     "pool": nc.gpsimd, "sp": nc.sync}
        self.sems = {}
        self.cnt = {}
        self.seen = {k: {} for k in self.engs}
        for k in self.engs:
            self.sems[k] = stack.enter_context(nc.semaphore("s_" + k))
            self.cnt[k] = 0
        self.nbuf = 0
        self.ninst = 0
        self.nwait = 0
        self.allbufs = []
        self.stage = 0
        self.stop_at = int(os.environ.get('K_STAGE', '0'))

    def buf(self, name=None):
        self.nbuf += 1
        b = Buf(name or "b%d" % self.nbuf)
        self.allbufs.append(b)
        return b

    def sb(self, name, shape, dt=F32):
        t = self.stack.enter_context(self.nc.sbuf_tensor(name, list(shape), dt))
        return t, self.buf(name)

    def _dsem(self, b):
        if b.dsem is None:
            key = "ds%d" % len(self.sems)
            b.dsem = key
            self.sems[key] = self.stack.enter_context(self.nc.semaphore(key))
        return b.dsem

    def _wait(self, e, events):
        need = {}
        for (k, v) in events:
            if need.get(k, 0) < v:
                need[k] = v
        seen = self.seen[e]
        for k, v in need.items():
            if e == "pe" and k == "pe":
                continue
            if seen.get(k, 0) >= v:
                continue
            self.engs[e].wait_ge(self.sems[k], v)
            self.nwait += 1
            seen[k] = v

    def _deps(self, e, reads, writes):
        ev = []
        for b in reads:
            ev.extend(b.wr)
        for b in writes:
            ev.extend(b.wr)
            ev.extend(b.rd)
        self._wait(e, ev)

    def _commit(self, event, reads, writes):
        for b in reads:
            b.rd.append(event)
            if len(b.rd) > 48:
                m = {}
                for (k, v) in b.rd:
                    if m.get(k, 0) < v:
                        m[k] = v
                b.rd = list(m.items())
        for b in writes:
            b.wr = [event]
            b.rd = []

    def op(self, e, fn, reads=(), writes=()):
        self._deps(e, reads, writes)
        ins = fn(self.engs[e])
        self.cnt[e] += 1
        ins.then_inc(self.sems[e], 1)
        self._commit((e, self.cnt[e]), reads, writes)
        self.ninst += 1
        return ins

    def dma(self, q, out, in_, reads=(), writes=(), owner=None, fn=None, slow=False):
        self._deps(q, reads, writes)
        key = self._dsem(owner)
        if fn is not None:
            ins = fn(self.engs[q])
        elif slow:
            ins = self.engs[q].dma_start(out=out, in_=in_, allow_slow_non_contiguous=True)
        else:
            ins = self.engs[q].dma_start(out=out, in_=in_)
        owner.dcnt += 16
        ins.then_inc(self.sems[key], 16)
        self._commit((key, owner.dcnt), reads, writes)
        self.ninst += 1
        return ins

    def mark(self, n):
        if self.stop_at and n >= self.stop_at:
            raise Stop()

    def finish(self, bufs, e="sp"):
        ev = []
        for b in bufs:
            ev.extend(b.wr)
            ev.extend(b.rd)
        self._wait(e, ev)


class Stop(Exception):
    pass


def bc(ap, shape):
    return ap.to_broadcast(list(shape))


def build_program(debug=False, nlayers=4, ntiles_p=NPT, nseq_s=NSS):
    nc = bass.Bass("TRN2", target_bir_lowering=False)

    def din(name, shape, dt=F32):
        return nc.dram_tensor(name, list(shape), dt, kind="ExternalInput").ap()

    def dout(name, shape, dt=F32):
        return nc.dram_tensor(name, list(shape), dt, kind="ExternalOutput").ap()

    x_p = din("x_p", [SEQ, D])
    x_s = din("x_s", [NSS * TS, D])
    st_ssm = din("st_ssm", [2, NSS, 2048, 128])
    st_conv = din("st_conv", [2, NSS, 3, 4096])
    ck = din("ck", [NSS, 128, 256])
    cv = din("cv", [NSS, 128, 256])
    norm_mix = din("norm_mix", [4, D])
    norm_ffn = din("norm_ffn", [4, D])
    norm_kv = din("norm_kv", [1, D])
    norm_final = din("norm_final", [1, D])
    m_w_in = din("m_w_in", [2, D, 6176])
    m_conv_w = din("m_conv_w", [2, 128, 32, 4])
    m_conv_b = din("m_conv_b", [2, 128, 32])
    m_dt_bias = din("m_dt_bias", [2, 32])
    m_a_log = din("m_a_log", [2, 32])
    m_d_skip = din("m_d_skip", [2, 32])
    m_norm = din("m_norm", [2, 2048])
    m_w_out = din("m_w_out", [2, 2048, D])
    a_w_kv = din("a_w_kv", [D, 512])
    a_b_kv = din("a_b_kv", [1, 512])
    a_b_kT = din("a_b_kT", [64, 4])
    a_w_q = din("a_w_q", [2, D, D])
    a_b_qT = din("a_b_qT", [2, 64, 16])
    a_sinks = din("a_sinks", [2, 16])
    a_w_o = din("a_w_o", [2, D, D])
    a_b_o = din("a_b_o", [2, D])
    p_w_q = din("p_w_q", [4, D, 2048])
    p_k1T = din("p_k1T", [4, 128, 128])
    p_k2T = din("p_k2T", [4, 128, 128])
    p_u = din("p_u", [4 * 16384, D])
    p_v = din("p_v", [4 * 16384, D])

    y_p = dout("y_p", [SEQ, D])
    y_s = dout("y_s", [NSS * TS, D])
    o_pssm = dout("o_pssm", [2, 2048, 128])
    o_pconv = dout("o_pconv", [2, 3, 4096])
    o_pk = dout("o_pk", [128, 256])
    o_pv = dout("o_pv", [128, 256])
    o_sssm = dout("o_sssm", [2, NSS, 2048, 128])
    o_sconv = dout("o_sconv", [2, NSS, 3, 4096])
    o_sk = dout("o_sk", [NSS, 128, 256])
    o_sv = dout("o_sv", [NSS, 128, 256])

    NTOK = SEQ + NSS * TS
    hck = []
    for i in range(2 * 4):
        if debug:
            hck.append(dout("hck%d" % i, [NTOK, D]))
        else:
            hck.append(nc.dram_tensor("hck%d" % i, [NTOK, D], F32).ap())
    kT_d = nc.dram_tensor("kT_d", [NPT + NSS, 64, 4, 128], F32).ap()
    v_d = nc.dram_tensor("v_d", [NPT + NSS, 128, 256], F32).ap()

    with ExitStack() as st:
        fw = FW(nc, st)
        op = fw.op

        class Tile:
            pass
        tiles = []
        for i in range(ntiles_p):
            t = Tile()
            t.kind, t.idx, t.nt, t.row0 = "p", i, 128, i * 128
            t.chunks = [(0, 64), (64, 64)]
            tiles.append(t)
        for s in range(nseq_s):
            t = Tile()
            t.kind, t.idx, t.nt, t.row0 = "s", s, TS, SEQ + s * TS
            t.chunks = [(0, TS)]
            tiles.append(t)
        for t in tiles:
            t.hb = [fw.buf("h%d_%s%d" % (i, t.kind, t.idx)) for i in range(9)]
            t.kvslot = NPT + t.idx if t.kind == "s" else t.idx
            t.kTd_b = fw.buf()
            t.vd_b = fw.buf()

        speer = Tile()
        speer.kind, speer.idx, speer.nt, speer.row0 = "sp", 0, nseq_s * TS, SEQ
        stiles = [t for t in tiles if t.kind == "s"]

        def hbl(t, ci):
            if t.kind == "sp":
                return [s_.hb[ci] for s_ in stiles]
            return [t.hb[ci]]

        def h_ap(ci, t):
            if ci == 0:
                return x_p[t.row0:t.row0 + t.nt, :] if t.kind == "p" else x_s[t.idx * TS:(t.idx + 1) * TS, :]
            return hck[ci - 1][t.row0:t.row0 + t.nt, :]

        P = []
        for i in range(8):
            P.append(fw.sb("P%d" % i, [128, 2048]))
        xbc_t, xbc_b = fw.sb("xbc", [128, 32, 131])
        halo_t, halo_b = fw.sb("halo", [128, 32, 3])
        xnT_t, xnT_b = fw.sb("xnT", [128, 8, 128])
        ht_t, ht_b = fw.sb("ht", [128, 1024])
        xn_t, xn_b = P[4][0][:, 0:1024], P[4][1]
        W = [fw.sb("W%d" % i, [128, 4096]) for i in range(2)]
        NG = 7
        G = [fw.sb("G%d" % i, [128, 1024]) for i in range(NG)]
        ST_t, ST_b = fw.sb("ST", [128, 2048])
        kTr = [fw.sb("kTr%d" % i, [64, 4, 128]) for i in range(2)]
        vr = [fw.sb("vr%d" % i, [128, 256]) for i in range(2)]
        gmix_t, gmix_b = fw.sb("gmix", [128, 1024])
        gffn_t, gffn_b = fw.sb("gffn", [128, 1024])
        gX_t, gX_b = fw.sb("gX", [128, 2048])
        ident_t, ident_b = fw.sb("ident", [128, 128])
        tri_t, tri_b = fw.sb("tri", [64, 64])
        negm_t, negm_b = fw.sb("negm", [64, 64])
        ones_t, ones_b = fw.sb("ones", [64, 128])
        iota_t, iota_b = fw.sb("iota16", [128, 16])
        thr_t, thr_b = fw.sb("thr15", [128, 15])
        cst_t, cst_b = fw.sb("cst", [128, 4])
        lc_t, lc_b = fw.sb("lc", [128, 320])
        amask_t, amask_b = fw.sb("amask", [128, 384])
        sk_t, sk_b = fw.sb("subk", [128, 2, 128])
        sm_t, sm_b = fw.sb("small", [128, 1024])
        bo_t, bo_b = fw.sb("bo", [128, 1024])
        htp_t, htp_b = fw.sb("htp", [128, 1024])
        xnp_t, xnp_b = fw.sb("xnp", [128, 1024])
        psm_t, psm_b = fw.sb("psm", [128, 1288])
        eid_t, eid_b = fw.sb("eid", [128, 128], U32)
        idx_t, idx_b = fw.sb("idxu", [128, 384], U32)

        PS = []
        for i in range(8):
            t_ = st.enter_context(nc.psum_tensor("ps%d" % i, [128, 512], F32))
            PS.append((t_, fw.buf("ps%d" % i)))
        ps_i = [0]

        def psum():
            r = PS[ps_i[0] % 8]
            ps_i[0] += 1
            return r

        w_i = [0]

        def wslot():
            r = W[w_i[0] % 2]
            w_i[0] += 1
            return r

        cb_ = cst_b
        op("pool", lambda e: e.memset(cst_t[:, 0:1], 1e-6), writes=[cb_])
        op("pool", lambda e: e.memset(cst_t[:, 1:2], 1.0), writes=[cb_])
        op("pool", lambda e: e.memset(cst_t[:, 2:4], 0.0), writes=[cb_])
        EPS = cst_t[:, 0:1]
        ONE = cst_t[:, 1:2]
        op("pool", lambda e: e.memset(ident_t[:], 0.0), writes=[ident_b])
        op("pool", lambda e: e.affine_select(out=ident_t[:], in_=ident_t[:], pattern=[[-1, 128]],
                                             compare_op=ALU.not_equal, fill=1.0, base=0,
                                             channel_multiplier=1), reads=[ident_b], writes=[ident_b])
        op("pool", lambda e: e.memset(tri_t[:], 1.0), writes=[tri_b])
        op("pool", lambda e: e.affine_select(out=tri_t[:], in_=tri_t[:], pattern=[[1, 64]],
                                             compare_op=ALU.is_ge, fill=0.0, base=0,
                                             channel_multiplier=-1), reads=[tri_b], writes=[tri_b])
        op("pool", lambda e: e.memset(negm_t[:], 0.0), writes=[negm_b])
        op("pool", lambda e: e.affine_select(out=negm_t[:], in_=negm_t[:], pattern=[[1, 64]],
                                             compare_op=ALU.is_ge, fill=NEG, base=0,
                                             channel_multiplier=-1), reads=[negm_b], writes=[negm_b])
        op("pool", lambda e: e.memset(ones_t[:], 1.0), writes=[ones_b])
        op("pool", lambda e: e.iota(iota_t[:], pattern=[[1, 16]], base=0, channel_multiplier=0,
                                    allow_small_or_imprecise_dtypes=True), writes=[iota_b])
        op("pool", lambda e: e.iota(thr_t[:], pattern=[[16, 15]], base=16, channel_multiplier=0,
                                    allow_small_or_imprecise_dtypes=True), writes=[thr_b])
        op("pool", lambda e: e.memset(amask_t[:], 0.0), writes=[amask_b])
        op("pool", lambda e: e.memset(amask_t[0:64, 192:256], NEG), reads=[amask_b], writes=[amask_b])
        op("pool", lambda e: e.memset(amask_t[64:128, 0:64], NEG), reads=[amask_b], writes=[amask_b])
        op("pool", lambda e: e.memset(amask_t[0:64, 256 + 64:256 + 128], NEG), reads=[amask_b], writes=[amask_b])

        def rmsnorm(src_t, src_b, nt, g_ap, g_b, dst_t, dst_b, ss_ap, ss_b):
            op("dve", lambda e: e.memset(ss_ap[:, 0:1], 0.0), writes=[ss_b])
            op("act", lambda e: e.activation(out=dst_t[0:nt, :], in_=src_t[0:nt, :], func=ACT.Square,
                                             accum_out=ss_ap[:, 0:1]), reads=[src_b, ss_b], writes=[dst_b, ss_b])
            op("act", lambda e: e.activation(out=ss_ap[:, 1:2], in_=ss_ap[:, 0:1], func=ACT.Sqrt,
                                             scale=1.0 / D, bias=EPS[0:nt, :]), reads=[ss_b, cst_b], writes=[ss_b])
            op("dve", lambda e: e.reciprocal(out=ss_ap[:, 2:3], in_=ss_ap[:, 1:2]), reads=[ss_b], writes=[ss_b])
            op("dve", lambda e: e.scalar_tensor_tensor(out=dst_t[0:nt, :], in0=src_t[0:nt, :], scalar=ss_ap[:, 2:3],
                                                        in1=g_ap[0:nt, :], op0=ALU.mult, op1=ALU.mult),
               reads=[src_b, ss_b, g_b], writes=[dst_b])

        def to_featmajor(src_t, src_b, nt, ncols, dstT, dstT_b, tok0=0, eng_alt=True):
            nb = ncols // 128
            for b0 in range(0, nb, 4):
                pt, pb = psum()
                nn = min(4, nb - b0)
                for c in range(nn):
                    op("pe", lambda e, c=c: e.transpose(out=pt[:, c * 128:c * 128 + nt],
                                                        in_=src_t[0:nt, (b0 + c) * 128:(b0 + c + 1) * 128],
                                                        identity=ident_t[0:nt, 0:nt]),
                       reads=[src_b, ident_b], writes=[pb])
                src_v = pt[:, 0:nn * 128].rearrange("p (c n) -> p c n", c=nn)[:, :, 0:nt]
                eng = "act" if (b0 // 4) % 2 == 0 else "dve"
                if eng == "act":
                    op("act", lambda e: e.copy(out=dstT[:, b0:b0 + nn, tok0:tok0 + nt], in_=src_v),
                       reads=[pb], writes=[dstT_b])
                else:
                    op("dve", lambda e: e.tensor_copy(out=dstT[:, b0:b0 + nn, tok0:tok0 + nt], in_=src_v),
                       reads=[pb], writes=[dstT_b])

        def load_w(w_ap, nk, c0, ncols, prows=128):
            wt, wb = wslot()
            view = wt[0:prows, 0:nk * ncols].rearrange("p (k n) -> p k n", k=nk)
            src = w_ap[:, c0:c0 + ncols].rearrange("(k p) n -> p k n", p=prows)
            fw.dma("sp", view, src, writes=[wb], owner=wb)
            return view, wb

        def load_bc(dst_ap, dst_b, src_row_ap, nparts=128):
            fw.dma("sp", dst_ap, src_row_ap.partition_broadcast(nparts), writes=[dst_b], owner=dst_b)

        sm = sm_t
        smb = sm_b

        def mamba_tile(l, t, ci):
            nt = t.nt
            first_tile = (t.kind == "s") or (t.idx == 0)
            last_tile = (t.kind == "s") or (t.idx == NPT - 1)
            fw.dma("sp", ht_t[0:nt, :], h_ap(ci, t), reads=[t.hb[ci]], writes=[ht_b], owner=ht_b)
            rmsnorm(ht_t, ht_b, nt, gmix_t, gmix_b, xn_t, xn_b, sm[0:nt, 0:4], smb)
            to_featmajor(xn_t, xn_b, nt, D, xnT_t, xnT_b)
            fw.mark(1)
            yield
            w_in = m_w_in[l]
            if first_tile:
                if t.kind == "p":
                    op("pool", lambda e: e.memset(ST_t[:], 0.0), writes=[ST_b])
                    op("pool", lambda e: e.memset(halo_t[:], 0.0), writes=[halo_b])
                else:
                    for c0 in range(32):
                        src = st_conv[l, t.idx][:, c0 * 128:(c0 + 1) * 128].rearrange("r p -> p r")
                        fw.dma("sp", halo_t[:, c0, :], src, writes=[halo_b], owner=halo_b, slow=True)
                    stg_t, stg_b = P[4]
                    fw.dma("sp", stg_t[:].rearrange("p (m n) -> p m n", m=16),
                           st_ssm[l, t.idx].rearrange("(m p) n -> p m n", p=128), writes=[stg_b], owner=stg_b)
                    to_featmajor(stg_t, stg_b, 128, 2048, ST_t[:].rearrange("p (m n) -> p m n", m=16), ST_b)

            ynT_t, ynT_b = P[7]
            ynT = ynT_t[:].rearrange("p (k n) -> p k n", k=16)
            op("pool", lambda e: e.tensor_copy(out=xbc_t[:, :, 0:3], in_=halo_t[:]), reads=[halo_b], writes=[xbc_b])
            for blk in range(8):
                wv, wb = load_w(w_in, 8, 2048 + blk * 512, 512)
                pt, pb = psum()
                for cc in range(4):
                    for k in range(8):
                        op("pe", lambda e, cc=cc, k=k: e.matmul(pt[:, cc * nt:(cc + 1) * nt], lhsT=wv[:, k, cc * 128:(cc + 1) * 128],
                                                                 rhs=xnT_t[:, k, 0:nt], start=(k == 0), stop=(k == 7)),
                           reads=[wb, xnT_b], writes=[pb])
                    yield
                op("act", lambda e: e.copy(out=xbc_t[:, blk * 4:blk * 4 + 4, 3:3 + nt],
                                           in_=pt[:, 0:4 * nt].rearrange("p (c n) -> p c n", c=4)),
                   reads=[pb], writes=[xbc_b])
            op("pool", lambda e: e.tensor_copy(out=halo_t[:], in_=xbc_t[:, :, nt:nt + 3]), reads=[xbc_b], writes=[halo_b])
            for (q0, Q) in t.chunks:
                xc_t, xc_b = P[0]
                xt_t, xt_b = P[1]
                zs_t, zs_b = P[2]
                ab_t, ab_b = P[3]
                lm_t, lm_b = P[4]
                xd_t, xd_b = P[5]
                bt_t, bt_b = P[6]
                xc = xc_t[:, 0:32 * Q].rearrange("p (c n) -> p c n", c=32)
                abc = ab_t[:, 0:32 * Q].rearrange("p (c n) -> p c n", c=32)
                lm = lm_t[0:Q, 0:32 * Q].rearrange("p (c n) -> p c n", c=32)
                xbe = xbc_t[:, :, q0:q0 + Q + 3]
                fw.mark(2)
                for blk in range(4):
                    wv, wb = load_w(w_in, 8, blk * 512, 512)
                    pt, pb = psum()
                    for k in range(8):
                        op("pe", lambda e, k=k: e.matmul(pt[0:Q, :], lhsT=xnT_t[:, k, q0:q0 + Q], rhs=wv[:, k, :],
                                                         start=(k == 0), stop=(k == 7)), reads=[wb, xnT_b], writes=[pb])
                    op("act", lambda e: e.activation(out=zs_t[0:Q, blk * 512:(blk + 1) * 512], in_=pt[0:Q, :], func=ACT.Silu),
                       reads=[pb], writes=[zs_b])
                    yield
                wv, wb = load_w(w_in, 8, 6144, 32)
                pt, pb = psum()
                for k in range(8):
                    op("pe", lambda e, k=k: e.matmul(pt[0:Q, 0:32], lhsT=xnT_t[:, k, q0:q0 + Q], rhs=wv[:, k, :],
                                                     start=(k == 0), stop=(k == 7)), reads=[wb, xnT_b], writes=[pb])
                s_x = sm[0:Q, 32:64]
                s_ax = sm[0:Q, 64:96]
                s_e = sm[0:Q, 96:128]
                s_dt = sm[0:Q, 128:160]
                s_a = sm[0:Q, 160:192]
                s_ac = sm[0:Q, 192:224]
                dec_bc = sm[:, 224:256]
                R = [smb]
                op("dve", lambda e: e.tensor_tensor(out=s_x, in0=pt[0:Q, 0:32], in1=lc_t[0:Q, 0:32], op=ALU.add),
                   reads=[pb, lc_b], writes=R)
                op("act", lambda e: e.activation(out=s_ax, in_=s_x, func=ACT.Abs), reads=R, writes=R)
                op("act", lambda e: e.activation(out=s_e, in_=s_ax, func=ACT.Exp, scale=-1.0), reads=R, writes=R)
                op("act", lambda e: e.activation(out=s_e, in_=s_e, func=ACT.Ln, bias=ONE[0:Q, :]), reads=R + [cst_b], writes=R)
                op("dve", lambda e: e.tensor_scalar_max(out=s_x, in0=s_x, scalar1=0.0), reads=R, writes=R)
                op("dve", lambda e: e.tensor_tensor(out=s_dt, in0=s_x, in1=s_e, op=ALU.add), reads=R, writes=R)
                op("dve", lambda e: e.tensor_tensor(out=s_a, in0=s_dt, in1=lc_t[0:Q, 32:64], op=ALU.mult), reads=R + [lc_b], writes=R)
                fw.mark(3)
                yield
                tmp = xd_t[:, 0:32 * Q].rearrange("p (c n) -> p c n", c=32)
                cw = lc_t[:, 128:256].rearrange("p (c k) -> p c k", k=4)
                cbias = lc_t[:, 256:288]
                op("dve", lambda e: e.tensor_tensor(out=xc, in0=xbe[:, :, 0:Q], in1=bc(cw[:, :, 0:1], [128, 32, Q]), op=ALU.mult),
                   reads=[xbc_b, lc_b], writes=[xc_b])
                for k in range(1, 4):
                    op("dve", lambda e, k=k: e.tensor_tensor(out=tmp, in0=xbe[:, :, k:k + Q], in1=bc(cw[:, :, k:k + 1], [128, 32, Q]), op=ALU.mult),
                       reads=[xbc_b, lc_b], writes=[xd_b])
                    op("dve", lambda e: e.tensor_tensor(out=xc, in0=xc, in1=tmp, op=ALU.add), reads=[xc_b, xd_b], writes=[xc_b])
                    yield
                op("dve", lambda e: e.tensor_tensor(out=xc, in0=xc, in1=bc(cbias.unsqueeze(2), [128, 32, Q]), op=ALU.add),
                   reads=[xc_b, lc_b], writes=[xc_b])
                op("act", lambda e: e.activation(out=xc, in_=xc, func=ACT.Silu), reads=[xc_b], writes=[xc_b])
                fw.mark(4)
                yield
                for b0 in range(0, 24, 4):
                    pt2, pb2 = psum()
                    for c in range(4):
                        op("pe", lambda e, c=c: e.transpose(out=pt2[0:Q, c * 128:(c + 1) * 128], in_=xc[:, b0 + c, :], identity=ident_t[:]),
                           reads=[xc_b, ident_b], writes=[pb2])
                    if b0 < 16:
                        dst, dstb = xt_t[0:Q, b0 * 128:(b0 + 4) * 128], xt_b
                    else:
                        dst, dstb = bt_t[0:Q, (b0 - 16) * 128:(b0 - 12) * 128], bt_b
                    if (b0 // 4) % 2 == 0:
                        op("act", lambda e: e.copy(out=dst, in_=pt2[0:Q, :]), reads=[pb2], writes=[dstb])
                    else:
                        op("dve", lambda e: e.tensor_copy(out=dst, in_=pt2[0:Q, :]), reads=[pb2], writes=[dstb])
                    yield
                fw.mark(5)
                yield
                pt3, pb3 = psum()
                op("pe", lambda e: e.matmul(pt3[0:Q, 0:32], lhsT=tri_t[0:Q, 0:Q], rhs=s_a, start=True, stop=True),
                   reads=[tri_b] + R, writes=[pb3])
                op("act", lambda e: e.copy(out=s_ac, in_=pt3[0:Q, 0:32]), reads=[pb3], writes=R)
                op("dve", lambda e: e.tensor_tensor(out=lm, in0=bc(s_a.unsqueeze(2), [Q, 32, Q]),
                                                    in1=bc(tri_t[0:Q, 0:Q].unsqueeze(1), [Q, 32, Q]), op=ALU.mult),
                   reads=R + [tri_b], writes=[lm_b])
                ncol = 32 * Q
                for c0 in range(0, ncol, 512):
                    n_ = min(512, ncol - c0)
                    pt4, pb4 = psum()
                    op("pe", lambda e: e.matmul(pt4[:, 0:n_], lhsT=ones_t[0:Q, :], rhs=lm_t[0:Q, c0:c0 + n_], start=True, stop=True),
                       reads=[ones_b, lm_b], writes=[pb4])
                    op("act", lambda e: e.copy(out=ab_t[:, c0:c0 + n_], in_=pt4[:, 0:n_]), reads=[pb4], writes=[ab_b])
                    yield
                fw.mark(6)
                yield
                op("dve", lambda e: e.tensor_tensor(out=lm, in0=abc[0:Q], in1=bc(s_ac.unsqueeze(2), [Q, 32, Q]), op=ALU.subtract),
                   reads=[ab_b] + R, writes=[lm_b])
                op("dve", lambda e: e.tensor_tensor(out=lm, in0=lm, in1=bc(negm_t[0:Q, 0:Q].unsqueeze(1), [Q, 32, Q]), op=ALU.add),
                   reads=[lm_b, negm_b], writes=[lm_b])
                op("act", lambda e: e.activation(out=lm, in_=lm, func=ACT.Exp), reads=[lm_b], writes=[lm_b])
                s_de = sm[0:Q, 288:320]
                op("dve", lambda e: e.tensor_copy(out=s_de, in_=lm[:, :, Q - 1]), reads=[lm_b], writes=R)
                op("act", lambda e: e.activation(out=abc, in_=abc, func=ACT.Exp), reads=[ab_b], writes=[ab_b])
                op("dve", lambda e: e.tensor_copy(out=dec_bc, in_=abc[:, :, Q - 1]), reads=[ab_b], writes=R)
                abc4 = ab_t[:, 0:32 * Q].rearrange("p (g r n) -> p g r n", g=8, r=4)
                op("dve", lambda e: e.tensor_tensor(out=abc4, in0=abc4, in1=bc(xc[:, 24:32, :].unsqueeze(2), [128, 8, 4, Q]), op=ALU.mult),
                   reads=[ab_b, xc_b], writes=[ab_b])
                fw.mark(7)
                yield
                pt5, pb5 = psum()
                for g in range(8):
                    op("pe", lambda e, g=g: e.matmul(pt5[0:Q, g * Q:(g + 1) * Q], lhsT=xc[:, 16 + g, :], rhs=xc[:, 24 + g, :], start=True, stop=True),
                       reads=[xc_b], writes=[pb5])
                lm4 = lm_t[0:Q, 0:32 * Q].rearrange("p (g r n) -> p g r n", g=8, r=4)
                cb4 = pt5[0:Q, 0:8 * Q].rearrange("p (g n) -> p g n", g=8)
                op("dve", lambda e: e.tensor_tensor(out=lm4, in0=lm4, in1=bc(cb4.unsqueeze(2), [Q, 8, 4, Q]), op=ALU.mult),
                   reads=[lm_b, pb5], writes=[lm_b])
                xt3 = xt_t[0:Q, :].rearrange("p (h d) -> p h d", h=32)
                xd3 = xd_t[0:Q, :].rearrange("p (h d) -> p h d", h=32)
                op("dve", lambda e: e.tensor_tensor(out=xd3, in0=xt3, in1=bc(s_dt.unsqueeze(2), [Q, 32, 64]), op=ALU.mult),
                   reads=[xt_b] + R, writes=[xd_b])
                op("dve", lambda e: e.tensor_tensor(out=xt3, in0=xt3, in1=bc(lc_t[0:Q, 64:96].unsqueeze(2), [Q, 32, 64]), op=ALU.mult),
                   reads=[xt_b, lc_b], writes=[xt_b])
                fw.mark(8)
                yield
                for grp in range(4):
                    pt6, pb6 = psum()
                    for hh in range(8):
                        h_ = grp * 8 + hh
                        op("pe", lambda e, h_=h_, hh=hh: e.matmul(pt6[0:Q, hh * 64:(hh + 1) * 64], lhsT=lm[:, h_, :], rhs=xd_t[0:Q, h_ * 64:(h_ + 1) * 64],
                                                                   start=True, stop=False), reads=[lm_b, xd_b], writes=[pb6])
                        op("pe", lambda e, h_=h_, hh=hh: e.matmul(pt6[0:Q, hh * 64:(hh + 1) * 64], lhsT=abc[:, h_, :], rhs=ST_t[:, h_ * 64:(h_ + 1) * 64],
                                                                   start=False, stop=True), reads=[ab_b, ST_b], writes=[pb6])
                    op("dve", lambda e: e.tensor_tensor(out=xt_t[0:Q, grp * 512:(grp + 1) * 512], in0=xt_t[0:Q, grp * 512:(grp + 1) * 512],
                                                        in1=pt6[0:Q, :], op=ALU.add), reads=[xt_b, pb6], writes=[xt_b])
                    yield
                op("dve", lambda e: e.tensor_tensor(out=xd3, in0=xd3, in1=bc(s_de.unsqueeze(2), [Q, 32, 64]), op=ALU.mult),
                   reads=[xd_b] + R, writes=[xd_b])
                fw.mark(9)
                yield
                ST3 = ST_t[:].rearrange("p (h d) -> p h d", h=32)
                for grp in range(4):
                    pt7, pb7 = psum()
                    for hh in range(8):
                        h_ = grp * 8 + hh
                        g = h_ // 4
                        op("pe", lambda e, h_=h_, hh=hh, g=g: e.matmul(pt7[:, hh * 64:(hh + 1) * 64], lhsT=bt_t[0:Q, g * 128:(g + 1) * 128],
                                                                        rhs=xd_t[0:Q, h_ * 64:(h_ + 1) * 64], start=True, stop=True),
                           reads=[bt_b, xd_b], writes=[pb7])
                    sl = ST3[:, grp * 8:(grp + 1) * 8, :]
                    op("dve", lambda e: e.tensor_tensor(out=sl, in0=sl, in1=bc(dec_bc[:, grp * 8:(grp + 1) * 8].unsqueeze(2), [128, 8, 64]), op=ALU.mult),
                       reads=[ST_b] + R, writes=[ST_b])
                    op("dve", lambda e: e.tensor_tensor(out=ST_t[:, grp * 512:(grp + 1) * 512], in0=ST_t[:, grp * 512:(grp + 1) * 512],
                                                        in1=pt7[:, :], op=ALU.add), reads=[ST_b, pb7], writes=[ST_b])
                    yield
                fw.mark(10)
                yield
                y = xt_t[0:Q, :]
                op("dve", lambda e: e.tensor_tensor(out=y, in0=y, in1=zs_t[0:Q, :], op=ALU.mult), reads=[xt_b, zs_b], writes=[xt_b])
                ss = sm[0:Q, 0:4]
                op("dve", lambda e: e.memset(ss[:, 0:1], 0.0), writes=R)
                op("act", lambda e: e.activation(out=zs_t[0:Q, :], in_=y, func=ACT.Square, accum_out=ss[:, 0:1]),
                   reads=[xt_b] + R, writes=[zs_b] + R)
                op("act", lambda e: e.activation(out=ss[:, 1:2], in_=ss[:, 0:1], func=ACT.Sqrt, scale=1.0 / 2048, bias=EPS[0:Q, :]),
                   reads=R + [cst_b], writes=R)
                op("dve", lambda e: e.reciprocal(out=ss[:, 2:3], in_=ss[:, 1:2]), reads=R, writes=R)
                op("dve", lambda e: e.scalar_tensor_tensor(out=y, in0=y, scalar=ss[:, 2:3], in1=gX_t[0:Q, :], op0=ALU.mult, op1=ALU.mult),
                   reads=[xt_b, gX_b] + R, writes=[xt_b])
                to_featmajor(xt_t, xt_b, Q, 2048, ynT, ynT_b, tok0=q0)

            fw.mark(11)
            yield
            for blk in range(4):
                wv, wb = load_w(m_w_out[l], 16, blk * 256, 256)
                pt, pb = psum()
                for k in range(16):
                    op("pe", lambda e, k=k: e.matmul(pt[0:nt, 0:256], lhsT=ynT[:, k, 0:nt], rhs=wv[:, k, :], start=(k == 0), stop=(k == 15)),
                       reads=[wb, ynT_b], writes=[pb])
                op("dve", lambda e: e.tensor_tensor(out=ht_t[0:nt, blk * 256:(blk + 1) * 256], in0=ht_t[0:nt, blk * 256:(blk + 1) * 256],
                                                    in1=pt[0:nt, 0:256], op=ALU.add), reads=[ht_b, pb], writes=[ht_b])
                yield
            fw.dma("sp", h_ap(ci + 1, t), ht_t[0:nt, :], reads=[ht_b], writes=[t.hb[ci + 1]], owner=ht_b)

            fw.mark(12)
            if last_tile:
                stg_t, stg_b = P[4]
                to_featmajor(ST_t, ST_b, 128, 2048, stg_t[:].rearrange("p (m n) -> p m n", m=16), stg_b)
                dst = o_pssm[l] if t.kind == "p" else o_sssm[l, t.idx]
                fw.dma("sp", dst.rearrange("(m p) n -> p m n", p=128), stg_t[:].rearrange("p (m n) -> p m n", m=16),
                       reads=[stg_b], owner=stg_b)
                outs_b.append(stg_b)
                fw.mark(13)
                dstc = o_pconv[l] if t.kind == "p" else o_sconv[l, t.idx]
                for c0 in range(32):
                    dv = dstc[:, c0 * 128:(c0 + 1) * 128].rearrange("r p -> p r")
                    fw.dma("sp", dv, halo_t[:, c0, :], reads=[halo_b], owner=halo_b, slow=True)
                outs_b.append(halo_b)

        def attn_tile(l, t, ci):
            j = l - 2
            nt = t.nt
            fw.dma("sp", ht_t[0:nt, :], h_ap(ci, t), reads=[t.hb[ci]], writes=[ht_b], owner=ht_b)
            xkT_t, xkT_b = P[7]
            xkT = xkT_t[:, 0:1024].rearrange("p (k n) -> p k n", k=8)
            stg_t, stg_b = P[6]
            if t.kind == "p":
                cur = t.idx % 2
                prev = 1 - cur
                blocks = ([(prev, 128)] if t.idx > 0 else []) + [(cur, 128)]
            else:
                prev, cur = 0, 1
                blocks = [(prev, 128), (cur, TS)]
                fw.dma("sp", stg_t[:, 0:256], ck[t.idx], writes=[stg_b], owner=stg_b)
                fw.dma("sp", vr[prev][0][:, :], cv[t.idx], writes=[vr[prev][1]], owner=vr[prev][1])
                if j == 0:
                    fw.dma("sp", o_sk[t.idx][0:128 - TS, :], stg_t[TS:128, 0:256], reads=[stg_b], owner=stg_b)
                    fw.dma("sp", o_sv[t.idx][0:128 - TS, :], vr[prev][0][TS:128, :], reads=[vr[prev][1]], owner=vr[prev][1])
                    outs_b.extend([stg_b, vr[prev][1]])
                pt, pb = psum()
                for kh in range(4):
                    op("pe", lambda e, kh=kh: e.transpose(out=pt[0:64, kh * 128:(kh + 1) * 128], in_=stg_t[:, kh * 64:(kh + 1) * 64], identity=ident_t[:]),
                       reads=[stg_b, ident_b], writes=[pb])
                op("act", lambda e: e.copy(out=kTr[prev][0][:], in_=pt[0:64, :].rearrange("p (k n) -> p k n", k=4)),
                   reads=[pb], writes=[kTr[prev][1]])
            kT_c, kT_cb = kTr[cur]
            yield
            v_c, v_cb = vr[cur]
            if j == 0:
                rmsnorm(ht_t, ht_b, nt, gX_t[:, 0:1024], gX_b, xn_t, xn_b, sm[0:nt, 0:4], smb)
                to_featmajor(xn_t, xn_b, nt, D, xkT, xkT_b)
                wv, wb = load_w(a_w_kv, 8, 0, 512)
                pt, pb = psum()
                for kh in range(4):
                    for k in range(8):
                        op("pe", lambda e, kh=kh, k=k: e.matmul(pt[0:64, kh * nt:(kh + 1) * nt], lhsT=wv[:, k, kh * 64:(kh + 1) * 64], rhs=xkT[:, k, 0:nt],
                                                                 start=(k == 0), stop=(k == 7)), reads=[wb, xkT_b], writes=[pb])
                for kh in range(4):
                    op("act", lambda e, kh=kh: e.activation(out=kT_c[:, kh, 0:nt], in_=pt[0:64, kh * nt:(kh + 1) * nt], func=ACT.Identity,
                                                            bias=lc_t[0:64, 96 + kh:97 + kh]), reads=[pb, lc_b], writes=[kT_cb])
                pt2, pb2 = psum()
                for k in range(8):
                    op("pe", lambda e, k=k: e.matmul(pt2[0:nt, :], lhsT=xkT[:, k, 0:nt], rhs=wv[:, k, :], start=(k == 0), stop=(k == 7)),
                       reads=[wb, xkT_b], writes=[pb2])
                kvt_t, kvt_b = P[5]
                op("dve", lambda e: e.tensor_tensor(out=kvt_t[0:nt, 0:512], in0=pt2[0:nt, :], in1=gX_t[0:nt, 1024:1536], op=ALU.add),
                   reads=[pb2, gX_b], writes=[kvt_b])
                op("pool", lambda e: e.tensor_copy(out=v_c[0:nt, :], in_=kvt_t[0:nt, 256:512]), reads=[kvt_b], writes=[v_cb])
                fw.dma("sp", kT_d[t.kvslot][:, :, 0:nt], kT_c[:, :, 0:nt], reads=[kT_cb], writes=[t.kTd_b], owner=kT_cb)
                fw.dma("sp", v_d[t.kvslot][0:nt, :], v_c[0:nt, :], reads=[v_cb], writes=[t.vd_b], owner=v_cb)
                if t.kind == "p" and t.idx == NPT - 1:
                    fw.dma("sp", o_pk, kvt_t[0:128, 0:256], reads=[kvt_b], owner=kvt_b)
                    fw.dma("sp", o_pv, kvt_t[0:128, 256:512], reads=[kvt_b], owner=kvt_b)
                    outs_b.append(kvt_b)
                if t.kind == "s":
                    fw.dma("sp", o_sk[t.idx][128 - TS:128, :], kvt_t[0:TS, 0:256], reads=[kvt_b], owner=kvt_b)
                    fw.dma("sp", o_sv[t.idx][128 - TS:128, :], kvt_t[0:TS, 256:512], reads=[kvt_b], owner=kvt_b)
                    outs_b.append(kvt_b)
            else:
                fw.dma("sp", kT_c[:, :, 0:nt], kT_d[t.kvslot][:, :, 0:nt], reads=[t.kTd_b], writes=[kT_cb], owner=kT_cb)
                fw.dma("sp", v_c[0:nt, :], v_d[t.kvslot][0:nt, :], reads=[t.vd_b], writes=[v_cb], owner=v_cb)
            yield
            rmsnorm(ht_t, ht_b, nt, gmix_t, gmix_b, xn_t, xn_b, sm[0:nt, 0:4], smb)
            to_featmajor(xn_t, xn_b, nt, D, xnT_t, xnT_b)
            QT_t, QT_b = P[0]
            QT = QT_t[0:64, :].rearrange("p (h n) -> p h n", h=16)
            for blk in range(2):
                wv, wb = load_w(a_w_q[j], 8, blk * 512, 512)
                for h4 in range(2):
                    pt, pb = psum()
                    for hh in range(4):
                        for k in range(8):
                            op("pe", lambda e, hh=hh, k=k: e.matmul(pt[0:64, hh * nt:(hh + 1) * nt], lhsT=wv[:, k, (h4 * 4 + hh) * 64:(h4 * 4 + hh + 1) * 64],
                                                                     rhs=xnT_t[:, k, 0:nt], start=(k == 0), stop=(k == 7)), reads=[wb, xnT_b], writes=[pb])
                    for hh in range(4):
                        hq = blk * 8 + h4 * 4 + hh
                        op("act", lambda e, hh=hh, hq=hq: e.activation(out=QT[:, hq, 0:nt], in_=pt[0:64, hh * nt:(hh + 1) * nt], func=ACT.Identity,
                                                                        scale=0.125, bias=lc_t[0:64, 112 + hq:113 + hq]), reads=[pb, lc_b], writes=[QT_b])
                    yield
            oT_t, oT_b = P[1]
            oT = oT_t[0:64, :].rearrange("p (h n) -> p h n", h=16)
            NK = sum(n for _, n in blocks)
            if t.kind == "p":
                mask = amask_t[0:nt, 0:256] if t.idx > 0 else amask_t[0:nt, 256:384]
            else:
                mask = None
            S_par = [P[2], P[4]]
            PT_par = [P[3], P[5]]
            pst = {}

            def stageA(hq):
                kvh = hq // 4
                S_t, S_b = S_par[hq % 2]
                pt, pb = psum()
                off = 0
                for (slot, n) in blocks:
                    op("pe", lambda e, slot=slot, n=n, off=off: e.matmul(pt[0:nt, off:off + n], lhsT=QT[:, hq, 0:nt], rhs=kTr[slot][0][:, kvh, 0:n],
                                                                          start=True, stop=True), reads=[QT_b, kTr[slot][1]], writes=[pb])
                    off += n
                Ss = S_t[0:nt, 0:NK]
                st_ = sm[0:nt, 8 + 8 * (hq % 2):16 + 8 * (hq % 2)]
                if mask is not None:
                    op("dve", lambda e: e.tensor_tensor(out=Ss, in0=pt[0:nt, 0:NK], in1=mask, op=ALU.add), reads=[pb, amask_b], writes=[S_b])
                else:
                    op("dve", lambda e: e.tensor_copy(out=Ss, in_=pt[0:nt, 0:NK]), reads=[pb], writes=[S_b])
                op("dve", lambda e: e.reduce_max(out=st_[:, 0:1], in_=Ss, axis=AX.X), reads=[S_b], writes=[smb])
                op("dve", lambda e: e.tensor_scalar_mul(out=st_[:, 1:2], in0=st_[:, 0:1], scalar1=-1.0), reads=[smb], writes=[smb])
                op("dve", lambda e: e.memset(st_[:, 2:3], 0.0), writes=[smb])
                op("act", lambda e: e.activation(out=Ss, in_=Ss, func=ACT.Exp, bias=st_[:, 1:2], accum_out=st_[:, 2:3]),
                   reads=[S_b, smb], writes=[S_b, smb])
                op("act", lambda e: e.activation(out=st_[:, 3:4], in_=lc_t[0:nt, 288 + hq:289 + hq], func=ACT.Exp, bias=st_[:, 1:2]),
                   reads=[lc_b, smb], writes=[smb])
                op("dve", lambda e: e.tensor_tensor(out=st_[:, 4:5], in0=st_[:, 2:3], in1=st_[:, 3:4], op=ALU.add), reads=[smb], writes=[smb])
                op("dve", lambda e: e.reciprocal(out=st_[:, 5:6], in_=st_[:, 4:5]), reads=[smb], writes=[smb])
                op("dve", lambda e: e.tensor_scalar_mul(out=Ss, in0=Ss, scalar1=st_[:, 5:6]), reads=[S_b, smb], writes=[S_b])

            def stageB(hq):
                kvh = hq // 4
                S_t, S_b = S_par[hq % 2]
                PT_t, PT_b = PT_par[hq % 2]
                pt2, pb2 = psum()
                off = 0
                for bi, (slot, n) in enumerate(blocks):
                    op("pe", lambda e, n=n, off=off, bi=bi: e.transpose(out=pt2[0:n, bi * 128:bi * 128 + nt],
                                                                        in_=S_t[0:nt, off:off + n],
                                                                        identity=ident_t[0:nt, 0:nt]), reads=[S_b, ident_b], writes=[pb2])
                    off += n
                PTs = PT_t[:, 0:256]
                for bi, (slot, n) in enumerate(blocks):
                    op("act", lambda e, n=n, bi=bi: e.copy(out=PTs[0:n, bi * 128:bi * 128 + nt], in_=pt2[0:n, bi * 128:bi * 128 + nt]),
                       reads=[pb2], writes=[PT_b])
                if hq % 4 == 0:
                    pst["po"], pst["pob"] = psum()
                po, pob = pst["po"], pst["pob"]
                for bi, (slot, n) in enumerate(blocks):
                    op("pe", lambda e, n=n, bi=bi, slot=slot: e.matmul(po[0:64, (hq % 4) * nt:(hq % 4 + 1) * nt], lhsT=vr[slot][0][0:n, kvh * 64:(kvh + 1) * 64],
                                                                        rhs=PTs[0:n, bi * 128:bi * 128 + nt], start=(bi == 0), stop=(bi == len(blocks) - 1)),
                       reads=[vr[slot][1], PT_b], writes=[pob])
                if hq % 4 == 3:
                    op("act", lambda e: e.copy(out=oT[:, hq - 3:hq + 1, 0:nt], in_=po[0:64, 0:4 * nt].rearrange("p (h n) -> p h n", h=4)),
                       reads=[pob], writes=[oT_b])

            stageA(0)
            yield
            for hq in range(16):
                if hq + 1 < 16:
                    stageA(hq + 1)
                    yield
                stageB(hq)
                yield
            op("dve", lambda e: e.tensor_tensor(out=ht_t[0:nt, :], in0=ht_t[0:nt, :], in1=bo_t[0:nt, :], op=ALU.add),
               reads=[ht_b, bo_b], writes=[ht_b])
            wo = a_w_o[j].rearrange("(h d) n -> d h n", d=64)
            for blk in range(4):
                wt, wb = wslot()
                wv = wt[0:64, 0:16 * 256].rearrange("p (h n) -> p h n", h=16)
                fw.dma("sp", wv, wo[:, :, blk * 256:(blk + 1) * 256], writes=[wb], owner=wb)
                pt, pb = psum()
                for hq in range(16):
                    op("pe", lambda e, hq=hq: e.matmul(pt[0:nt, 0:256], lhsT=oT[:, hq, 0:nt], rhs=wv[:, hq, :], start=(hq == 0), stop=(hq == 15)),
                       reads=[wb, oT_b], writes=[pb])
                op("dve", lambda e: e.tensor_tensor(out=ht_t[0:nt, blk * 256:(blk + 1) * 256], in0=ht_t[0:nt, blk * 256:(blk + 1) * 256],
                                                    in1=pt[0:nt, 0:256], op=ALU.add), reads=[ht_b, pb], writes=[ht_b])
                yield
            fw.dma("sp", h_ap(ci + 1, t), ht_t[0:nt, :], reads=[ht_b], writes=[t.hb[ci + 1]], owner=ht_b)

        def peer_a(l, t, ci):
            nt = t.nt
            fw.dma("sp", htp_t[0:nt, :], h_ap(ci, t), reads=hbl(t, ci), writes=[htp_b], owner=htp_b)
            rmsnorm(htp_t, htp_b, nt, gffn_t, gffn_b, xnp_t, xnp_b, sm[0:nt, 0:4], smb)
            to_featmajor(xnp_t, xnp_b, nt, D, xnT_t, xnT_b)
            fw.mark(17)
            qT_t, qT_b = P[0]
            qT = qT_t[:].rearrange("p (c n) -> p c n", c=16)
            S_t, S_b = P[1]
            Sw_t, Sw_b = P[2]
            S3 = S_t[0:nt, :].rearrange("p (c n) -> p c n", c=16)
            Sw3 = Sw_t[0:nt, :].rearrange("p (c n) -> p c n", c=16)
            for blk in range(4):
                wv, wb = load_w(p_w_q[l], 8, blk * 512, 512)
                pt, pb = psum()
                for cc in range(4):
                    for k in range(8):
                        op("pe", lambda e, cc=cc, k=k: e.matmul(pt[:, cc * nt:(cc + 1) * nt], lhsT=wv[:, k, cc * 128:(cc + 1) * 128], rhs=xnT_t[:, k, 0:nt],
                                                                 start=(k == 0), stop=(k == 7)), reads=[wb, xnT_b], writes=[pb])
                op("act", lambda e: e.copy(out=qT[:, blk * 4:blk * 4 + 4, 0:nt], in_=pt[:, 0:4 * nt].rearrange("p (c n) -> p c n", c=4)),
                   reads=[pb], writes=[qT_b])
            fw.mark(18)
            for blk in range(4):
                pt, pb = psum()
                for cc in range(4):
                    c = blk * 4 + cc
                    op("pe", lambda e, cc=cc, c=c: e.matmul(pt[0:nt, cc * 128:(cc + 1) * 128], lhsT=qT[:, c, 0:nt], rhs=sk_t[:, c % 2, :], start=True, stop=True),
                       reads=[qT_b, sk_b], writes=[pb])
                if os.environ.get('K_SKIPC') != '1':
                    op("act", lambda e: e.copy(out=S_t[0:nt, blk * 512:(blk + 1) * 512], in_=pt[0:nt, :]), reads=[pb], writes=[S_b])
                if os.environ.get('K_SKIPC') not in ('1', '2'):
                    op("dve", lambda e: e.tensor_copy(out=Sw_t[0:nt, blk * 512:(blk + 1) * 512], in_=S_t[0:nt, blk * 512:(blk + 1) * 512]), reads=[S_b], writes=[Sw_b])
            fw.mark(20)
            R = [smb]
            vals = sm[0:nt, 256:512].rearrange("p (c k) -> p c k", c=16)
            idxf = sm[0:nt, 512:768].rearrange("p (c k) -> p c k", c=16)
            vs = sm[0:nt, 768:896].rearrange("p (h k) -> p h k", h=8)
            posf = sm[0:nt, 896:1024].rearrange("p (h k) -> p h k", h=8)
            idxu = idx_t[0:nt, 0:256].rearrange("p (c k) -> p c k", c=16)
            posu = idx_t[0:nt, 256:384].rearrange("p (h k) -> p h k", h=8)
            for c in range(16):
                op("dve", lambda e, c=c: e.max(out=vals[:, c, 0:8], in_=Sw3[:, c, :]), reads=[Sw_b], writes=R)
                op("dve", lambda e, c=c: e.match_replace(out=Sw3[:, c, :], in_to_replace=vals[:, c, 0:8], in_values=Sw3[:, c, :], imm_value=-1e30),
                   reads=[Sw_b] + R, writes=[Sw_b])
                op("dve", lambda e, c=c: e.max(out=vals[:, c, 8:16], in_=Sw3[:, c, :]), reads=[Sw_b], writes=R)
                op("dve", lambda e, c=c: e.max_index(out=idxu[:, c, 0:8], in_max=vals[:, c, 0:8], in_values=S3[:, c, :]), reads=[S_b] + R, writes=[idx_b])
                op("dve", lambda e, c=c: e.max_index(out=idxu[:, c, 8:16], in_max=vals[:, c, 8:16], in_values=S3[:, c, :]), reads=[S_b] + R, writes=[idx_b])
            op("dve", lambda e: e.tensor_copy(out=idxf, in_=idxu), reads=[idx_b], writes=R)
            vals4 = sm[0:nt, 256:512].rearrange("p (h s k) -> p h s k", h=8, s=2)
            idxf4 = sm[0:nt, 512:768].rearrange("p (h s k) -> p h s k", h=8, s=2)
            cand_t, cand_b = P[3]
            cw_t, cw_b = P[4]
            T1_t, T1_b = P[5]
            cand = cand_t[0:nt, :].rearrange("p (h a b) -> p h a b", h=8, a=16)
            cand3 = cand_t[0:nt, :].rearrange("p (h n) -> p h n", h=8)
            cw3 = cw_t[0:nt, :].rearrange("p (h n) -> p h n", h=8)
            for h in range(8):
                op("dve", lambda e, h=h: e.tensor_tensor(out=cand[:, h], in0=bc(vals4[:, h, 0, :].unsqueeze(2), [nt, 16, 16]),
                                                         in1=bc(vals4[:, h, 1, :].unsqueeze(1), [nt, 16, 16]), op=ALU.add), reads=R, writes=[cand_b])
            op("dve", lambda e: e.tensor_copy(out=cw_t[0:nt, :], in_=cand_t[0:nt, :]), reads=[cand_b], writes=[cw_b])
            for h in range(8):
                op("dve", lambda e, h=h: e.max(out=vs[:, h, 0:8], in_=cw3[:, h, :]), reads=[cw_b], writes=R)
                op("dve", lambda e, h=h: e.match_replace(out=cw3[:, h, :], in_to_replace=vs[:, h, 0:8], in_values=cw3[:, h, :], imm_value=-1e30),
                   reads=[cw_b] + R, writes=[cw_b])
                op("dve", lambda e, h=h: e.max(out=vs[:, h, 8:16], in_=cw3[:, h, :]), reads=[cw_b], writes=R)
                op("dve", lambda e, h=h: e.max_index(out=posu[:, h, 0:8], in_max=vs[:, h, 0:8], in_values=cand3[:, h, :]), reads=[cand_b] + R, writes=[idx_b])
                op("dve", lambda e, h=h: e.max_index(out=posu[:, h, 8:16], in_max=vs[:, h, 8:16], in_values=cand3[:, h, :]), reads=[cand_b] + R, writes=[idx_b])
            op("dve", lambda e: e.tensor_copy(out=posf, in_=posu), reads=[idx_b], writes=R)
            fw.mark(21)
            T15 = cw_t[0:nt, 0:128 * 15].rearrange("p (s m) -> p s m", m=15)
            posf2 = sm[0:nt, 896:1024]
            af = psm_t[0:nt, 0:128]
            bf = psm_t[0:nt, 128:256]
            e1 = psm_t[0:nt, 256:384]
            e2 = psm_t[0:nt, 384:512]
            gate = psm_t[0:nt, 512:640]
            actv = psm_t[0:nt, 640:768]
            wgt = psm_t[0:nt, 768:896]
            tmp1 = psm_t[0:nt, 896:1024]
            tmp2 = psm_t[0:nt, 1024:1152]
            eidf = psm_t[0:nt, 1152:1280]
            Rb = [psm_b]
            op("dve", lambda e: e.tensor_tensor(out=T15, in0=bc(posf2.unsqueeze(2), [nt, 128, 15]), in1=bc(thr_t[0:nt, :].unsqueeze(1), [nt, 128, 15]), op=ALU.is_ge),
               reads=R + [thr_b], writes=[cw_b])
            op("dve", lambda e: e.reduce_sum(out=af, in_=T15, axis=AX.X), reads=[cw_b], writes=Rb)
            op("dve", lambda e: e.scalar_tensor_tensor(out=bf, in0=af, scalar=-16.0, in1=posf2, op0=ALU.mult, op1=ALU.add), reads=Rb + R, writes=Rb)
            T1 = T1_t[0:nt, :].rearrange("p (h k a) -> p h k a", h=8, k=16)
            for (src, which, dst) in ((af, 0, e1), (bf, 1, e2)):
                src3 = src.rearrange("p (h k) -> p h k", h=8)
                dst3 = dst.rearrange("p (h k) -> p h k", h=8)
                for h in range(8):
                    op("dve", lambda e, h=h, src3=src3: e.tensor_tensor(out=T1[:, h], in0=bc(src3[:, h, :].unsqueeze(2), [nt, 16, 16]),
                                                                        in1=bc(iota_t[0:nt, :].unsqueeze(1), [nt, 16, 16]), op=ALU.is_equal),
                       reads=Rb + [iota_b], writes=[T1_b])
                    op("dve", lambda e, h=h, which=which: e.tensor_tensor(out=T1[:, h], in0=T1[:, h], in1=bc(idxf4[:, h, which, :].unsqueeze(1), [nt, 16, 16]), op=ALU.mult),
                       reads=[T1_b] + R, writes=[T1_b])
                op("dve", lambda e, dst=dst: e.reduce_sum(out=dst, in_=T1_t[0:nt, :].rearrange("p (s a) -> p s a", a=16), axis=AX.X), reads=[T1_b], writes=Rb)
            op("dve", lambda e: e.scalar_tensor_tensor(out=eidf, in0=e1, scalar=128.0, in1=e2, op0=ALU.mult, op1=ALU.add), reads=Rb, writes=Rb)
            op("dve", lambda e: e.tensor_scalar_add(out=eidf, in0=eidf, scalar1=float(l * 16384)), reads=Rb, writes=Rb)
            op("dve", lambda e: e.tensor_copy(out=eid_t[0:nt, :], in_=eidf), reads=Rb, writes=[eid_b])
            fw.mark(22)
            gate3 = gate.rearrange("p (h k) -> p h k", h=8)
            op("dve", lambda e: e.tensor_tensor(out=gate3, in0=vs, in1=bc(vs[:, :, 0:1], [nt, 8, 16]), op=ALU.subtract), reads=R, writes=Rb)
            op("act", lambda e: e.activation(out=gate, in_=gate, func=ACT.Exp), reads=Rb, writes=Rb)
            op("dve", lambda e: e.reduce_sum(out=tmp1[:, 0:8], in_=gate3, axis=AX.X), reads=Rb, writes=Rb)
            op("dve", lambda e: e.reciprocal(out=tmp1[:, 8:16], in_=tmp1[:, 0:8]), reads=Rb, writes=Rb)
            op("dve", lambda e: e.tensor_tensor(out=gate3, in0=gate3, in1=bc(tmp1[:, 8:16].unsqueeze(2), [nt, 8, 16]), op=ALU.mult), reads=Rb, writes=Rb)
            fw.mark(23)
        def peer_b(l, t, ci):
            nt = t.nt
            Rb = [psm_b]
            gate = psm_t[0:nt, 512:640]
            actv = psm_t[0:nt, 640:768]
            wgt = psm_t[0:nt, 768:896]
            tmp1 = psm_t[0:nt, 896:1024]
            g_i = [0]

            def gather(tab, jj):
                gt, gb = G[g_i[0] % NG]
                g_i[0] += 1
                fw.dma("pool", None, None, reads=[eid_b], writes=[gb], owner=gb,
                       fn=lambda e: e.indirect_dma_start(out=gt[0:nt, :], out_offset=None, in_=tab,
                                                         in_offset=bass.IndirectOffsetOnAxis(ap=eid_t[0:nt, jj:jj + 1], axis=0)))
                return gt, gb
            op("dve", lambda e: e.memset(actv, 0.0), writes=Rb)
            for jj in range(128):
                gt, gb = gather(p_u, jj)
                op("dve", lambda e, gt=gt, jj=jj: e.scalar_tensor_tensor(out=gt[0:nt, :], in0=gt[0:nt, :], scalar=1.0, in1=xnp_t[0:nt, :],
                                                                           op0=ALU.mult, op1=ALU.mult, accum_out=actv[:, jj:jj + 1]),
                   reads=[gb, xnp_b], writes=[gb] + Rb)
                yield
            fw.mark(24)
            op("dve", lambda e: e.tensor_tensor(out=tmp1, in0=actv, in1=actv, op=ALU.mult), reads=Rb, writes=Rb)
            op("dve", lambda e: e.tensor_scalar(out=tmp1, in0=tmp1, scalar1=0.044715, scalar2=1.0, op0=ALU.mult, op1=ALU.add), reads=Rb, writes=Rb)
            op("dve", lambda e: e.tensor_tensor(out=tmp1, in0=tmp1, in1=actv, op=ALU.mult), reads=Rb, writes=Rb)
            op("act", lambda e: e.activation(out=tmp1, in_=tmp1, func=ACT.Tanh, scale=0.7978845608028654), reads=Rb, writes=Rb)
            op("dve", lambda e: e.tensor_scalar(out=tmp1, in0=tmp1, scalar1=1.0, scalar2=0.5, op0=ALU.add, op1=ALU.mult), reads=Rb, writes=Rb)
            op("dve", lambda e: e.tensor_tensor(out=tmp1, in0=tmp1, in1=actv, op=ALU.mult), reads=Rb, writes=Rb)
            op("dve", lambda e: e.tensor_tensor(out=wgt, in0=tmp1, in1=gate, op=ALU.mult), reads=Rb, writes=Rb)
            for jj in range(128):
                gt, gb = gather(p_v, jj)
                op("dve", lambda e, gt=gt, jj=jj: e.scalar_tensor_tensor(out=htp_t[0:nt, :], in0=gt[0:nt, :], scalar=wgt[:, jj:jj + 1], in1=htp_t[0:nt, :],
                                                                           op0=ALU.mult, op1=ALU.add), reads=[gb, htp_b] + Rb, writes=[htp_b])
                yield
            fw.dma("sp", h_ap(ci + 1, t), htp_t[0:nt, :], reads=[htp_b], writes=hbl(t, ci + 1), owner=htp_b)
            if l == 3:
                rmsnorm(htp_t, htp_b, nt, gX_t[:, 0:1024], gX_b, xnp_t, xnp_b, psm_t[0:nt, 1280:1284], psm_b)
                dst = y_p[t.row0:t.row0 + nt, :] if t.kind == "p" else (y_s[0:nt, :] if t.kind == "sp" else y_s[t.idx * TS:(t.idx + 1) * TS, :])
                fw.dma("sp", dst, xnp_t[0:nt, :], reads=[xnp_b], owner=xnp_b)
                outs_b.append(xnp_b)

        outs_b = []
        pending = [None]
        RATIO = {(True, 'p'): float(os.environ.get('K_RM', '2.4')), (True, 's'): float(os.environ.get('K_RMS', '3.8')),
                 (False, 'p'): float(os.environ.get('K_RA', '6.1')), (False, 's'): float(os.environ.get('K_RAS', '6.1'))}

        def drain2(g1, g2, r=1.0):
            n1 = n2 = 0
            acc = 0.0
            while g1 is not None or g2 is not None:
                if g1 is not None:
                    try:
                        next(g1)
                        n1 += 1
                    except StopIteration:
                        g1 = None
                    acc += r
                else:
                    acc += 1.0
                while g2 is not None and acc >= 1.0:
                    acc -= 1.0
                    try:
                        next(g2)
                        n2 += 1
                    except StopIteration:
                        g2 = None
                if g2 is None:
                    acc = 0.0
            if os.environ.get('K_CNT'):
                print("drain2 yields mixer=%d peer=%d" % (n1, n2))

        for l in range(nlayers):
            load_bc(gmix_t[:], gmix_b, norm_mix[l])
            load_bc(gffn_t[:], gffn_b, norm_ffn[l])
            fw.dma("sp", sk_t[:, 0, :], p_k1T[l], writes=[sk_b], owner=sk_b)
            fw.dma("sp", sk_t[:, 1, :], p_k2T[l], writes=[sk_b], owner=sk_b)
            if l < 2:
                load_bc(gX_t[:], gX_b, m_norm[l])
                load_bc(lc_t[:, 0:32], lc_b, m_dt_bias[l])
                load_bc(lc_t[:, 32:64], lc_b, m_a_log[l])
                load_bc(lc_t[:, 64:96], lc_b, m_d_skip[l])
                fw.dma("sp", lc_t[:, 128:256], m_conv_w[l].rearrange("p c k -> p (c k)"), writes=[lc_b], owner=lc_b)
                fw.dma("sp", lc_t[:, 256:288], m_conv_b[l], writes=[lc_b], owner=lc_b)
                op("act", lambda e: e.activation(out=lc_t[:, 32:64], in_=lc_t[:, 32:64], func=ACT.Exp), reads=[lc_b], writes=[lc_b])
                op("dve", lambda e: e.tensor_scalar_mul(out=lc_t[:, 32:64], in0=lc_t[:, 32:64], scalar1=-1.0), reads=[lc_b], writes=[lc_b])
            else:
                j = l - 2
                if j == 0:
                    load_bc(gX_t[:, 0:1024], gX_b, norm_kv[0])
                    load_bc(gX_t[:, 1024:1536], gX_b, a_b_kv[0])
                    fw.dma("sp", lc_t[0:64, 96:100], a_b_kT, writes=[lc_b], owner=lc_b)
                load_bc(bo_t[:], bo_b, a_b_o[j])
                fw.dma("sp", lc_t[0:64, 112:128], a_b_qT[j], writes=[lc_b], owner=lc_b)
                op("dve", lambda e: e.tensor_scalar_mul(out=lc_t[0:64, 112:128], in0=lc_t[0:64, 112:128], scalar1=0.125), reads=[lc_b], writes=[lc_b])
                load_bc(lc_t[:, 288:304], lc_b, a_sinks[j])
            if l == 3:
                load_bc(gX_t[:, 0:1024], gX_b, norm_final[0])
            try:
                for t in tiles:
                    ci = 2 * l
                    mix = mamba_tile(l, t, ci) if l < 2 else attn_tile(l, t, ci)
                    drain2(mix, pending[0], RATIO[(l < 2, t.kind)])
                    pending[0] = None
                    pt_ = t
                    if t.kind == "s":
                        if t is not stiles[-1]:
                            continue
                        pt_ = speer
                    if not os.environ.get('K_NOPEER'):
                        peer_a(l, pt_, ci + 1)
                        pending[0] = peer_b(l, pt_, ci + 1)
                        if os.environ.get('K_NOILV') or l >= ILV_MAXL:
                            drain2(pending[0], None)
                            pending[0] = None
            except Stop:
                break
        if pending[0] is not None:
            drain2(pending[0], None)
        fw.finish(fw.allbufs, "sp")
        fw.finish(fw.allbufs, "act")
        print("program: ninst=%d nwait=%d sems=%d" % (fw.ninst, fw.nwait, len(fw.sems)))
    return nc


_CACHE = {}


def make_in_maps(inp):
    f = lambda a: np.ascontiguousarray(np.asarray(a, dtype=np.float32))
    shared = {
        "norm_mix": f(inp["norm_mix"]), "norm_ffn": f(inp["norm_ffn"]),
        "norm_kv": f(inp["norm_kv"]).reshape(1, D), "norm_final": f(inp["norm_final"]).reshape(1, D),
        "m_w_in": f(inp["m_w_in"]),
        "m_conv_w": f(np.asarray(inp["m_conv_w"]).reshape(2, 4, 32, 128).transpose(0, 3, 2, 1)),
        "m_conv_b": f(np.asarray(inp["m_conv_b"]).reshape(2, 32, 128).transpose(0, 2, 1)),
        "m_dt_bias": f(inp["m_dt_bias"]), "m_a_log": f(inp["m_a_log"]), "m_d_skip": f(inp["m_d_skip"]),
        "m_norm": f(inp["m_norm"]), "m_w_out": f(inp["m_w_out"]),
        "a_w_kv": f(inp["a_w_kv"]), "a_b_kv": f(inp["a_b_kv"]).reshape(1, 512),
        "a_b_kT": f(np.asarray(inp["a_b_kv"])[:256].reshape(4, 64).T),
        "a_w_q": f(inp["a_w_q"]),
        "a_b_qT": f(np.asarray(inp["a_b_q"]).reshape(2, 16, 64).transpose(0, 2, 1)),
        "a_sinks": f(inp["a_sinks"]), "a_w_o": f(inp["a_w_o"]), "a_b_o": f(inp["a_b_o"]),
        "p_w_q": f(inp["p_w_q"]),
        "p_k1T": f(np.asarray(inp["p_sub_k1"]).transpose(0, 2, 1)),
        "p_k2T": f(np.asarray(inp["p_sub_k2"]).transpose(0, 2, 1)),
        "p_u": f(inp["p_u"]).reshape(4 * 16384, D), "p_v": f(inp["p_v"]).reshape(4 * 16384, D),
    }
    xp = np.asarray(inp["x_prompt"], dtype=np.float32)
    xs = np.asarray(inp["x_sample"], dtype=np.float32)
    ssm = np.asarray(inp["state_ssm"], dtype=np.float32)
    conv = np.asarray(inp["state_conv"], dtype=np.float32)
    ckw = np.asarray(inp["cache_k_win"], dtype=np.float32)
    cvw = np.asarray(inp["cache_v_win"], dtype=np.float32)
    maps = []
    for c in range(8):
        m = dict(shared)
        m["x_p"] = f(xp[c])
        m["x_s"] = f(xs[4 * c:4 * c + 4].reshape(NSS * TS, D))
        m["st_ssm"] = f(ssm[:, 4 * c:4 * c + 4].reshape(2, NSS, 2048, 128))
        m["st_conv"] = f(conv[:, 4 * c:4 * c + 4])
        m["ck"] = f(ckw[4 * c:4 * c + 4].reshape(NSS, 128, 256))
        m["cv"] = f(cvw[4 * c:4 * c + 4].reshape(NSS, 128, 256))
        maps.append(m)
    return maps


def assemble(results):
    r = results
    y_prompt = np.stack([r[c]["y_p"] for c in range(8)], 0)
    y_sample = np.concatenate([r[c]["y_s"].reshape(NSS, TS, D) for c in range(8)], 0)
    pr_ssm = np.stack([r[c]["o_pssm"].reshape(2, 32, 64, 128) for c in range(8)], 1)
    pr_conv = np.stack([r[c]["o_pconv"] for c in range(8)], 1)
    pr_k = np.stack([r[c]["o_pk"].reshape(128, 4, 64) for c in range(8)], 0)
    pr_v = np.stack([r[c]["o_pv"].reshape(128, 4, 64) for c in range(8)], 0)
    sm_ssm = np.concatenate([r[c]["o_sssm"].reshape(2, NSS, 32, 64, 128) for c in range(8)], 1)
    sm_conv = np.concatenate([r[c]["o_sconv"] for c in range(8)], 1)
    sm_k = np.concatenate([r[c]["o_sk"].reshape(NSS, 128, 4, 64) for c in range(8)], 0)
    sm_v = np.concatenate([r[c]["o_sv"].reshape(NSS, 128, 4, 64) for c in range(8)], 0)
    outs = (y_prompt, y_sample, pr_ssm, pr_conv, pr_k, pr_v, sm_ssm, sm_conv, sm_k, sm_v)
    return tuple(np.ascontiguousarray(o, dtype=np.float32) for o in outs)


def kernel(**inputs):
    if "nc" not in _CACHE:
        _CACHE["nc"] = build_program()
    nc = _CACHE["nc"]
    maps = make_in_maps(inputs)
    res = run_bass_kernel_spmd(nc, maps, core_ids=list(range(8)))
    return assemble(res.results)
```

```python
import os
import numpy as np
from contextlib import ExitStack
import concourse.bass as bass
import concourse.mybir as mybir
from concourse.bass_utils import run_bass_kernel_spmd

F32 = mybir.dt.float32
U32 = mybir.dt.uint32
ALU = mybir.AluOpType
ACT = mybir.ActivationFunctionType
AX = mybir.AxisListType

D = 1024
SEQ = 2048
NPT = 16
NSS = 4
TS = 16
NEG = -30000.0
ILV_MAXL = int(os.environ.get('K_ILVL', '4'))


class Buf:
    __slots__ = ("name", "wr", "rd", "dsem", "dcnt")

    def __init__(self, name):
        self.name = name
        self.wr = []
        self.rd = []
        self.dsem = None
        self.dcnt = 0


class FW:
    def __init__(self, nc, stack):
        self.nc = nc
        self.stack = stack
        self.engs = {"pe": nc.tensor, "dve": nc.vector, "act": nc.scalar,
                     "pool": nc.gpsimd, "sp": nc.sync}
        self.sems = {}
        self.cnt = {}
        self.seen = {k: {} for k in self.engs}
        for k in self.engs:
            self.sems[k] = stack.enter_context(nc.semaphore("s_" + k))
            self.cnt[k] = 0
        self.nbuf = 0
        self.ninst = 0
        self.nwait = 0
        self.allbufs = []
        self.stage = 0
        self.stop_at = int(os.environ.get('K_STAGE', '0'))

    def buf(self, name=None):
        self.nbuf += 1
        b = Buf(name or "b%d" % self.nbuf)
        self.allbufs.append(b)
        return b

    def sb(self, name, shape, dt=F32):
        t = self.stack.enter_context(self.nc.sbuf_tensor(name, list(shape), dt))
        return t, self.buf(name)

    def _dsem(self, b):
        if b.dsem is None:
            key = "ds%d" % len(self.sems)
            b.dsem = key
            self.sems[key] = self.stack.enter_context(self.nc.semaphore(key))
        return b.dsem

    def _wait(self, e, events):
        need = {}
        for (k, v) in events:
            if need.get(k, 0) < v:
                need[k] = v
        seen = self.seen[e]
        for k, v in need.items():
            if e == "pe" and k == "pe":
                continue
            if seen.get(k, 0) >= v:
                continue
            self.engs[e].wait_ge(self.sems[k], v)
            self.nwait += 1
            seen[k] = v

    def _deps(self, e, reads, writes):
        ev = []
        for b in reads:
            ev.extend(b.wr)
        for b in writes:
            ev.extend(b.wr)
            ev.extend(b.rd)
        self._wait(e, ev)

    def _commit(self, event, reads, writes):
        for b in reads:
            b.rd.append(event)
            if len(b.rd) > 48:
                m = {}
                for (k, v) in b.rd:
                    if m.get(k, 0) < v:
                        m[k] = v
                b.rd = list(m.items())
        for b in writes:
            b.wr = [event]
            b.rd = []

    def op(self, e, fn, reads=(), writes=()):
        self._deps(e, reads, writes)
        ins = fn(self.engs[e])
        self.cnt[e] += 1
        ins.then_inc(self.sems[e], 1)
        self._commit((e, self.cnt[e]), reads, writes)
        self.ninst += 1
        return ins

    def dma(self, q, out, in_, reads=(), writes=(), owner=None, fn=None, slow=False):
        self._deps(q, reads, writes)
        key = self._dsem(owner)
        if fn is not None:
            ins = fn(self.engs[q])
        elif slow:
            ins = self.engs[q].dma_start(out=out, in_=in_, allow_slow_non_contiguous=True)
        else:
            ins = self.engs[q].dma_start(out=out, in_=in_)
        owner.dcnt += 16
        ins.then_inc(self.sems[key], 16)
        self._commit((key, owner.dcnt), reads, writes)
        self.ninst += 1
        return ins

    def mark(self, n):
        if self.stop_at and n >= self.stop_at:
            raise Stop()

    def finish(self, bufs, e="sp"):
        ev = []
        for b in bufs:
            ev.extend(b.wr)
            ev.extend(b.rd)
        self._wait(e, ev)


class Stop(Exception):
    pass


def bc(ap, shape):
    return ap.to_broadcast(list(shape))


def build_program(debug=False, nlayers=4, ntiles_p=NPT, nseq_s=NSS):
    nc = bass.Bass("TRN2", target_bir_lowering=False)

    def din(name, shape, dt=F32):
        return nc.dram_tensor(name, list(shape), dt, kind="ExternalInput").ap()

    def dout(name, shape, dt=F32):
        return nc.dram_tensor(name, list(shape), dt, kind="ExternalOutput").ap()

    x_p = din("x_p", [SEQ, D])
    x_s = din("x_s", [NSS * TS, D])
    st_ssm = din("st_ssm", [2, NSS, 2048, 128])
    st_conv = din("st_conv", [2, NSS, 3, 4096])
    ck = din("ck", [NSS, 128, 256])
    cv = din("cv", [NSS, 128, 256])
    norm_mix = din("norm_mix", [4, D])
    norm_ffn = din("norm_ffn", [4, D])
    norm_kv = din("norm_kv", [1, D])
    norm_final = din("norm_final", [1, D])
    m_w_in = din("m_w_in", [2, D, 6176])
    m_conv_w = din("m_conv_w", [2, 128, 32, 4])
    m_conv_b = din("m_conv_b", [2, 128, 32])
    m_dt_bias = din("m_dt_bias", [2, 32])
    m_a_log = din("m_a_log", [2, 32])
    m_d_skip = din("m_d_skip", [2, 32])
    m_norm = din("m_norm", [2, 2048])
    m_w_out = din("m_w_out", [2, 2048, D])
    a_w_kv = din("a_w_kv", [D, 512])
    a_b_kv = din("a_b_kv", [1, 512])
    a_b_kT = din("a_b_kT", [64, 4])
    a_w_q = din("a_w_q", [2, D, D])
    a_b_qT = din("a_b_qT", [2, 64, 16])
    a_sinks = din("a_sinks", [2, 16])
    a_w_o = din("a_w_o", [2, D, D])
    a_b_o = din("a_b_o", [2, D])
    p_w_q = din("p_w_q", [4, D, 2048])
    p_k1T = din("p_k1T", [4, 128, 128])
    p_k2T = din("p_k2T", [4, 128, 128])
    p_u = din("p_u", [4 * 16384, D])
    p_v = din("p_v", [4 * 16384, D])

    y_p = dout("y_p", [SEQ, D])
    y_s = dout("y_s", [NSS * TS, D])
    o_pssm = dout("o_pssm", [2, 2048, 128])
    o_pconv = dout("o_pconv", [2, 3, 4096])
    o_pk = dout("o_pk", [128, 256])
    o_pv = dout("o_pv", [128, 256])
    o_sssm = dout("o_sssm", [2, NSS, 2048, 128])
    o_sconv = dout("o_sconv", [2, NSS, 3, 4096])
    o_sk = dout("o_sk", [NSS, 128, 256])
    o_sv = dout("o_sv", [NSS, 128, 256])

    NTOK = SEQ + NSS * TS
    hck = []
    for i in range(2 * 4):
        if debug:
            hck.append(dout("hck%d" % i, [NTOK, D]))
        else:
            hck.append(nc.dram_tensor("hck%d" % i, [NTOK, D], F32).ap())
    kT_d = nc.dram_tensor("kT_d", [NPT + NSS, 64, 4, 128], F32).ap()
    v_d = nc.dram_tensor("v_d", [NPT + NSS, 128, 256], F32).ap()

    with ExitStack() as st:
        fw = FW(nc, st)
        op = fw.op

        class Tile:
            pass
        tiles = []
        for i in range(ntiles_p):
            t = Tile()
            t.kind, t.idx, t.nt, t.row0 = "p", i, 128, i * 128
            t.chunks = [(0, 64), (64, 64)]
            tiles.append(t)
        for s in range(nseq_s):
            t = Tile()
            t.kind, t.idx, t.nt, t.row0 = "s", s, TS, SEQ + s * TS
            t.chunks = [(0, TS)]
            tiles.append(t)
        for t in tiles:
            t.hb = [fw.buf("h%d_%s%d" % (i, t.kind, t.idx)) for i in range(9)]
            t.kvslot = NPT + t.idx if t.kind == "s" else t.idx
            t.kTd_b = fw.buf()
            t.vd_b = fw.buf()

        speer = Tile()
        speer.kind, speer.idx, speer.nt, speer.row0 = "sp", 0, nseq_s * TS, SEQ
        stiles = [t for t in tiles if t.kind == "s"]

        def hbl(t, ci):
            if t.kind == "sp":
                return [s_.hb[ci] for s_ in stiles]
            return [t.hb[ci]]

        def h_ap(ci, t):
            if ci == 0:
                return x_p[t.row0:t.row0 + t.nt, :] if t.kind == "p" else x_s[t.idx * TS:(t.idx + 1) * TS, :]
            return hck[ci - 1][t.row0:t.row0 + t.nt, :]

        P = []
        for i in range(8):
            P.append(fw.sb("P%d" % i, [128, 2048]))
        xbc_t, xbc_b = fw.sb("xbc", [128, 32, 131])
        halo_t, halo_b = fw.sb("halo", [128, 32, 3])
        xnT_t, xnT_b = fw.sb("xnT", [128, 8, 128])
        ht_t, ht_b = fw.sb("ht", [128, 1024])
        xn_t, xn_b = P[4][0][:, 0:1024], P[4][1]
        W = [fw.sb("W%d" % i, [128, 4096]) for i in range(2)]
        NG = 8
        G = [fw.sb("G%d" % i, [128, 1024]) for i in range(NG)]
        ST_t, ST_b = fw.sb("ST", [128, 2048])
        bo_t, bo_b = ST_t[:, 0:1024], ST_b
        kTr = [fw.sb("kTr%d" % i, [64, 4, 128]) for i in range(2)]
        vr = [fw.sb("vr%d" % i, [128, 256]) for i in range(2)]
        gmix_t, gmix_b = fw.sb("gmix", [128, 1024])
        gffn_t, gffn_b = fw.sb("gffn", [128, 1024])
        gX_t, gX_b = fw.sb("gX", [128, 2048])
        ident_t, ident_b = fw.sb("ident", [128, 128])
        tri_t, tri_b = fw.sb("tri", [64, 64])
        negm_t, negm_b = fw.sb("negm", [64, 64])
        ones_t, ones_b = fw.sb("ones", [64, 128])
        iota_t, iota_b = fw.sb("iota16", [128, 16])
        thr_t, thr_b = fw.sb("thr15", [128, 15])
        cst_t, cst_b = fw.sb("cst", [128, 4])
        lc_t, lc_b = fw.sb("lc", [128, 320])
        amask_t, amask_b = fw.sb("amask", [128, 384])
        sk_t, sk_b = fw.sb("subk", [128, 2, 128])
        sm_t, sm_b = fw.sb("small", [128, 1024])
        htp_t, htp_b = fw.sb("htp", [128, 1024])
        xnp_t, xnp_b = fw.sb("xnp", [128, 1024])
        psm_t, psm_b = fw.sb("psm", [128, 1288])
        eid_t, eid_b = fw.sb("eid", [128, 128], U32)
        idx_t, idx_b = fw.sb("idxu", [128, 384], U32)

        PS = []
        for i in range(8):
            t_ = st.enter_context(nc.psum_tensor("ps%d" % i, [128, 512], F32))
            PS.append((t_, fw.buf("ps%d" % i)))
        ps_i = [0]

        def psum():
            r = PS[ps_i[0] % 8]
            ps_i[0] += 1
            return r

        w_i = [0]

        def wslot():
            r = W[w_i[0] % 2]
            w_i[0] += 1
            return r

        cb_ = cst_b
        op("pool", lambda e: e.memset(cst_t[:, 0:1], 1e-6), writes=[cb_])
        op("pool", lambda e: e.memset(cst_t[:, 1:2], 1.0), writes=[cb_])
        op("pool", lambda e: e.memset(cst_t[:, 2:4], 0.0), writes=[cb_])
        EPS = cst_t[:, 0:1]
        ONE = cst_t[:, 1:2]
        op("pool", lambda e: e.memset(ident_t[:], 0.0), writes=[ident_b])
        op("pool", lambda e: e.affine_select(out=ident_t[:], in_=ident_t[:], pattern=[[-1, 128]],
                                             compare_op=ALU.not_equal, fill=1.0, base=0,
                                             channel_multiplier=1), reads=[ident_b], writes=[ident_b])
        op("pool", lambda e: e.memset(tri_t[:], 1.0), writes=[tri_b])
        op("pool", lambda e: e.affine_select(out=tri_t[:], in_=tri_t[:], pattern=[[1, 64]],
                                             compare_op=ALU.is_ge, fill=0.0, base=0,
                                             channel_multiplier=-1), reads=[tri_b], writes=[tri_b])
        op("pool", lambda e: e.memset(negm_t[:], 0.0), writes=[negm_b])
        op("pool", lambda e: e.affine_select(out=negm_t[:], in_=negm_t[:], pattern=[[1, 64]],
                                             compare_op=ALU.is_ge, fill=NEG, base=0,
                                             channel_multiplier=-1), reads=[negm_b], writes=[negm_b])
        op("pool", lambda e: e.memset(ones_t[:], 1.0), writes=[ones_b])
        op("pool", lambda e: e.iota(iota_t[:], pattern=[[1, 16]], base=0, channel_multiplier=0,
                                    allow_small_or_imprecise_dtypes=True), writes=[iota_b])
        op("pool", lambda e: e.iota(thr_t[:], pattern=[[16, 15]], base=16, channel_multiplier=0,
                                    allow_small_or_imprecise_dtypes=True), writes=[thr_b])
        op("pool", lambda e: e.memset(amask_t[:], 0.0), writes=[amask_b])
        op("pool", lambda e: e.memset(amask_t[0:64, 192:256], NEG), reads=[amask_b], writes=[amask_b])
        op("pool", lambda e: e.memset(amask_t[64:128, 0:64], NEG), reads=[amask_b], writes=[amask_b])
        op("pool", lambda e: e.memset(amask_t[0:64, 256 + 64:256 + 128], NEG), reads=[amask_b], writes=[amask_b])

        def rmsnorm(src_t, src_b, nt, g_ap, g_b, dst_t, dst_b, ss_ap, ss_b):
            op("dve", lambda e: e.memset(ss_ap[:, 0:1], 0.0), writes=[ss_b])
            op("act", lambda e: e.activation(out=dst_t[0:nt, :], in_=src_t[0:nt, :], func=ACT.Square,
                                             accum_out=ss_ap[:, 0:1]), reads=[src_b, ss_b], writes=[dst_b, ss_b])
            op("act", lambda e: e.activation(out=ss_ap[:, 1:2], in_=ss_ap[:, 0:1], func=ACT.Sqrt,
                                             scale=1.0 / D, bias=EPS[0:nt, :]), reads=[ss_b, cst_b], writes=[ss_b])
            op("dve", lambda e: e.reciprocal(out=ss_ap[:, 2:3], in_=ss_ap[:, 1:2]), reads=[ss_b], writes=[ss_b])
            op("dve", lambda e: e.scalar_tensor_tensor(out=dst_t[0:nt, :], in0=src_t[0:nt, :], scalar=ss_ap[:, 2:3],
                                                        in1=g_ap[0:nt, :], op0=ALU.mult, op1=ALU.mult),
               reads=[src_b, ss_b, g_b], writes=[dst_b])

        def to_featmajor(src_t, src_b, nt, ncols, dstT, dstT_b, tok0=0, eng_alt=True):
            nb = ncols // 128
            for b0 in range(0, nb, 4):
                pt, pb = psum()
                nn = min(4, nb - b0)
                for c in range(nn):
                    op("pe", lambda e, c=c: e.transpose(out=pt[:, c * 128:c * 128 + nt],
                                                        in_=src_t[0:nt, (b0 + c) * 128:(b0 + c + 1) * 128],
                                                        identity=ident_t[0:nt, 0:nt]),
                       reads=[src_b, ident_b], writes=[pb])
                src_v = pt[:, 0:nn * 128].rearrange("p (c n) -> p c n", c=nn)[:, :, 0:nt]
                eng = "act" if (b0 // 4) % 2 == 0 else "dve"
                if eng == "act":
                    op("act", lambda e: e.copy(out=dstT[:, b0:b0 + nn, tok0:tok0 + nt], in_=src_v),
                       reads=[pb], writes=[dstT_b])
                else:
                    op("dve", lambda e: e.tensor_copy(out=dstT[:, b0:b0 + nn, tok0:tok0 + nt], in_=src_v),
                       reads=[pb], writes=[dstT_b])

        def load_w(w_ap, nk, c0, ncols, prows=128):
            wt, wb = wslot()
            view = wt[0:prows, 0:nk * ncols].rearrange("p (k n) -> p k n", k=nk)
            src = w_ap[:, c0:c0 + ncols].rearrange("(k p) n -> p k n", p=prows)
            fw.dma("sp", view, src, writes=[wb], owner=wb)
            return view, wb

        def load_bc(dst_ap, dst_b, src_row_ap, nparts=128):
            fw.dma("sp", dst_ap, src_row_ap.partition_broadcast(nparts), writes=[dst_b], owner=dst_b)

        sm = sm_t
        smb = sm_b

        def mamba_tile(l, t, ci):
            nt = t.nt
            first_tile = (t.kind == "s") or (t.idx == 0)
            last_tile = (t.kind == "s") or (t.idx == NPT - 1)
            fw.dma("sp", ht_t[0:nt, :], h_ap(ci, t), reads=[t.hb[ci]], writes=[ht_b], owner=ht_b)
            rmsnorm(ht_t, ht_b, nt, gmix_t, gmix_b, xn_t, xn_b, sm[0:nt, 0:4], smb)
            to_featmajor(xn_t, xn_b, nt, D, xnT_t, xnT_b)
            fw.mark(1)
            yield
            w_in = m_w_in[l]
            if first_tile:
                if t.kind == "p":
                    op("pool", lambda e: e.memset(ST_t[:], 0.0), writes=[ST_b])
                    op("pool", lambda e: e.memset(halo_t[:], 0.0), writes=[halo_b])
                else:
                    for c0 in range(32):
                        src = st_conv[l, t.idx][:, c0 * 128:(c0 + 1) * 128].rearrange("r p -> p r")
                        fw.dma("sp", halo_t[:, c0, :], src, writes=[halo_b], owner=halo_b, slow=True)
                    stg_t, stg_b = P[4]
                    fw.dma("sp", stg_t[:].rearrange("p (m n) -> p m n", m=16),
                           st_ssm[l, t.idx].rearrange("(m p) n -> p m n", p=128), writes=[stg_b], owner=stg_b)
                    to_featmajor(stg_t, stg_b, 128, 2048, ST_t[:].rearrange("p (m n) -> p m n", m=16), ST_b)

            ynT_t, ynT_b = P[7]
            ynT = ynT_t[:].rearrange("p (k n) -> p k n", k=16)
            op("pool", lambda e: e.tensor_copy(out=xbc_t[:, :, 0:3], in_=halo_t[:]), reads=[halo_b], writes=[xbc_b])
            for blk in range(8):
                wv, wb = load_w(w_in, 8, 2048 + blk * 512, 512)
                pt, pb = psum()
                for cc in range(4):
                    for k in range(8):
                        op("pe", lambda e, cc=cc, k=k: e.matmul(pt[:, cc * nt:(cc + 1) * nt], lhsT=wv[:, k, cc * 128:(cc + 1) * 128],
                                                                 rhs=xnT_t[:, k, 0:nt], start=(k == 0), stop=(k == 7)),
                           reads=[wb, xnT_b], writes=[pb])
                    yield
                op("act", lambda e: e.copy(out=xbc_t[:, blk * 4:blk * 4 + 4, 3:3 + nt],
                                           in_=pt[:, 0:4 * nt].rearrange("p (c n) -> p c n", c=4)),
                   reads=[pb], writes=[xbc_b])
            op("pool", lambda e: e.tensor_copy(out=halo_t[:], in_=xbc_t[:, :, nt:nt + 3]), reads=[xbc_b], writes=[halo_b])
            for (q0, Q) in t.chunks:
                xc_t, xc_b = P[0]
                xt_t, xt_b = P[1]
                zs_t, zs_b = P[2]
                ab_t, ab_b = P[3]
                lm_t, lm_b = P[4]
                xd_t, xd_b = P[5]
                bt_t, bt_b = P[6]
                xc = xc_t[:, 0:32 * Q].rearrange("p (c n) -> p c n", c=32)
                abc = ab_t[:, 0:32 * Q].rearrange("p (c n) -> p c n", c=32)
                lm = lm_t[0:Q, 0:32 * Q].rearrange("p (c n) -> p c n", c=32)
                xbe = xbc_t[:, :, q0:q0 + Q + 3]
                fw.mark(2)
                for blk in range(4):
                    wv, wb = load_w(w_in, 8, blk * 512, 512)
                    pt, pb = psum()
                    for k in range(8):
                        op("pe", lambda e, k=k: e.matmul(pt[0:Q, :], lhsT=xnT_t[:, k, q0:q0 + Q], rhs=wv[:, k, :],
                                                         start=(k == 0), stop=(k == 7)), reads=[wb, xnT_b], writes=[pb])
                    op("act", lambda e: e.activation(out=zs_t[0:Q, blk * 512:(blk + 1) * 512], in_=pt[0:Q, :], func=ACT.Silu),
                       reads=[pb], writes=[zs_b])
                    yield
                wv, wb = load_w(w_in, 8, 6144, 32)
                pt, pb = psum()
                for k in range(8):
                    op("pe", lambda e, k=k: e.matmul(pt[0:Q, 0:32], lhsT=xnT_t[:, k, q0:q0 + Q], rhs=wv[:, k, :],
                                                     start=(k == 0), stop=(k == 7)), reads=[wb, xnT_b], writes=[pb])
                s_x = sm[0:Q, 32:64]
                s_ax = sm[0:Q, 64:96]
                s_e = sm[0:Q, 96:128]
                s_dt = sm[0:Q, 128:160]
                s_a = sm[0:Q, 160:192]
                s_ac = sm[0:Q, 192:224]
                dec_bc = sm[:, 224:256]
                R = [smb]
                op("dve", lambda e: e.tensor_tensor(out=s_x, in0=pt[0:Q, 0:32], in1=lc_t[0:Q, 0:32], op=ALU.add),
                   reads=[pb, lc_b], writes=R)
                op("act", lambda e: e.activation(out=s_ax, in_=s_x, func=ACT.Abs), reads=R, writes=R)
                op("act", lambda e: e.activation(out=s_e, in_=s_ax, func=ACT.Exp, scale=-1.0), reads=R, writes=R)
                op("act", lambda e: e.activation(out=s_e, in_=s_e, func=ACT.Ln, bias=ONE[0:Q, :]), reads=R + [cst_b], writes=R)
                op("dve", lambda e: e.tensor_scalar_max(out=s_x, in0=s_x, scalar1=0.0), reads=R, writes=R)
                op("dve", lambda e: e.tensor_tensor(out=s_dt, in0=s_x, in1=s_e, op=ALU.add), reads=R, writes=R)
                op("dve", lambda e: e.tensor_tensor(out=s_a, in0=s_dt, in1=lc_t[0:Q, 32:64], op=ALU.mult), reads=R + [lc_b], writes=R)
                fw.mark(3)
                yield
                tmp = xd_t[:, 0:32 * Q].rearrange("p (c n) -> p c n", c=32)
                cw = lc_t[:, 128:256].rearrange("p (c k) -> p c k", k=4)
                cbias = lc_t[:, 256:288]
                op("dve", lambda e: e.tensor_tensor(out=xc, in0=xbe[:, :, 0:Q], in1=bc(cw[:, :, 0:1], [128, 32, Q]), op=ALU.mult),
                   reads=[xbc_b, lc_b], writes=[xc_b])
                for k in range(1, 4):
                    op("dve", lambda e, k=k: e.tensor_tensor(out=tmp, in0=xbe[:, :, k:k + Q], in1=bc(cw[:, :, k:k + 1], [128, 32, Q]), op=ALU.mult),
                       reads=[xbc_b, lc_b], writes=[xd_b])
                    op("dve", lambda e: e.tensor_tensor(out=xc, in0=xc, in1=tmp, op=ALU.add), reads=[xc_b, xd_b], writes=[xc_b])
                    yield
                op("dve", lambda e: e.tensor_tensor(out=xc, in0=xc, in1=bc(cbias.unsqueeze(2), [128, 32, Q]), op=ALU.add),
                   reads=[xc_b, lc_b], writes=[xc_b])
                op("act", lambda e: e.activation(out=xc, in_=xc, func=ACT.Silu), reads=[xc_b], writes=[xc_b])
                fw.mark(4)
                yield
                for b0 in range(0, 24, 4):
                    pt2, pb2 = psum()
                    for c in range(4):
                        op("pe", lambda e, c=c: e.transpose(out=pt2[0:Q, c * 128:(c + 1) * 128], in_=xc[:, b0 + c, :], identity=ident_t[:]),
                           reads=[xc_b, ident_b], writes=[pb2])
                    if b0 < 16:
                        dst, dstb = xt_t[0:Q, b0 * 128:(b0 + 4) * 128], xt_b
                    else:
                        dst, dstb = bt_t[0:Q, (b0 - 16) * 128:(b0 - 12) * 128], bt_b
                    if (b0 // 4) % 2 == 0:
                        op("act", lambda e: e.copy(out=dst, in_=pt2[0:Q, :]), reads=[pb2], writes=[dstb])
                    else:
                        op("dve", lambda e: e.tensor_copy(out=dst, in_=pt2[0:Q, :]), reads=[pb2], writes=[dstb])
                    yield
                fw.mark(5)
                yield
                pt3, pb3 = psum()
                op("pe", lambda e: e.matmul(pt3[0:Q, 0:32], lhsT=tri_t[0:Q, 0:Q], rhs=s_a, start=True, stop=True),
                   reads=[tri_b] + R, writes=[pb3])
                op("act", lambda e: e.copy(out=s_ac, in_=pt3[0:Q, 0:32]), reads=[pb3], writes=R)
                op("dve", lambda e: e.tensor_tensor(out=lm, in0=bc(s_a.unsqueeze(2), [Q, 32, Q]),
                                                    in1=bc(tri_t[0:Q, 0:Q].unsqueeze(1), [Q, 32, Q]), op=ALU.mult),
                   reads=R + [tri_b], writes=[lm_b])
                ncol = 32 * Q
                for c0 in range(0, ncol, 512):
                    n_ = min(512, ncol - c0)
                    pt4, pb4 = psum()
                    op("pe", lambda e: e.matmul(pt4[:, 0:n_], lhsT=ones_t[0:Q, :], rhs=lm_t[0:Q, c0:c0 + n_], start=True, stop=True),
                       reads=[ones_b, lm_b], writes=[pb4])
                    op("act", lambda e: e.copy(out=ab_t[:, c0:c0 + n_], in_=pt4[:, 0:n_]), reads=[pb4], writes=[ab_b])
                    yield
                fw.mark(6)
                yield
                op("dve", lambda e: e.tensor_tensor(out=lm, in0=abc[0:Q], in1=bc(s_ac.unsqueeze(2), [Q, 32, Q]), op=ALU.subtract),
                   reads=[ab_b] + R, writes=[lm_b])
                op("dve", lambda e: e.tensor_tensor(out=lm, in0=lm, in1=bc(negm_t[0:Q, 0:Q].unsqueeze(1), [Q, 32, Q]), op=ALU.add),
                   reads=[lm_b, negm_b], writes=[lm_b])
                op("act", lambda e: e.activation(out=lm, in_=lm, func=ACT.Exp), reads=[lm_b], writes=[lm_b])
                s_de = sm[0:Q, 288:320]
                op("dve", lambda e: e.tensor_copy(out=s_de, in_=lm[:, :, Q - 1]), reads=[lm_b], writes=R)
                op("act", lambda e: e.activation(out=abc, in_=abc, func=ACT.Exp), reads=[ab_b], writes=[ab_b])
                op("dve", lambda e: e.tensor_copy(out=dec_bc, in_=abc[:, :, Q - 1]), reads=[ab_b], writes=R)
                abc4 = ab_t[:, 0:32 * Q].rearrange("p (g r n) -> p g r n", g=8, r=4)
                op("dve", lambda e: e.tensor_tensor(out=abc4, in0=abc4, in1=bc(xc[:, 24:32, :].unsqueeze(2), [128, 8, 4, Q]), op=ALU.mult),
                   reads=[ab_b, xc_b], writes=[ab_b])
                fw.mark(7)
                yield
                pt5, pb5 = psum()
                for g in range(8):
                    op("pe", lambda e, g=g: e.matmul(pt5[0:Q, g * Q:(g + 1) * Q], lhsT=xc[:, 16 + g, :], rhs=xc[:, 24 + g, :], start=True, stop=True),
                       reads=[xc_b], writes=[pb5])
                lm4 = lm_t[0:Q, 0:32 * Q].rearrange("p (g r n) -> p g r n", g=8, r=4)
                cb4 = pt5[0:Q, 0:8 * Q].rearrange("p (g n) -> p g n", g=8)
                op("dve", lambda e: e.tensor_tensor(out=lm4, in0=lm4, in1=bc(cb4.unsqueeze(2), [Q, 8, 4, Q]), op=ALU.mult),
                   reads=[lm_b, pb5], writes=[lm_b])
                xt3 = xt_t[0:Q, :].rearrange("p (h d) -> p h d", h=32)
                xd3 = xd_t[0:Q, :].rearrange("p (h d) -> p h d", h=32)
                op("dve", lambda e: e.tensor_tensor(out=xd3, in0=xt3, in1=bc(s_dt.unsqueeze(2), [Q, 32, 64]), op=ALU.mult),
                   reads=[xt_b] + R, writes=[xd_b])
                op("dve", lambda e: e.tensor_tensor(out=xt3, in0=xt3, in1=bc(lc_t[0:Q, 64:96].unsqueeze(2), [Q, 32, 64]), op=ALU.mult),
                   reads=[xt_b, lc_b], writes=[xt_b])
                fw.mark(8)
                yield
                for grp in range(4):
                    pt6, pb6 = psum()
                    for hh in range(8):
                        h_ = grp * 8 + hh
                        op("pe", lambda e, h_=h_, hh=hh: e.matmul(pt6[0:Q, hh * 64:(hh + 1) * 64], lhsT=lm[:, h_, :], rhs=xd_t[0:Q, h_ * 64:(h_ + 1) * 64],
                                                                   start=True, stop=False), reads=[lm_b, xd_b], writes=[pb6])
                        op("pe", lambda e, h_=h_, hh=hh: e.matmul(pt6[0:Q, hh * 64:(hh + 1) * 64], lhsT=abc[:, h_, :], rhs=ST_t[:, h_ * 64:(h_ + 1) * 64],
                                                                   start=False, stop=True), reads=[ab_b, ST_b], writes=[pb6])
                    op("dve", lambda e: e.tensor_tensor(out=xt_t[0:Q, grp * 512:(grp + 1) * 512], in0=xt_t[0:Q, grp * 512:(grp + 1) * 512],
                                                        in1=pt6[0:Q, :], op=ALU.add), reads=[xt_b, pb6], writes=[xt_b])
                    yield
                op("dve", lambda e: e.tensor_tensor(out=xd3, in0=xd3, in1=bc(s_de.unsqueeze(2), [Q, 32, 64]), op=ALU.mult),
                   reads=[xd_b] + R, writes=[xd_b])
                fw.mark(9)
                yield
                ST3 = ST_t[:].rearrange("p (h d) -> p h d", h=32)
                for grp in range(4):
                    pt7, pb7 = psum()
                    for hh in range(8):
                        h_ = grp * 8 + hh
                        g = h_ // 4
                        op("pe", lambda e, h_=h_, hh=hh, g=g: e.matmul(pt7[:, hh * 64:(hh + 1) * 64], lhsT=bt_t[0:Q, g * 128:(g + 1) * 128],
                                                                        rhs=xd_t[0:Q, h_ * 64:(h_ + 1) * 64], start=True, stop=True),
                           reads=[bt_b, xd_b], writes=[pb7])
                    sl = ST3[:, grp * 8:(grp + 1) * 8, :]
                    op("dve", lambda e: e.tensor_tensor(out=sl, in0=sl, in1=bc(dec_bc[:, grp * 8:(grp + 1) * 8].unsqueeze(2), [128, 8, 64]), op=ALU.mult),
                       reads=[ST_b] + R, writes=[ST_b])
                    op("dve", lambda e: e.tensor_tensor(out=ST_t[:, grp * 512:(grp + 1) * 512], in0=ST_t[:, grp * 512:(grp + 1) * 512],
                                                        in1=pt7[:, :], op=ALU.add), reads=[ST_b, pb7], writes=[ST_b])
                    yield
                fw.mark(10)
                yield
                y = xt_t[0:Q, :]
                op("dve", lambda e: e.tensor_tensor(out=y, in0=y, in1=zs_t[0:Q, :], op=ALU.mult), reads=[xt_b, zs_b], writes=[xt_b])
                ss = sm[0:Q, 0:4]
                op("dve", lambda e: e.memset(ss[:, 0:1], 0.0), writes=R)
                op("act", lambda e: e.activation(out=zs_t[0:Q, :], in_=y, func=ACT.Square, accum_out=ss[:, 0:1]),
                   reads=[xt_b] + R, writes=[zs_b] + R)
                op("act", lambda e: e.activation(out=ss[:, 1:2], in_=ss[:, 0:1], func=ACT.Sqrt, scale=1.0 / 2048, bias=EPS[0:Q, :]),
                   reads=R + [cst_b], writes=R)
                op("dve", lambda e: e.reciprocal(out=ss[:, 2:3], in_=ss[:, 1:2]), reads=R, writes=R)
                op("dve", lambda e: e.scalar_tensor_tensor(out=y, in0=y, scalar=ss[:, 2:3], in1=gX_t[0:Q, :], op0=ALU.mult, op1=ALU.mult),
                   reads=[xt_b, gX_b] + R, writes=[xt_b])
                to_featmajor(xt_t, xt_b, Q, 2048, ynT, ynT_b, tok0=q0)

            fw.mark(11)
            yield
            for blk in range(4):
                wv, wb = load_w(m_w_out[l], 16, blk * 256, 256)
                pt, pb = psum()
                for k in range(16):
                    op("pe", lambda e, k=k: e.matmul(pt[0:nt, 0:256], lhsT=ynT[:, k, 0:nt], rhs=wv[:, k, :], start=(k == 0), stop=(k == 15)),
                       reads=[wb, ynT_b], writes=[pb])
                op("dve", lambda e: e.tensor_tensor(out=ht_t[0:nt, blk * 256:(blk + 1) * 256], in0=ht_t[0:nt, blk * 256:(blk + 1) * 256],
                                                    in1=pt[0:nt, 0:256], op=ALU.add), reads=[ht_b, pb], writes=[ht_b])
                yield
            fw.dma("sp", h_ap(ci + 1, t), ht_t[0:nt, :], reads=[ht_b], writes=[t.hb[ci + 1]], owner=ht_b)

            fw.mark(12)
            if last_tile:
                stg_t, stg_b = P[4]
                to_featmajor(ST_t, ST_b, 128, 2048, stg_t[:].rearrange("p (m n) -> p m n", m=16), stg_b)
                dst = o_pssm[l] if t.kind == "p" else o_sssm[l, t.idx]
                fw.dma("sp", dst.rearrange("(m p) n -> p m n", p=128), stg_t[:].rearrange("p (m n) -> p m n", m=16),
                       reads=[stg_b], owner=stg_b)
                outs_b.append(stg_b)
                fw.mark(13)
                dstc = o_pconv[l] if t.kind == "p" else o_sconv[l, t.idx]
                for c0 in range(32):
                    dv = dstc[:, c0 * 128:(c0 + 1) * 128].rearrange("r p -> p r")
                    fw.dma("sp", dv, halo_t[:, c0, :], reads=[halo_b], owner=halo_b, slow=True)
                outs_b.append(halo_b)

        def attn_tile(l, t, ci):
            j = l - 2
            nt = t.nt
            fw.dma("sp", ht_t[0:nt, :], h_ap(ci, t), reads=[t.hb[ci]], writes=[ht_b], owner=ht_b)
            xkT_t, xkT_b = P[7]
            xkT = xkT_t[:, 0:1024].rearrange("p (k n) -> p k n", k=8)
            stg_t, stg_b = P[6]
            if t.kind == "p":
                cur = t.idx % 2
                prev = 1 - cur
                blocks = ([(prev, 128)] if t.idx > 0 else []) + [(cur, 128)]
            else:
                prev, cur = 0, 1
                blocks = [(prev, 128), (cur, TS)]
                fw.dma("sp", stg_t[:, 0:256], ck[t.idx], writes=[stg_b], owner=stg_b)
                fw.dma("sp", vr[prev][0][:, :], cv[t.idx], writes=[vr[prev][1]], owner=vr[prev][1])
                if j == 0:
                    fw.dma("sp", o_sk[t.idx][0:128 - TS, :], stg_t[TS:128, 0:256], reads=[stg_b], owner=stg_b)
                    fw.dma("sp", o_sv[t.idx][0:128 - TS, :], vr[prev][0][TS:128, :], reads=[vr[prev][1]], owner=vr[prev][1])
                    outs_b.extend([stg_b, vr[prev][1]])
                pt, pb = psum()
                for kh in range(4):
                    op("pe", lambda e, kh=kh: e.transpose(out=pt[0:64, kh * 128:(kh + 1) * 128], in_=stg_t[:, kh * 64:(kh + 1) * 64], identity=ident_t[:]),
                       reads=[stg_b, ident_b], writes=[pb])
                op("act", lambda e: e.copy(out=kTr[prev][0][:], in_=pt[0:64, :].rearrange("p (k n) -> p k n", k=4)),
                   reads=[pb], writes=[kTr[prev][1]])
            kT_c, kT_cb = kTr[cur]
            yield
            v_c, v_cb = vr[cur]
            if j == 0:
                rmsnorm(ht_t, ht_b, nt, gX_t[:, 0:1024], gX_b, xn_t, xn_b, sm[0:nt, 0:4], smb)
                to_featmajor(xn_t, xn_b, nt, D, xkT, xkT_b)
                wv, wb = load_w(a_w_kv, 8, 0, 512)
                pt, pb = psum()
                for kh in range(4):
                    for k in range(8):
                        op("pe", lambda e, kh=kh, k=k: e.matmul(pt[0:64, kh * nt:(kh + 1) * nt], lhsT=wv[:, k, kh * 64:(kh + 1) * 64], rhs=xkT[:, k, 0:nt],
                                                                 start=(k == 0), stop=(k == 7)), reads=[wb, xkT_b], writes=[pb])
                for kh in range(4):
                    op("act", lambda e, kh=kh: e.activation(out=kT_c[:, kh, 0:nt], in_=pt[0:64, kh * nt:(kh + 1) * nt], func=ACT.Identity,
                                                            bias=lc_t[0:64, 96 + kh:97 + kh]), reads=[pb, lc_b], writes=[kT_cb])
                pt2, pb2 = psum()
                for k in range(8):
                    op("pe", lambda e, k=k: e.matmul(pt2[0:nt, :], lhsT=xkT[:, k, 0:nt], rhs=wv[:, k, :], start=(k == 0), stop=(k == 7)),
                       reads=[wb, xkT_b], writes=[pb2])
                kvt_t, kvt_b = P[5]
                op("dve", lambda e: e.tensor_tensor(out=kvt_t[0:nt, 0:512], in0=pt2[0:nt, :], in1=gX_t[0:nt, 1024:1536], op=ALU.add),
                   reads=[pb2, gX_b], writes=[kvt_b])
                op("pool", lambda e: e.tensor_copy(out=v_c[0:nt, :], in_=kvt_t[0:nt, 256:512]), reads=[kvt_b], writes=[v_cb])
                fw.dma("sp", kT_d[t.kvslot][:, :, 0:nt], kT_c[:, :, 0:nt], reads=[kT_cb], writes=[t.kTd_b], owner=kT_cb)
                fw.dma("sp", v_d[t.kvslot][0:nt, :], v_c[0:nt, :], reads=[v_cb], writes=[t.vd_b], owner=v_cb)
                if t.kind == "p" and t.idx == NPT - 1:
                    fw.dma("sp", o_pk, kvt_t[0:128, 0:256], reads=[kvt_b], owner=kvt_b)
                    fw.dma("sp", o_pv, kvt_t[0:128, 256:512], reads=[kvt_b], owner=kvt_b)
                    outs_b.append(kvt_b)
                if t.kind == "s":
                    fw.dma("sp", o_sk[t.idx][128 - TS:128, :], kvt_t[0:TS, 0:256], reads=[kvt_b], owner=kvt_b)
                    fw.dma("sp", o_sv[t.idx][128 - TS:128, :], kvt_t[0:TS, 256:512], reads=[kvt_b], owner=kvt_b)
                    outs_b.append(kvt_b)
            else:
                fw.dma("sp", kT_c[:, :, 0:nt], kT_d[t.kvslot][:, :, 0:nt], reads=[t.kTd_b], writes=[kT_cb], owner=kT_cb)
                fw.dma("sp", v_c[0:nt, :], v_d[t.kvslot][0:nt, :], reads=[t.vd_b], writes=[v_cb], owner=v_cb)
            yield
            rmsnorm(ht_t, ht_b, nt, gmix_t, gmix_b, xn_t, xn_b, sm[0:nt, 0:4], smb)
            to_featmajor(xn_t, xn_b, nt, D, xnT_t, xnT_b)
            QT_t, QT_b = P[0]
            QT = QT_t[0:64, :].rearrange("p (h n) -> p h n", h=16)
            for blk in range(2):
                wv, wb = load_w(a_w_q[j], 8, blk * 512, 512)
                for h4 in range(2):
                    pt, pb = psum()
                    for hh in range(4):
                        for k in range(8):
                            op("pe", lambda e, hh=hh, k=k: e.matmul(pt[0:64, hh * nt:(hh + 1) * nt], lhsT=wv[:, k, (h4 * 4 + hh) * 64:(h4 * 4 + hh + 1) * 64],
                                                                     rhs=xnT_t[:, k, 0:nt], start=(k == 0), stop=(k == 7)), reads=[wb, xnT_b], writes=[pb])
                    for hh in range(4):
                        hq = blk * 8 + h4 * 4 + hh
                        op("act", lambda e, hh=hh, hq=hq: e.activation(out=QT[:, hq, 0:nt], in_=pt[0:64, hh * nt:(hh + 1) * nt], func=ACT.Identity,
                                                                        scale=0.125, bias=lc_t[0:64, 112 + hq:113 + hq]), reads=[pb, lc_b], writes=[QT_b])
                    yield
            oT_t, oT_b = P[1]
            oT = oT_t[0:64, :].rearrange("p (h n) -> p h n", h=16)
            NK = sum(n for _, n in blocks)
            if t.kind == "p":
                mask = amask_t[0:nt, 0:256] if t.idx > 0 else amask_t[0:nt, 256:384]
            else:
                mask = None
            S_par = [P[2], P[4]]
            PT_par = [P[3], P[5]]
            pst = {}

            def stageA(hq):
                kvh = hq // 4
                S_t, S_b = S_par[hq % 2]
                pt, pb = psum()
                off = 0
                for (slot, n) in blocks:
                    op("pe", lambda e, slot=slot, n=n, off=off: e.matmul(pt[0:nt, off:off + n], lhsT=QT[:, hq, 0:nt], rhs=kTr[slot][0][:, kvh, 0:n],
                                                                          start=True, stop=True), reads=[QT_b, kTr[slot][1]], writes=[pb])
                    off += n
                Ss = S_t[0:nt, 0:NK]
                st_ = sm[0:nt, 8 + 8 * (hq % 2):16 + 8 * (hq % 2)]
                if mask is not None:
                    op("dve", lambda e: e.tensor_tensor(out=Ss, in0=pt[0:nt, 0:NK], in1=mask, op=ALU.add), reads=[pb, amask_b], writes=[S_b])
                else:
                    op("dve", lambda e: e.tensor_copy(out=Ss, in_=pt[0:nt, 0:NK]), reads=[pb], writes=[S_b])
                op("dve", lambda e: e.reduce_max(out=st_[:, 0:1], in_=Ss, axis=AX.X), reads=[S_b], writes=[smb])
                op("dve", lambda e: e.tensor_scalar_mul(out=st_[:, 1:2], in0=st_[:, 0:1], scalar1=-1.0), reads=[smb], writes=[smb])
                op("dve", lambda e: e.memset(st_[:, 2:3], 0.0), writes=[smb])
                op("act", lambda e: e.activation(out=Ss, in_=Ss, func=ACT.Exp, bias=st_[:, 1:2], accum_out=st_[:, 2:3]),
                   reads=[S_b, smb], writes=[S_b, smb])
                op("act", lambda e: e.activation(out=st_[:, 3:4], in_=lc_t[0:nt, 288 + hq:289 + hq], func=ACT.Exp, bias=st_[:, 1:2]),
                   reads=[lc_b, smb], writes=[smb])
                op("dve", lambda e: e.tensor_tensor(out=st_[:, 4:5], in0=st_[:, 2:3], in1=st_[:, 3:4], op=ALU.add), reads=[smb], writes=[smb])
                op("dve", lambda e: e.reciprocal(out=st_[:, 5:6], in_=st_[:, 4:5]), reads=[smb], writes=[smb])
                op("dve", lambda e: e.tensor_scalar_mul(out=Ss, in0=Ss, scalar1=st_[:, 5:6]), reads=[S_b, smb], writes=[S_b])

            def stageB(hq):
                kvh = hq // 4
                S_t, S_b = S_par[hq % 2]
                PT_t, PT_b = PT_par[hq % 2]
                pt2, pb2 = psum()
                off = 0
                for bi, (slot, n) in enumerate(blocks):
                    op("pe", lambda e, n=n, off=off, bi=bi: e.transpose(out=pt2[0:n, bi * 128:bi * 128 + nt],
                                                                        in_=S_t[0:nt, off:off + n],
                                                                        identity=ident_t[0:nt, 0:nt]), reads=[S_b, ident_b], writes=[pb2])
                    off += n
                PTs = PT_t[:, 0:256]
                for bi, (slot, n) in enumerate(blocks):
                    op("act", lambda e, n=n, bi=bi: e.copy(out=PTs[0:n, bi * 128:bi * 128 + nt], in_=pt2[0:n, bi * 128:bi * 128 + nt]),
                       reads=[pb2], writes=[PT_b])
                if hq % 4 == 0:
                    pst["po"], pst["pob"] = psum()
                po, pob = pst["po"], pst["pob"]
                for bi, (slot, n) in enumerate(blocks):
                    op("pe", lambda e, n=n, bi=bi, slot=slot: e.matmul(po[0:64, (hq % 4) * nt:(hq % 4 + 1) * nt], lhsT=vr[slot][0][0:n, kvh * 64:(kvh + 1) * 64],
                                                                        rhs=PTs[0:n, bi * 128:bi * 128 + nt], start=(bi == 0), stop=(bi == len(blocks) - 1)),
                       reads=[vr[slot][1], PT_b], writes=[pob])
                if hq % 4 == 3:
                    op("act", lambda e: e.copy(out=oT[:, hq - 3:hq + 1, 0:nt], in_=po[0:64, 0:4 * nt].rearrange("p (h n) -> p h n", h=4)),
                       reads=[pob], writes=[oT_b])

            stageA(0)
            yield
            for hq in range(16):
                if hq + 1 < 16:
                    stageA(hq + 1)
                    yield
                stageB(hq)
                yield
            op("dve", lambda e: e.tensor_tensor(out=ht_t[0:nt, :], in0=ht_t[0:nt, :], in1=bo_t[0:nt, :], op=ALU.add),
               reads=[ht_b, bo_b], writes=[ht_b])
            wo = a_w_o[j].rearrange("(h d) n -> d h n", d=64)
            for blk in range(4):
                wt, wb = wslot()
                wv = wt[0:64, 0:16 * 256].rearrange("p (h n) -> p h n", h=16)
                fw.dma("sp", wv, wo[:, :, blk * 256:(blk + 1) * 256], writes=[wb], owner=wb)
                pt, pb = psum()
                for hq in range(16):
                    op("pe", lambda e, hq=hq: e.matmul(pt[0:nt, 0:256], lhsT=oT[:, hq, 0:nt], rhs=wv[:, hq, :], start=(hq == 0), stop=(hq == 15)),
                       reads=[wb, oT_b], writes=[pb])
                op("dve", lambda e: e.tensor_tensor(out=ht_t[0:nt, blk * 256:(blk + 1) * 256], in0=ht_t[0:nt, blk * 256:(blk + 1) * 256],
                                                    in1=pt[0:nt, 0:256], op=ALU.add), reads=[ht_b, pb], writes=[ht_b])
                yield
            fw.dma("sp", h_ap(ci + 1, t), ht_t[0:nt, :], reads=[ht_b], writes=[t.hb[ci + 1]], owner=ht_b)

        def peer_a(l, t, ci):
            nt = t.nt
            fw.dma("sp", htp_t[0:nt, :], h_ap(ci, t), reads=hbl(t, ci), writes=[htp_b], owner=htp_b)
            rmsnorm(htp_t, htp_b, nt, gffn_t, gffn_b, xnp_t, xnp_b, sm[0:nt, 0:4], smb)
            to_featmajor(xnp_t, xnp_b, nt, D, xnT_t, xnT_b)
            fw.mark(17)
            qT_t, qT_b = P[0]
            qT = qT_t[:].rearrange("p (c n) -> p c n", c=16)
            S_t, S_b = P[1]
            Sw_t, Sw_b = P[2]
            S3 = S_t[0:nt, :].rearrange("p (c n) -> p c n", c=16)
            Sw3 = Sw_t[0:nt, :].rearrange("p (c n) -> p c n", c=16)
            for blk in range(4):
                wv, wb = load_w(p_w_q[l], 8, blk * 512, 512)
                pt, pb = psum()
                for cc in range(4):
                    for k in range(8):
                        op("pe", lambda e, cc=cc, k=k: e.matmul(pt[:, cc * nt:(cc + 1) * nt], lhsT=wv[:, k, cc * 128:(cc + 1) * 128], rhs=xnT_t[:, k, 0:nt],
                                                                 start=(k == 0), stop=(k == 7)), reads=[wb, xnT_b], writes=[pb])
                op("act", lambda e: e.copy(out=qT[:, blk * 4:blk * 4 + 4, 0:nt], in_=pt[:, 0:4 * nt].rearrange("p (c n) -> p c n", c=4)),
                   reads=[pb], writes=[qT_b])
            fw.mark(18)
            for blk in range(4):
                pt, pb = psum()
                for cc in range(4):
                    c = blk * 4 + cc
                    op("pe", lambda e, cc=cc, c=c: e.matmul(pt[0:nt, cc * 128:(cc + 1) * 128], lhsT=qT[:, c, 0:nt], rhs=sk_t[:, c % 2, :], start=True, stop=True),
                       reads=[qT_b, sk_b], writes=[pb])
                if os.environ.get('K_SKIPC') != '1':
                    op("act", lambda e: e.copy(out=S_t[0:nt, blk * 512:(blk + 1) * 512], in_=pt[0:nt, :]), reads=[pb], writes=[S_b])
                if os.environ.get('K_SKIPC') not in ('1', '2'):
                    op("dve", lambda e: e.tensor_copy(out=Sw_t[0:nt, blk * 512:(blk + 1) * 512], in_=S_t[0:nt, blk * 512:(blk + 1) * 512]), reads=[S_b], writes=[Sw_b])
            fw.mark(20)
            R = [smb]
            vals = sm[0:nt, 256:512].rearrange("p (c k) -> p c k", c=16)
            idxf = sm[0:nt, 512:768].rearrange("p (c k) -> p c k", c=16)
            vs = sm[0:nt, 768:896].rearrange("p (h k) -> p h k", h=8)
            posf = sm[0:nt, 896:1024].rearrange("p (h k) -> p h k", h=8)
            idxu = idx_t[0:nt, 0:256].rearrange("p (c k) -> p c k", c=16)
            posu = idx_t[0:nt, 256:384].rearrange("p (h k) -> p h k", h=8)
            for c in range(16):
                op("dve", lambda e, c=c: e.max(out=vals[:, c, 0:8], in_=Sw3[:, c, :]), reads=[Sw_b], writes=R)
                op("dve", lambda e, c=c: e.match_replace(out=Sw3[:, c, :], in_to_replace=vals[:, c, 0:8], in_values=Sw3[:, c, :], imm_value=-1e30),
                   reads=[Sw_b] + R, writes=[Sw_b])
                op("dve", lambda e, c=c: e.max(out=vals[:, c, 8:16], in_=Sw3[:, c, :]), reads=[Sw_b], writes=R)
                op("dve", lambda e, c=c: e.max_index(out=idxu[:, c, 0:8], in_max=vals[:, c, 0:8], in_values=S3[:, c, :]), reads=[S_b] + R, writes=[idx_b])
                op("dve", lambda e, c=c: e.max_index(out=idxu[:, c, 8:16], in_max=vals[:, c, 8:16], in_values=S3[:, c, :]), reads=[S_b] + R, writes=[idx_b])
            op("dve", lambda e: e.tensor_copy(out=idxf, in_=idxu), reads=[idx_b], writes=R)
            vals4 = sm[0:nt, 256:512].rearrange("p (h s k) -> p h s k", h=8, s=2)
            idxf4 = sm[0:nt, 512:768].rearrange("p (h s k) -> p h s k", h=8, s=2)
            cand_t, cand_b = P[3]
            cw_t, cw_b = P[4]
            T1_t, T1_b = P[5]
            cand = cand_t[0:nt, :].rearrange("p (h a b) -> p h a b", h=8, a=16)
            cand3 = cand_t[0:nt, :].rearrange("p (h n) -> p h n", h=8)
            cw3 = cw_t[0:nt, :].rearrange("p (h n) -> p h n", h=8)
            for h in range(8):
                op("dve", lambda e, h=h: e.tensor_tensor(out=cand[:, h], in0=bc(vals4[:, h, 0, :].unsqueeze(2), [nt, 16, 16]),
                                                         in1=bc(vals4[:, h, 1, :].unsqueeze(1), [nt, 16, 16]), op=ALU.add), reads=R, writes=[cand_b])
            op("dve", lambda e: e.tensor_copy(out=cw_t[0:nt, :], in_=cand_t[0:nt, :]), reads=[cand_b], writes=[cw_b])
            for h in range(8):
                op("dve", lambda e, h=h: e.max(out=vs[:, h, 0:8], in_=cw3[:, h, :]), reads=[cw_b], writes=R)
                op("dve", lambda e, h=h: e.match_replace(out=cw3[:, h, :], in_to_replace=vs[:, h, 0:8], in_values=cw3[:, h, :], imm_value=-1e30),
                   reads=[cw_b] + R, writes=[cw_b])
                op("dve", lambda e, h=h: e.max(out=vs[:, h, 8:16], in_=cw3[:, h, :]), reads=[cw_b], writes=R)
                op("dve", lambda e, h=h: e.max_index(out=posu[:, h, 0:8], in_max=vs[:, h, 0:8], in_values=cand3[:, h, :]), reads=[cand_b] + R, writes=[idx_b])
                op("dve", lambda e, h=h: e.max_index(out=posu[:, h, 8:16], in_max=vs[:, h, 8:16], in_values=cand3[:, h, :]), reads=[cand_b] + R, writes=[idx_b])
            op("dve", lambda e: e.tensor_copy(out=posf, in_=posu), reads=[idx_b], writes=R)
            fw.mark(21)
            T15 = cw_t[0:nt, 0:128 * 15].rearrange("p (s m) -> p s m", m=15)
            posf2 = sm[0:nt, 896:1024]
            af = psm_t[0:nt, 0:128]
            bf = psm_t[0:nt, 128:256]
            e1 = psm_t[0:nt, 256:384]
            e2 = psm_t[0:nt, 384:512]
            gate = psm_t[0:nt, 512:640]
            actv = psm_t[0:nt, 640:768]
            wgt = psm_t[0:nt, 768:896]
            tmp1 = psm_t[0:nt, 896:1024]
            tmp2 = psm_t[0:nt, 1024:1152]
            eidf = psm_t[0:nt, 1152:1280]
            Rb = [psm_b]
            op("dve", lambda e: e.tensor_tensor(out=T15, in0=bc(posf2.unsqueeze(2), [nt, 128, 15]), in1=bc(thr_t[0:nt, :].unsqueeze(1), [nt, 128, 15]), op=ALU.is_ge),
               reads=R + [thr_b], writes=[cw_b])
            op("dve", lambda e: e.reduce_sum(out=af, in_=T15, axis=AX.X), reads=[cw_b], writes=Rb)
            op("dve", lambda e: e.scalar_tensor_tensor(out=bf, in0=af, scalar=-16.0, in1=posf2, op0=ALU.mult, op1=ALU.add), reads=Rb + R, writes=Rb)
            T1 = T1_t[0:nt, :].rearrange("p (h k a) -> p h k a", h=8, k=16)
            for (src, which, dst) in ((af, 0, e1), (bf, 1, e2)):
                src3 = src.rearrange("p (h k) -> p h k", h=8)
                dst3 = dst.rearrange("p (h k) -> p h k", h=8)
                for h in range(8):
                    op("dve", lambda e, h=h, src3=src3: e.tensor_tensor(out=T1[:, h], in0=bc(src3[:, h, :].unsqueeze(2), [nt, 16, 16]),
                                                                        in1=bc(iota_t[0:nt, :].unsqueeze(1), [nt, 16, 16]), op=ALU.is_equal),
                       reads=Rb + [iota_b], writes=[T1_b])
                    op("dve", lambda e, h=h, which=which: e.tensor_tensor(out=T1[:, h], in0=T1[:, h], in1=bc(idxf4[:, h, which, :].unsqueeze(1), [nt, 16, 16]), op=ALU.mult),
                       reads=[T1_b] + R, writes=[T1_b])
                op("dve", lambda e, dst=dst: e.reduce_sum(out=dst, in_=T1_t[0:nt, :].rearrange("p (s a) -> p s a", a=16), axis=AX.X), reads=[T1_b], writes=Rb)
            op("dve", lambda e: e.scalar_tensor_tensor(out=eidf, in0=e1, scalar=128.0, in1=e2, op0=ALU.mult, op1=ALU.add), reads=Rb, writes=Rb)
            op("dve", lambda e: e.tensor_scalar_add(out=eidf, in0=eidf, scalar1=float(l * 16384)), reads=Rb, writes=Rb)
            op("dve", lambda e: e.tensor_copy(out=eid_t[0:nt, :], in_=eidf), reads=Rb, writes=[eid_b])
            fw.mark(22)
            gate3 = gate.rearrange("p (h k) -> p h k", h=8)
            op("dve", lambda e: e.tensor_tensor(out=gate3, in0=vs, in1=bc(vs[:, :, 0:1], [nt, 8, 16]), op=ALU.subtract), reads=R, writes=Rb)
            op("act", lambda e: e.activation(out=gate, in_=gate, func=ACT.Exp), reads=Rb, writes=Rb)
            op("dve", lambda e: e.reduce_sum(out=tmp1[:, 0:8], in_=gate3, axis=AX.X), reads=Rb, writes=Rb)
            op("dve", lambda e: e.reciprocal(out=tmp1[:, 8:16], in_=tmp1[:, 0:8]), reads=Rb, writes=Rb)
            op("dve", lambda e: e.tensor_tensor(out=gate3, in0=gate3, in1=bc(tmp1[:, 8:16].unsqueeze(2), [nt, 8, 16]), op=ALU.mult), reads=Rb, writes=Rb)
            fw.mark(23)
        def peer_b(l, t, ci):
            nt = t.nt
            Rb = [psm_b]
            gate = psm_t[0:nt, 512:640]
            actv = psm_t[0:nt, 640:768]
            wgt = psm_t[0:nt, 768:896]
            tmp1 = psm_t[0:nt, 896:1024]
            g_i = [0]

            def gather(tab, jj):
                gt, gb = G[g_i[0] % NG]
                g_i[0] += 1
                fw.dma("pool", None, None, reads=[eid_b], writes=[gb], owner=gb,
                       fn=lambda e: e.indirect_dma_start(out=gt[0:nt, :], out_offset=None, in_=tab,
                                                         in_offset=bass.IndirectOffsetOnAxis(ap=eid_t[0:nt, jj:jj + 1], axis=0)))
                return gt, gb
            op("dve", lambda e: e.memset(actv, 0.0), writes=Rb)
            for jj in range(128):
                gt, gb = gather(p_u, jj)
                op("dve", lambda e, gt=gt, jj=jj: e.scalar_tensor_tensor(out=gt[0:nt, :], in0=gt[0:nt, :], scalar=1.0, in1=xnp_t[0:nt, :],
                                                                           op0=ALU.mult, op1=ALU.mult, accum_out=actv[:, jj:jj + 1]),
                   reads=[gb, xnp_b], writes=[gb] + Rb)
                yield
            fw.mark(24)
            op("dve", lambda e: e.tensor_tensor(out=tmp1, in0=actv, in1=actv, op=ALU.mult), reads=Rb, writes=Rb)
            op("dve", lambda e: e.tensor_scalar(out=tmp1, in0=tmp1, scalar1=0.044715, scalar2=1.0, op0=ALU.mult, op1=ALU.add), reads=Rb, writes=Rb)
            op("dve", lambda e: e.tensor_tensor(out=tmp1, in0=tmp1, in1=actv, op=ALU.mult), reads=Rb, writes=Rb)
            op("act", lambda e: e.activation(out=tmp1, in_=tmp1, func=ACT.Tanh, scale=0.7978845608028654), reads=Rb, writes=Rb)
            op("dve", lambda e: e.tensor_scalar(out=tmp1, in0=tmp1, scalar1=1.0, scalar2=0.5, op0=ALU.add, op1=ALU.mult), reads=Rb, writes=Rb)
            op("dve", lambda e: e.tensor_tensor(out=tmp1, in0=tmp1, in1=actv, op=ALU.mult), reads=Rb, writes=Rb)
            op("dve", lambda e: e.tensor_tensor(out=wgt, in0=tmp1, in1=gate, op=ALU.mult), reads=Rb, writes=Rb)
            for jj in range(128):
                gt, gb = gather(p_v, jj)
                op("dve", lambda e, gt=gt, jj=jj: e.scalar_tensor_tensor(out=htp_t[0:nt, :], in0=gt[0:nt, :], scalar=wgt[:, jj:jj + 1], in1=htp_t[0:nt, :],
                                                                           op0=ALU.mult, op1=ALU.add), reads=[gb, htp_b] + Rb, writes=[htp_b])
                yield
            fw.dma("sp", h_ap(ci + 1, t), htp_t[0:nt, :], reads=[htp_b], writes=hbl(t, ci + 1), owner=htp_b)
            if l == 3:
                rmsnorm(htp_t, htp_b, nt, gX_t[:, 0:1024], gX_b, xnp_t, xnp_b, psm_t[0:nt, 1280:1284], psm_b)
                dst = y_p[t.row0:t.row0 + nt, :] if t.kind == "p" else (y_s[0:nt, :] if t.kind == "sp" else y_s[t.idx * TS:(t.idx + 1) * TS, :])
                fw.dma("sp", dst, xnp_t[0:nt, :], reads=[xnp_b], owner=xnp_b)
                outs_b.append(xnp_b)

        outs_b = []
        pending = [None]
        RATIO = {(True, 'p'): float(os.environ.get('K_RM', '2.4')), (True, 's'): float(os.environ.get('K_RMS', '3.8')),
                 (False, 'p'): float(os.environ.get('K_RA', '6.1')), (False, 's'): float(os.environ.get('K_RAS', '6.1'))}

        def drain2(g1, g2, r=1.0):
            n1 = n2 = 0
            acc = 0.0
            while g1 is not None or g2 is not None:
                if g1 is not None:
                    try:
                        next(g1)
                        n1 += 1
                    except StopIteration:
                        g1 = None
                    acc += r
                else:
                    acc += 1.0
                while g2 is not None and acc >= 1.0:
                    acc -= 1.0
                    try:
                        next(g2)
                        n2 += 1
                    except StopIteration:
                        g2 = None
                if g2 is None:
                    acc = 0.0
            if os.environ.get('K_CNT'):
                print("drain2 yields mixer=%d peer=%d" % (n1, n2))

        for l in range(nlayers):
            load_bc(gmix_t[:], gmix_b, norm_mix[l])
            load_bc(gffn_t[:], gffn_b, norm_ffn[l])
            fw.dma("sp", sk_t[:, 0, :], p_k1T[l], writes=[sk_b], owner=sk_b)
            fw.dma("sp", sk_t[:, 1, :], p_k2T[l], writes=[sk_b], owner=sk_b)
            if l < 2:
                load_bc(gX_t[:], gX_b, m_norm[l])
                load_bc(lc_t[:, 0:32], lc_b, m_dt_bias[l])
                load_bc(lc_t[:, 32:64], lc_b, m_a_log[l])
                load_bc(lc_t[:, 64:96], lc_b, m_d_skip[l])
                fw.dma("sp", lc_t[:, 128:256], m_conv_w[l].rearrange("p c k -> p (c k)"), writes=[lc_b], owner=lc_b)
                fw.dma("sp", lc_t[:, 256:288], m_conv_b[l], writes=[lc_b], owner=lc_b)
                op("act", lambda e: e.activation(out=lc_t[:, 32:64], in_=lc_t[:, 32:64], func=ACT.Exp), reads=[lc_b], writes=[lc_b])
                op("dve", lambda e: e.tensor_scalar_mul(out=lc_t[:, 32:64], in0=lc_t[:, 32:64], scalar1=-1.0), reads=[lc_b], writes=[lc_b])
            else:
                j = l - 2
                if j == 0:
                    load_bc(gX_t[:, 0:1024], gX_b, norm_kv[0])
                    load_bc(gX_t[:, 1024:1536], gX_b, a_b_kv[0])
                    fw.dma("sp", lc_t[0:64, 96:100], a_b_kT, writes=[lc_b], owner=lc_b)
                load_bc(bo_t[:], bo_b, a_b_o[j])
                fw.dma("sp", lc_t[0:64, 112:128], a_b_qT[j], writes=[lc_b], owner=lc_b)
                op("dve", lambda e: e.tensor_scalar_mul(out=lc_t[0:64, 112:128], in0=lc_t[0:64, 112:128], scalar1=0.125), reads=[lc_b], writes=[lc_b])
                load_bc(lc_t[:, 288:304], lc_b, a_sinks[j])
            if l == 3:
                load_bc(gX_t[:, 0:1024], gX_b, norm_final[0])
            try:
                for t in tiles:
                    ci = 2 * l
                    mix = mamba_tile(l, t, ci) if l < 2 else attn_tile(l, t, ci)
                    drain2(mix, pending[0], RATIO[(l < 2, t.kind)])
                    pending[0] = None
                    pt_ = t
                    if t.kind == "s":
                        if t is not stiles[-1]:
                            continue
                        pt_ = speer
                    if not os.environ.get('K_NOPEER'):
                        peer_a(l, pt_, ci + 1)
                        pending[0] = peer_b(l, pt_, ci + 1)
                        if os.environ.get('K_NOILV') or l >= ILV_MAXL:
                            drain2(pending[0], None)
                            pending[0] = None
            except Stop:
                break
        if pending[0] is not None:
            drain2(pending[0], None)
        fw.finish(fw.allbufs, "sp")
        fw.finish(fw.allbufs, "act")
        print("program: ninst=%d nwait=%d sems=%d" % (fw.ninst, fw.nwait, len(fw.sems)))
    return nc


_CACHE = {}


def make_in_maps(inp):
    f = lambda a: np.ascontiguousarray(np.asarray(a, dtype=np.float32))
    shared = {
        "norm_mix": f(inp["norm_mix"]), "norm_ffn": f(inp["norm_ffn"]),
        "norm_kv": f(inp["norm_kv"]).reshape(1, D), "norm_final": f(inp["norm_final"]).reshape(1, D),
        "m_w_in": f(inp["m_w_in"]),
        "m_conv_w": f(np.asarray(inp["m_conv_w"]).reshape(2, 4, 32, 128).transpose(0, 3, 2, 1)),
        "m_conv_b": f(np.asarray(inp["m_conv_b"]).reshape(2, 32, 128).transpose(0, 2, 1)),
        "m_dt_bias": f(inp["m_dt_bias"]), "m_a_log": f(inp["m_a_log"]), "m_d_skip": f(inp["m_d_skip"]),
        "m_norm": f(inp["m_norm"]), "m_w_out": f(inp["m_w_out"]),
        "a_w_kv": f(inp["a_w_kv"]), "a_b_kv": f(inp["a_b_kv"]).reshape(1, 512),
        "a_b_kT": f(np.asarray(inp["a_b_kv"])[:256].reshape(4, 64).T),
        "a_w_q": f(inp["a_w_q"]),
        "a_b_qT": f(np.asarray(inp["a_b_q"]).reshape(2, 16, 64).transpose(0, 2, 1)),
        "a_sinks": f(inp["a_sinks"]), "a_w_o": f(inp["a_w_o"]), "a_b_o": f(inp["a_b_o"]),
        "p_w_q": f(inp["p_w_q"]),
        "p_k1T": f(np.asarray(inp["p_sub_k1"]).transpose(0, 2, 1)),
        "p_k2T": f(np.asarray(inp["p_sub_k2"]).transpose(0, 2, 1)),
        "p_u": f(inp["p_u"]).reshape(4 * 16384, D), "p_v": f(inp["p_v"]).reshape(4 * 16384, D),
    }
    xp = np.asarray(inp["x_prompt"], dtype=np.float32)
    xs = np.asarray(inp["x_sample"], dtype=np.float32)
    ssm = np.asarray(inp["state_ssm"], dtype=np.float32)
    conv = np.asarray(inp["state_conv"], dtype=np.float32)
    ckw = np.asarray(inp["cache_k_win"], dtype=np.float32)
    cvw = np.asarray(inp["cache_v_win"], dtype=np.float32)
    maps = []
    for c in range(8):
        m = dict(shared)
        m["x_p"] = f(xp[c])
        m["x_s"] = f(xs[4 * c:4 * c + 4].reshape(NSS * TS, D))
        m["st_ssm"] = f(ssm[:, 4 * c:4 * c + 4].reshape(2, NSS, 2048, 128))
        m["st_conv"] = f(conv[:, 4 * c:4 * c + 4])
        m["ck"] = f(ckw[4 * c:4 * c + 4].reshape(NSS, 128, 256))
        m["cv"] = f(cvw[4 * c:4 * c + 4].reshape(NSS, 128, 256))
        maps.append(m)
    return maps


def assemble(results):
    r = results
    y_prompt = np.stack([r[c]["y_p"] for c in range(8)], 0)
    y_sample = np.concatenate([r[c]["y_s"].reshape(NSS, TS, D) for c in range(8)], 0)
    pr_ssm = np.stack([r[c]["o_pssm"].reshape(2, 32, 64, 128) for c in range(8)], 1)
    pr_conv = np.stack([r[c]["o_pconv"] for c in range(8)], 1)
    pr_k = np.stack([r[c]["o_pk"].reshape(128, 4, 64) for c in range(8)], 0)
    pr_v = np.stack([r[c]["o_pv"].reshape(128, 4, 64) for c in range(8)], 0)
    sm_ssm = np.concatenate([r[c]["o_sssm"].reshape(2, NSS, 32, 64, 128) for c in range(8)], 1)
    sm_conv = np.concatenate([r[c]["o_sconv"] for c in range(8)], 1)
    sm_k = np.concatenate([r[c]["o_sk"].reshape(NSS, 128, 4, 64) for c in range(8)], 0)
    sm_v = np.concatenate([r[c]["o_sv"].reshape(NSS, 128, 4, 64) for c in range(8)], 0)
    outs = (y_prompt, y_sample, pr_ssm, pr_conv, pr_k, pr_v, sm_ssm, sm_conv, sm_k, sm_v)
    return tuple(np.ascontiguousarray(o, dtype=np.float32) for o in outs)


def kernel(**inputs):
    if "nc" not in _CACHE:
        _CACHE["nc"] = build_program()
    nc = _CACHE["nc"]
    maps = make_in_maps(inputs)
    res = run_bass_kernel_spmd(nc, maps, core_ids=list(range(8)))
    return assemble(res.results)
```

```python
import os
import numpy as np
from contextlib import ExitStack
import concourse.bass as bass
import concourse.mybir as mybir
from concourse.bass_utils import run_bass_kernel_spmd

F32 = mybir.dt.float32
U32 = mybir.dt.uint32
ALU = mybir.AluOpType
ACT = mybir.ActivationFunctionType
AX = mybir.AxisListType

D = 1024
SEQ = 2048
NPT = 16
NSS = 4
TS = 16
NEG = -30000.0
ILV_MAXL = int(os.environ.get('K_ILVL', '4'))


class Buf:
    __slots__ = ("name", "wr", "rd", "dsem", "dcnt")

    def __init__(self, name):
        self.name = name
        self.wr = []
        self.rd = []
        self.dsem = None
        self.dcnt = 0


class FW:
    def __init__(self, nc, stack):
        self.nc = nc
        self.stack = stack
        self.engs = {"pe": nc.tensor, "dve": nc.vector, "act": nc.scalar,
                     "pool": nc.gpsimd, "sp": nc.sync}
        self.sems = {}
        self.cnt = {}
        self.seen = {k: {} for k in self.engs}
        for k in self.engs:
            self.sems[k] = stack.enter_context(nc.semaphore("s_" + k))
            self.cnt[k] = 0
        self.nbuf = 0
        self.ninst = 0
        self.nwait = 0
        self.allbufs = []
        self.stage = 0
        self.stop_at = int(os.environ.get('K_STAGE', '0'))

    def buf(self, name=None):
        self.nbuf += 1
        b = Buf(name or "b%d" % self.nbuf)
        self.allbufs.append(b)
        return b

    def sb(self, name, shape, dt=F32):
        t = self.stack.enter_context(self.nc.sbuf_tensor(name, list(shape), dt))
        return t, self.buf(name)

    def _dsem(self, b):
        if b.dsem is None:
            key = "ds%d" % len(self.sems)
            b.dsem = key
            self.sems[key] = self.stack.enter_context(self.nc.semaphore(key))
        return b.dsem

    def _wait(self, e, events):
        need = {}
        for (k, v) in events:
            if need.get(k, 0) < v:
                need[k] = v
        seen = self.seen[e]
        for k, v in need.items():
            if e == "pe" and k == "pe":
                continue
            if seen.get(k, 0) >= v:
                continue
            self.engs[e].wait_ge(self.sems[k], v)
            self.nwait += 1
            seen[k] = v

    def _deps(self, e, reads, writes):
        ev = []
        for b in reads:
            ev.extend(b.wr)
        for b in writes:
            ev.extend(b.wr)
            ev.extend(b.rd)
        self._wait(e, ev)

    def _commit(self, event, reads, writes):
        for b in reads:
            b.rd.append(event)
            if len(b.rd) > 48:
                m = {}
                for (k, v) in b.rd:
                    if m.get(k, 0) < v:
                        m[k] = v
                b.rd = list(m.items())
        for b in writes:
            b.wr = [event]
            b.rd = []

    def op(self, e, fn, reads=(), writes=()):
        self._deps(e, reads, writes)
        ins = fn(self.engs[e])
        self.cnt[e] += 1
        ins.then_inc(self.sems[e], 1)
        self._commit((e, self.cnt[e]), reads, writes)
        self.ninst += 1
        return ins

    def dma(self, q, out, in_, reads=(), writes=(), owner=None, fn=None, slow=False):
        self._deps(q, reads, writes)
        key = self._dsem(owner)
        if fn is not None:
            ins = fn(self.engs[q])
        elif slow:
            ins = self.engs[q].dma_start(out=out, in_=in_, allow_slow_non_contiguous=True)
        else:
            ins = self.engs[q].dma_start(out=out, in_=in_)
        owner.dcnt += 16
        ins.then_inc(self.sems[key], 16)
        self._commit((key, owner.dcnt), reads, writes)
        self.ninst += 1
        return ins

    def mark(self, n):
        if self.stop_at and n >= self.stop_at:
            raise Stop()

    def finish(self, bufs, e="sp"):
        ev = []
        for b in bufs:
            ev.extend(b.wr)
            ev.extend(b.rd)
        self._wait(e, ev)


class Stop(Exception):
    pass


def bc(ap, shape):
    return ap.to_broadcast(list(shape))


def build_program(debug=False, nlayers=4, ntiles_p=NPT, nseq_s=NSS):
    nc = bass.Bass("TRN2", target_bir_lowering=False)

    def din(name, shape, dt=F32):
        return nc.dram_tensor(name, list(shape), dt, kind="ExternalInput").ap()

    def dout(name, shape, dt=F32):
        return nc.dram_tensor(name, list(shape), dt, kind="ExternalOutput").ap()

    x_p = din("x_p", [SEQ, D])
    x_s = din("x_s", [NSS * TS, D])
    st_ssm = din("st_ssm", [2, NSS, 2048, 128])
    st_conv = din("st_conv", [2, NSS, 3, 4096])
    ck = din("ck", [NSS, 128, 256])
    cv = din("cv", [NSS, 128, 256])
    norm_mix = din("norm_mix", [4, D])
    norm_ffn = din("norm_ffn", [4, D])
    norm_kv = din("norm_kv", [1, D])
    norm_final = din("norm_final", [1, D])
    m_w_in = din("m_w_in", [2, D, 6176])
    m_conv_w = din("m_conv_w", [2, 128, 32, 4])
    m_conv_b = din("m_conv_b", [2, 128, 32])
    m_dt_bias = din("m_dt_bias", [2, 32])
    m_a_log = din("m_a_log", [2, 32])
    m_d_skip = din("m_d_skip", [2, 32])
    m_norm = din("m_norm", [2, 2048])
    m_w_out = din("m_w_out", [2, 2048, D])
    a_w_kv = din("a_w_kv", [D, 512])
    a_b_kv = din("a_b_kv", [1, 512])
    a_b_kT = din("a_b_kT", [64, 4])
    a_w_q = din("a_w_q", [2, D, D])
    a_b_qT = din("a_b_qT", [2, 64, 16])
    a_sinks = din("a_sinks", [2, 16])
    a_w_o = din("a_w_o", [2, D, D])
    a_b_o = din("a_b_o", [2, D])
    p_w_q = din("p_w_q", [4, D, 2048])
    p_k1T = din("p_k1T", [4, 128, 128])
    p_k2T = din("p_k2T", [4, 128, 128])
    p_u = din("p_u", [4 * 16384, D])
    p_v = din("p_v", [4 * 16384, D])

    y_p = dout("y_p", [SEQ, D])
    y_s = dout("y_s", [NSS * TS, D])
    o_pssm = dout("o_pssm", [2, 2048, 128])
    o_pconv = dout("o_pconv", [2, 3, 4096])
    o_pk = dout("o_pk", [128, 256])
    o_pv = dout("o_pv", [128, 256])
    o_sssm = dout("o_sssm", [2, NSS, 2048, 128])
    o_sconv = dout("o_sconv", [2, NSS, 3, 4096])
    o_sk = dout("o_sk", [NSS, 128, 256])
    o_sv = dout("o_sv", [NSS, 128, 256])

    NTOK = SEQ + NSS * TS
    hck = []
    for i in range(2 * 4):
        if debug:
            hck.append(dout("hck%d" % i, [NTOK, D]))
        else:
            hck.append(nc.dram_tensor("hck%d" % i, [NTOK, D], F32).ap())
    kT_d = nc.dram_tensor("kT_d", [NPT + NSS, 64, 4, 128], F32).ap()
    v_d = nc.dram_tensor("v_d", [NPT + NSS, 128, 256], F32).ap()

    with ExitStack() as st:
        fw = FW(nc, st)
        op = fw.op

        class Tile:
            pass
        tiles = []
        for i in range(ntiles_p):
            t = Tile()
            t.kind, t.idx, t.nt, t.row0 = "p", i, 128, i * 128
            t.chunks = [(0, 64), (64, 64)]
            tiles.append(t)
        for s in range(nseq_s):
            t = Tile()
            t.kind, t.idx, t.nt, t.row0 = "s", s, TS, SEQ + s * TS
            t.chunks = [(0, TS)]
            tiles.append(t)
        for t in tiles:
            t.hb = [fw.buf("h%d_%s%d" % (i, t.kind, t.idx)) for i in range(9)]
            t.kvslot = NPT + t.idx if t.kind == "s" else t.idx
            t.kTd_b = fw.buf()
            t.vd_b = fw.buf()

        speer = Tile()
        speer.kind, speer.idx, speer.nt, speer.row0 = "sp", 0, nseq_s * TS, SEQ
        stiles = [t for t in tiles if t.kind == "s"]

        def hbl(t, ci):
            if t.kind == "sp":
                return [s_.hb[ci] for s_ in stiles]
            return [t.hb[ci]]

        def h_ap(ci, t):
            if ci == 0:
                return x_p[t.row0:t.row0 + t.nt, :] if t.kind == "p" else x_s[t.idx * TS:(t.idx + 1) * TS, :]
            return hck[ci - 1][t.row0:t.row0 + t.nt, :]

        P = []
        for i in range(8):
            P.append(fw.sb("P%d" % i, [128, 2048]))
        xbc_t, xbc_b = fw.sb("xbc", [128, 32, 131])
        halo_t, halo_b = fw.sb("halo", [128, 32, 3])
        xnT_t, xnT_b = fw.sb("xnT", [128, 8, 128])
        ht_t, ht_b = fw.sb("ht", [128, 1024])
        xn_t, xn_b = P[4][0][:, 0:1024], P[4][1]
        W = [fw.sb("W%d" % i, [128, 4096]) for i in range(2)]
        NG = 6
        zsB_t, zsB_b = fw.sb("zsB", [64, 2048])
        G = [fw.sb("G%d" % i, [128, 1024]) for i in range(NG)]
        ST_t, ST_b = fw.sb("ST", [128, 2048])
        bo_t, bo_b = ST_t[:, 0:1024], ST_b
        kTr = [fw.sb("kTr%d" % i, [64, 4, 128]) for i in range(2)]
        vr = [fw.sb("vr%d" % i, [128, 256]) for i in range(2)]
        gmix_t, gmix_b = fw.sb("gmix", [128, 1024])
        gffn_t, gffn_b = fw.sb("gffn", [128, 1024])
        gX_t, gX_b = fw.sb("gX", [128, 2048])
        ident_t, ident_b = fw.sb("ident", [128, 128])
        tri_t, tri_b = fw.sb("tri", [64, 64])
        negm_t, negm_b = fw.sb("negm", [64, 64])
        ones_t, ones_b = fw.sb("ones", [64, 128])
        iota_t, iota_b = fw.sb("iota16", [128, 16])
        thr_t, thr_b = fw.sb("thr15", [128, 15])
        cst_t, cst_b = fw.sb("cst", [128, 4])
        lc_t, lc_b = fw.sb("lc", [128, 320])
        amask_t, amask_b = fw.sb("amask", [128, 384])
        sk_t, sk_b = fw.sb("subk", [128, 2, 128])
        sm_t, sm_b = fw.sb("small", [128, 1024])
        htp_t, htp_b = fw.sb("htp", [128, 1024])
        xnp_t, xnp_b = fw.sb("xnp", [128, 1024])
        psm_t, psm_b = fw.sb("psm", [128, 1288])
        eid_t, eid_b = fw.sb("eid", [128, 128], U32)
        idx_t, idx_b = fw.sb("idxu", [128, 384], U32)

        PS = []
        for i in range(8):
            t_ = st.enter_context(nc.psum_tensor("ps%d" % i, [128, 512], F32))
            PS.append((t_, fw.buf("ps%d" % i)))
        ps_i = [0]

        def psum():
            r = PS[ps_i[0] % 8]
            ps_i[0] += 1
            return r

        w_i = [0]

        def wslot():
            r = W[w_i[0] % 2]
            w_i[0] += 1
            return r

        cb_ = cst_b
        op("pool", lambda e: e.memset(cst_t[:, 0:1], 1e-6), writes=[cb_])
        op("pool", lambda e: e.memset(cst_t[:, 1:2], 1.0), writes=[cb_])
        op("pool", lambda e: e.memset(cst_t[:, 2:4], 0.0), writes=[cb_])
        EPS = cst_t[:, 0:1]
        ONE = cst_t[:, 1:2]
        op("pool", lambda e: e.memset(ident_t[:], 0.0), writes=[ident_b])
        op("pool", lambda e: e.affine_select(out=ident_t[:], in_=ident_t[:], pattern=[[-1, 128]],
                                             compare_op=ALU.not_equal, fill=1.0, base=0,
                                             channel_multiplier=1), reads=[ident_b], writes=[ident_b])
        op("pool", lambda e: e.memset(tri_t[:], 1.0), writes=[tri_b])
        op("pool", lambda e: e.affine_select(out=tri_t[:], in_=tri_t[:], pattern=[[1, 64]],
                                             compare_op=ALU.is_ge, fill=0.0, base=0,
                                             channel_multiplier=-1), reads=[tri_b], writes=[tri_b])
        op("pool", lambda e: e.memset(negm_t[:], 0.0), writes=[negm_b])
        op("pool", lambda e: e.affine_select(out=negm_t[:], in_=negm_t[:], pattern=[[1, 64]],
                                             compare_op=ALU.is_ge, fill=NEG, base=0,
                                             channel_multiplier=-1), reads=[negm_b], writes=[negm_b])
        op("pool", lambda e: e.memset(ones_t[:], 1.0), writes=[ones_b])
        op("pool", lambda e: e.iota(iota_t[:], pattern=[[1, 16]], base=0, channel_multiplier=0,
                                    allow_small_or_imprecise_dtypes=True), writes=[iota_b])
        op("pool", lambda e: e.iota(thr_t[:], pattern=[[16, 15]], base=16, channel_multiplier=0,
                                    allow_small_or_imprecise_dtypes=True), writes=[thr_b])
        op("pool", lambda e: e.memset(amask_t[:], 0.0), writes=[amask_b])
        op("pool", lambda e: e.memset(amask_t[0:64, 192:256], NEG), reads=[amask_b], writes=[amask_b])
        op("pool", lambda e: e.memset(amask_t[64:128, 0:64], NEG), reads=[amask_b], writes=[amask_b])
        op("pool", lambda e: e.memset(amask_t[0:64, 256 + 64:256 + 128], NEG), reads=[amask_b], writes=[amask_b])

        def rmsnorm(src_t, src_b, nt, g_ap, g_b, dst_t, dst_b, ss_ap, ss_b):
            op("dve", lambda e: e.memset(ss_ap[:, 0:1], 0.0), writes=[ss_b])
            op("act", lambda e: e.activation(out=dst_t[0:nt, :], in_=src_t[0:nt, :], func=ACT.Square,
                                             accum_out=ss_ap[:, 0:1]), reads=[src_b, ss_b], writes=[dst_b, ss_b])
            op("act", lambda e: e.activation(out=ss_ap[:, 1:2], in_=ss_ap[:, 0:1], func=ACT.Sqrt,
                                             scale=1.0 / D, bias=EPS[0:nt, :]), reads=[ss_b, cst_b], writes=[ss_b])
            op("dve", lambda e: e.reciprocal(out=ss_ap[:, 2:3], in_=ss_ap[:, 1:2]), reads=[ss_b], writes=[ss_b])
            op("dve", lambda e: e.scalar_tensor_tensor(out=dst_t[0:nt, :], in0=src_t[0:nt, :], scalar=ss_ap[:, 2:3],
                                                        in1=g_ap[0:nt, :], op0=ALU.mult, op1=ALU.mult),
               reads=[src_b, ss_b, g_b], writes=[dst_b])

        def to_featmajor(src_t, src_b, nt, ncols, dstT, dstT_b, tok0=0, eng_alt=True):
            nb = ncols // 128
            for b0 in range(0, nb, 4):
                pt, pb = psum()
                nn = min(4, nb - b0)
                for c in range(nn):
                    op("pe", lambda e, c=c: e.transpose(out=pt[:, c * 128:c * 128 + nt],
                                                        in_=src_t[0:nt, (b0 + c) * 128:(b0 + c + 1) * 128],
                                                        identity=ident_t[0:nt, 0:nt]),
                       reads=[src_b, ident_b], writes=[pb])
                src_v = pt[:, 0:nn * 128].rearrange("p (c n) -> p c n", c=nn)[:, :, 0:nt]
                eng = "act" if (b0 // 4) % 2 == 0 else "dve"
                if eng == "act":
                    op("act", lambda e: e.copy(out=dstT[:, b0:b0 + nn, tok0:tok0 + nt], in_=src_v),
                       reads=[pb], writes=[dstT_b])
                else:
                    op("dve", lambda e: e.tensor_copy(out=dstT[:, b0:b0 + nn, tok0:tok0 + nt], in_=src_v),
                       reads=[pb], writes=[dstT_b])

        def load_w(w_ap, nk, c0, ncols, prows=128):
            wt, wb = wslot()
            view = wt[0:prows, 0:nk * ncols].rearrange("p (k n) -> p k n", k=nk)
            src = w_ap[:, c0:c0 + ncols].rearrange("(k p) n -> p k n", p=prows)
            fw.dma("sp", view, src, writes=[wb], owner=wb)
            return view, wb

        def load_bc(dst_ap, dst_b, src_row_ap, nparts=128):
            fw.dma("sp", dst_ap, src_row_ap.partition_broadcast(nparts), writes=[dst_b], owner=dst_b)

        sm = sm_t
        smb = sm_b

        def mamba_tile(l, t, ci):
            nt = t.nt
            first_tile = (t.kind == "s") or (t.idx == 0)
            last_tile = (t.kind == "s") or (t.idx == NPT - 1)
            fw.dma("sp", ht_t[0:nt, :], h_ap(ci, t), reads=[t.hb[ci]], writes=[ht_b], owner=ht_b)
            rmsnorm(ht_t, ht_b, nt, gmix_t, gmix_b, xn_t, xn_b, sm[0:nt, 0:4], smb)
            to_featmajor(xn_t, xn_b, nt, D, xnT_t, xnT_b)
            fw.mark(1)
            yield
            w_in = m_w_in[l]
            if first_tile:
                if t.kind == "p":
                    op("pool", lambda e: e.memset(ST_t[:], 0.0), writes=[ST_b])
                    op("pool", lambda e: e.memset(halo_t[:], 0.0), writes=[halo_b])
                else:
                    for c0 in range(32):
                        src = st_conv[l, t.idx][:, c0 * 128:(c0 + 1) * 128].rearrange("r p -> p r")
                        fw.dma("sp", halo_t[:, c0, :], src, writes=[halo_b], owner=halo_b, slow=True)
                    stg_t, stg_b = P[4]
                    fw.dma("sp", stg_t[:].rearrange("p (m n) -> p m n", m=16),
                           st_ssm[l, t.idx].rearrange("(m p) n -> p m n", p=128), writes=[stg_b], owner=stg_b)
                    to_featmajor(stg_t, stg_b, 128, 2048, ST_t[:].rearrange("p (m n) -> p m n", m=16), ST_b)

            ynT_t, ynT_b = P[7]
            ynT = ynT_t[:].rearrange("p (k n) -> p k n", k=16)
            op("pool", lambda e: e.tensor_copy(out=xbc_t[:, :, 0:3], in_=halo_t[:]), reads=[halo_b], writes=[xbc_b])
            for blk in range(8):
                wv, wb = load_w(w_in, 8, 2048 + blk * 512, 512)
                pt, pb = psum()
                for cc in range(4):
                    for k in range(8):
                        op("pe", lambda e, cc=cc, k=k: e.matmul(pt[:, cc * nt:(cc + 1) * nt], lhsT=wv[:, k, cc * 128:(cc + 1) * 128],
                                                                 rhs=xnT_t[:, k, 0:nt], start=(k == 0), stop=(k == 7)),
                           reads=[wb, xnT_b], writes=[pb])
                    yield
                op("act", lambda e: e.copy(out=xbc_t[:, blk * 4:blk * 4 + 4, 3:3 + nt],
                                           in_=pt[:, 0:4 * nt].rearrange("p (c n) -> p c n", c=4)),
                   reads=[pb], writes=[xbc_b])
            op("pool", lambda e: e.tensor_copy(out=halo_t[:], in_=xbc_t[:, :, nt:nt + 3]), reads=[xbc_b], writes=[halo_b])
            ZS = [P[2], (zsB_t, zsB_b)]
            for blk in range(4):
                wv, wb = load_w(w_in, 8, blk * 512, 512)
                for ci_, (q0, Q) in enumerate(t.chunks):
                    zt_, zb_ = ZS[ci_]
                    pt, pb = psum()
                    for k in range(8):
                        op("pe", lambda e, k=k, q0=q0, Q=Q, pt=pt: e.matmul(pt[0:Q, :], lhsT=xnT_t[:, k, q0:q0 + Q], rhs=wv[:, k, :],
                                                                            start=(k == 0), stop=(k == 7)), reads=[wb, xnT_b], writes=[pb])
                    op("act", lambda e, Q=Q, zt_=zt_, pt=pt: e.activation(out=zt_[0:Q, blk * 512:(blk + 1) * 512], in_=pt[0:Q, :], func=ACT.Silu),
                       reads=[pb], writes=[zb_])
                    yield
            for ci_, (q0, Q) in enumerate(t.chunks):
                xc_t, xc_b = P[0]
                xt_t, xt_b = P[1]
                zs_t, zs_b = ZS[ci_]
                ab_t, ab_b = P[3]
                lm_t, lm_b = P[4]
                xd_t, xd_b = P[5]
                bt_t, bt_b = P[6]
                xc = xc_t[:, 0:32 * Q].rearrange("p (c n) -> p c n", c=32)
                abc = ab_t[:, 0:32 * Q].rearrange("p (c n) -> p c n", c=32)
                lm = lm_t[0:Q, 0:32 * Q].rearrange("p (c n) -> p c n", c=32)
                xbe = xbc_t[:, :, q0:q0 + Q + 3]
                fw.mark(2)
                wv, wb = load_w(w_in, 8, 6144, 32)
                pt, pb = psum()
                for k in range(8):
                    op("pe", lambda e, k=k: e.matmul(pt[0:Q, 0:32], lhsT=xnT_t[:, k, q0:q0 + Q], rhs=wv[:, k, :],
                                                     start=(k == 0), stop=(k == 7)), reads=[wb, xnT_b], writes=[pb])
                s_x = sm[0:Q, 32:64]
                s_ax = sm[0:Q, 64:96]
                s_e = sm[0:Q, 96:128]
                s_dt = sm[0:Q, 128:160]
                s_a = sm[0:Q, 160:192]
                s_ac = sm[0:Q, 192:224]
                dec_bc = sm[:, 224:256]
                R = [smb]
                op("dve", lambda e: e.tensor_tensor(out=s_x, in0=pt[0:Q, 0:32], in1=lc_t[0:Q, 0:32], op=ALU.add),
                   reads=[pb, lc_b], writes=R)
                op("act", lambda e: e.activation(out=s_ax, in_=s_x, func=ACT.Abs), reads=R, writes=R)
                op("act", lambda e: e.activation(out=s_e, in_=s_ax, func=ACT.Exp, scale=-1.0), reads=R, writes=R)
                op("act", lambda e: e.activation(out=s_e, in_=s_e, func=ACT.Ln, bias=ONE[0:Q, :]), reads=R + [cst_b], writes=R)
                op("dve", lambda e: e.tensor_scalar_max(out=s_x, in0=s_x, scalar1=0.0), reads=R, writes=R)
                op("dve", lambda e: e.tensor_tensor(out=s_dt, in0=s_x, in1=s_e, op=ALU.add), reads=R, writes=R)
                op("dve", lambda e: e.tensor_tensor(out=s_a, in0=s_dt, in1=lc_t[0:Q, 32:64], op=ALU.mult), reads=R + [lc_b], writes=R)
                fw.mark(3)
                yield
                tmp = xd_t[:, 0:32 * Q].rearrange("p (c n) -> p c n", c=32)
                cw = lc_t[:, 128:256].rearrange("p (c k) -> p c k", k=4)
                cbias = lc_t[:, 256:288]
                op("dve", lambda e: e.tensor_tensor(out=xc, in0=xbe[:, :, 0:Q], in1=bc(cw[:, :, 0:1], [128, 32, Q]), op=ALU.mult),
                   reads=[xbc_b, lc_b], writes=[xc_b])
                for k in range(1, 4):
                    op("dve", lambda e, k=k: e.tensor_tensor(out=tmp, in0=xbe[:, :, k:k + Q], in1=bc(cw[:, :, k:k + 1], [128, 32, Q]), op=ALU.mult),
                       reads=[xbc_b, lc_b], writes=[xd_b])
                    op("dve", lambda e: e.tensor_tensor(out=xc, in0=xc, in1=tmp, op=ALU.add), reads=[xc_b, xd_b], writes=[xc_b])
                    yield
                op("dve", lambda e: e.tensor_tensor(out=xc, in0=xc, in1=bc(cbias.unsqueeze(2), [128, 32, Q]), op=ALU.add),
                   reads=[xc_b, lc_b], writes=[xc_b])
                op("act", lambda e: e.activation(out=xc, in_=xc, func=ACT.Silu), reads=[xc_b], writes=[xc_b])
                fw.mark(4)
                yield
                for b0 in range(0, 24, 4):
                    pt2, pb2 = psum()
                    for c in range(4):
                        op("pe", lambda e, c=c: e.transpose(out=pt2[0:Q, c * 128:(c + 1) * 128], in_=xc[:, b0 + c, :], identity=ident_t[:]),
                           reads=[xc_b, ident_b], writes=[pb2])
                    if b0 < 16:
                        dst, dstb = xt_t[0:Q, b0 * 128:(b0 + 4) * 128], xt_b
                    else:
                        dst, dstb = bt_t[0:Q, (b0 - 16) * 128:(b0 - 12) * 128], bt_b
                    if (b0 // 4) % 2 == 0:
                        op("act", lambda e: e.copy(out=dst, in_=pt2[0:Q, :]), reads=[pb2], writes=[dstb])
                    else:
                        op("dve", lambda e: e.tensor_copy(out=dst, in_=pt2[0:Q, :]), reads=[pb2], writes=[dstb])
                    yield
                fw.mark(5)
                yield
                pt3, pb3 = psum()
                op("pe", lambda e: e.matmul(pt3[0:Q, 0:32], lhsT=tri_t[0:Q, 0:Q], rhs=s_a, start=True, stop=True),
                   reads=[tri_b] + R, writes=[pb3])
                op("act", lambda e: e.copy(out=s_ac, in_=pt3[0:Q, 0:32]), reads=[pb3], writes=R)
                op("dve", lambda e: e.tensor_tensor(out=lm, in0=bc(s_a.unsqueeze(2), [Q, 32, Q]),
                                                    in1=bc(tri_t[0:Q, 0:Q].unsqueeze(1), [Q, 32, Q]), op=ALU.mult),
                   reads=R + [tri_b], writes=[lm_b])
                ncol = 32 * Q
                for c0 in range(0, ncol, 512):
                    n_ = min(512, ncol - c0)
                    pt4, pb4 = psum()
                    op("pe", lambda e: e.matmul(pt4[:, 0:n_], lhsT=ones_t[0:Q, :], rhs=lm_t[0:Q, c0:c0 + n_], start=True, stop=True),
                       reads=[ones_b, lm_b], writes=[pb4])
                    op("act", lambda e: e.copy(out=ab_t[:, c0:c0 + n_], in_=pt4[:, 0:n_]), reads=[pb4], writes=[ab_b])
                    yield
                fw.mark(6)
                yield
                op("dve", lambda e: e.tensor_tensor(out=lm, in0=abc[0:Q], in1=bc(s_ac.unsqueeze(2), [Q, 32, Q]), op=ALU.subtract),
                   reads=[ab_b] + R, writes=[lm_b])
                op("dve", lambda e: e.tensor_tensor(out=lm, in0=lm, in1=bc(negm_t[0:Q, 0:Q].unsqueeze(1), [Q, 32, Q]), op=ALU.add),
                   reads=[lm_b, negm_b], writes=[lm_b])
                op("act", lambda e: e.activation(out=lm, in_=lm, func=ACT.Exp), reads=[lm_b], writes=[lm_b])
                s_de = sm[0:Q, 288:320]
                op("dve", lambda e: e.tensor_copy(out=s_de, in_=lm[:, :, Q - 1]), reads=[lm_b], writes=R)
                op("act", lambda e: e.activation(out=abc, in_=abc, func=ACT.Exp), reads=[ab_b], writes=[ab_b])
                op("dve", lambda e: e.tensor_copy(out=dec_bc, in_=abc[:, :, Q - 1]), reads=[ab_b], writes=R)
                abc4 = ab_t[:, 0:32 * Q].rearrange("p (g r n) -> p g r n", g=8, r=4)
                op("dve", lambda e: e.tensor_tensor(out=abc4, in0=abc4, in1=bc(xc[:, 24:32, :].unsqueeze(2), [128, 8, 4, Q]), op=ALU.mult),
                   reads=[ab_b, xc_b], writes=[ab_b])
                fw.mark(7)
                yield
                pt5, pb5 = psum()
                for g in range(8):
                    op("pe", lambda e, g=g: e.matmul(pt5[0:Q, g * Q:(g + 1) * Q], lhsT=xc[:, 16 + g, :], rhs=xc[:, 24 + g, :], start=True, stop=True),
                       reads=[xc_b], writes=[pb5])
                lm4 = lm_t[0:Q, 0:32 * Q].rearrange("p (g r n) -> p g r n", g=8, r=4)
                cb4 = pt5[0:Q, 0:8 * Q].rearrange("p (g n) -> p g n", g=8)
                op("dve", lambda e: e.tensor_tensor(out=lm4, in0=lm4, in1=bc(cb4.unsqueeze(2), [Q, 8, 4, Q]), op=ALU.mult),
                   reads=[lm_b, pb5], writes=[lm_b])
                xt3 = xt_t[0:Q, :].rearrange("p (h d) -> p h d", h=32)
                xd3 = xd_t[0:Q, :].rearrange("p (h d) -> p h d", h=32)
                op("dve", lambda e: e.tensor_tensor(out=xd3, in0=xt3, in1=bc(s_dt.unsqueeze(2), [Q, 32, 64]), op=ALU.mult),
                   reads=[xt_b] + R, writes=[xd_b])
                op("dve", lambda e: e.tensor_tensor(out=xt3, in0=xt3, in1=bc(lc_t[0:Q, 64:96].unsqueeze(2), [Q, 32, 64]), op=ALU.mult),
                   reads=[xt_b, lc_b], writes=[xt_b])
                fw.mark(8)
                yield
                for grp in range(4):
                    pt6, pb6 = psum()
                    for hh in range(8):
                        h_ = grp * 8 + hh
                        op("pe", lambda e, h_=h_, hh=hh: e.matmul(pt6[0:Q, hh * 64:(hh + 1) * 64], lhsT=lm[:, h_, :], rhs=xd_t[0:Q, h_ * 64:(h_ + 1) * 64],
                                                                   start=True, stop=False), reads=[lm_b, xd_b], writes=[pb6])
                        op("pe", lambda e, h_=h_, hh=hh: e.matmul(pt6[0:Q, hh * 64:(hh + 1) * 64], lhsT=abc[:, h_, :], rhs=ST_t[:, h_ * 64:(h_ + 1) * 64],
                                                                   start=False, stop=True), reads=[ab_b, ST_b], writes=[pb6])
                    op("dve", lambda e: e.tensor_tensor(out=xt_t[0:Q, grp * 512:(grp + 1) * 512], in0=xt_t[0:Q, grp * 512:(grp + 1) * 512],
                                                        in1=pt6[0:Q, :], op=ALU.add), reads=[xt_b, pb6], writes=[xt_b])
                    yield
                op("dve", lambda e: e.tensor_tensor(out=xd3, in0=xd3, in1=bc(s_de.unsqueeze(2), [Q, 32, 64]), op=ALU.mult),
                   reads=[xd_b] + R, writes=[xd_b])
                fw.mark(9)
                yield
                ST3 = ST_t[:].rearrange("p (h d) -> p h d", h=32)
                for grp in range(4):
                    pt7, pb7 = psum()
                    for hh in range(8):
                        h_ = grp * 8 + hh
                        g = h_ // 4
                        op("pe", lambda e, h_=h_, hh=hh, g=g: e.matmul(pt7[:, hh * 64:(hh + 1) * 64], lhsT=bt_t[0:Q, g * 128:(g + 1) * 128],
                                                                        rhs=xd_t[0:Q, h_ * 64:(h_ + 1) * 64], start=True, stop=True),
                           reads=[bt_b, xd_b], writes=[pb7])
                    sl = ST3[:, grp * 8:(grp + 1) * 8, :]
                    op("dve", lambda e: e.tensor_tensor(out=sl, in0=sl, in1=bc(dec_bc[:, grp * 8:(grp + 1) * 8].unsqueeze(2), [128, 8, 64]), op=ALU.mult),
                       reads=[ST_b] + R, writes=[ST_b])
                    op("dve", lambda e: e.tensor_tensor(out=ST_t[:, grp * 512:(grp + 1) * 512], in0=ST_t[:, grp * 512:(grp + 1) * 512],
                                                        in1=pt7[:, :], op=ALU.add), reads=[ST_b, pb7], writes=[ST_b])
                    yield
                fw.mark(10)
                yield
                y = xt_t[0:Q, :]
                op("dve", lambda e: e.tensor_tensor(out=y, in0=y, in1=zs_t[0:Q, :], op=ALU.mult), reads=[xt_b, zs_b], writes=[xt_b])
                ss = sm[0:Q, 0:4]
                op("dve", lambda e: e.memset(ss[:, 0:1], 0.0), writes=R)
                op("act", lambda e: e.activation(out=zs_t[0:Q, :], in_=y, func=ACT.Square, accum_out=ss[:, 0:1]),
                   reads=[xt_b] + R, writes=[zs_b] + R)
                op("act", lambda e: e.activation(out=ss[:, 1:2], in_=ss[:, 0:1], func=ACT.Sqrt, scale=1.0 / 2048, bias=EPS[0:Q, :]),
                   reads=R + [cst_b], writes=R)
                op("dve", lambda e: e.reciprocal(out=ss[:, 2:3], in_=ss[:, 1:2]), reads=R, writes=R)
                op("dve", lambda e: e.scalar_tensor_tensor(out=y, in0=y, scalar=ss[:, 2:3], in1=gX_t[0:Q, :], op0=ALU.mult, op1=ALU.mult),
                   reads=[xt_b, gX_b] + R, writes=[xt_b])
                to_featmajor(xt_t, xt_b, Q, 2048, ynT, ynT_b, tok0=q0)

            fw.mark(11)
            yield
            for blk in range(4):
                wv, wb = load_w(m_w_out[l], 16, blk * 256, 256)
                pt, pb = psum()
                for k in range(16):
                    op("pe", lambda e, k=k: e.matmul(pt[0:nt, 0:256], lhsT=ynT[:, k, 0:nt], rhs=wv[:, k, :], start=(k == 0), stop=(k == 15)),
                       reads=[wb, ynT_b], writes=[pb])
                op("dve", lambda e: e.tensor_tensor(out=ht_t[0:nt, blk * 256:(blk + 1) * 256], in0=ht_t[0:nt, blk * 256:(blk + 1) * 256],
                                                    in1=pt[0:nt, 0:256], op=ALU.add), reads=[ht_b, pb], writes=[ht_b])
                yield
            fw.dma("sp", h_ap(ci + 1, t), ht_t[0:nt, :], reads=[ht_b], writes=[t.hb[ci + 1]], owner=ht_b)

            fw.mark(12)
            if last_tile:
                stg_t, stg_b = P[4]
                to_featmajor(ST_t, ST_b, 128, 2048, stg_t[:].rearrange("p (m n) -> p m n", m=16), stg_b)
                dst = o_pssm[l] if t.kind == "p" else o_sssm[l, t.idx]
                fw.dma("sp", dst.rearrange("(m p) n -> p m n", p=128), stg_t[:].rearrange("p (m n) -> p m n", m=16),
                       reads=[stg_b], owner=stg_b)
                outs_b.append(stg_b)
                fw.mark(13)
                dstc = o_pconv[l] if t.kind == "p" else o_sconv[l, t.idx]
                for c0 in range(32):
                    dv = dstc[:, c0 * 128:(c0 + 1) * 128].rearrange("r p -> p r")
                    fw.dma("sp", dv, halo_t[:, c0, :], reads=[halo_b], owner=halo_b, slow=True)
                outs_b.append(halo_b)

        def attn_tile(l, t, ci):
            j = l - 2
            nt = t.nt
            fw.dma("sp", ht_t[0:nt, :], h_ap(ci, t), reads=[t.hb[ci]], writes=[ht_b], owner=ht_b)
            xkT_t, xkT_b = P[7]
            xkT = xkT_t[:, 0:1024].rearrange("p (k n) -> p k n", k=8)
            stg_t, stg_b = P[6]
            if t.kind == "p":
                cur = t.idx % 2
                prev = 1 - cur
                blocks = ([(prev, 128)] if t.idx > 0 else []) + [(cur, 128)]
            else:
                prev, cur = 0, 1
                blocks = [(prev, 128), (cur, TS)]
                fw.dma("sp", stg_t[:, 0:256], ck[t.idx], writes=[stg_b], owner=stg_b)
                fw.dma("sp", vr[prev][0][:, :], cv[t.idx], writes=[vr[prev][1]], owner=vr[prev][1])
                if j == 0:
                    fw.dma("sp", o_sk[t.idx][0:128 - TS, :], stg_t[TS:128, 0:256], reads=[stg_b], owner=stg_b)
                    fw.dma("sp", o_sv[t.idx][0:128 - TS, :], vr[prev][0][TS:128, :], reads=[vr[prev][1]], owner=vr[prev][1])
                    outs_b.extend([stg_b, vr[prev][1]])
                pt, pb = psum()
                for kh in range(4):
                    op("pe", lambda e, kh=kh: e.transpose(out=pt[0:64, kh * 128:(kh + 1) * 128], in_=stg_t[:, kh * 64:(kh + 1) * 64], identity=ident_t[:]),
                       reads=[stg_b, ident_b], writes=[pb])
                op("act", lambda e: e.copy(out=kTr[prev][0][:], in_=pt[0:64, :].rearrange("p (k n) -> p k n", k=4)),
                   reads=[pb], writes=[kTr[prev][1]])
            kT_c, kT_cb = kTr[cur]
            yield
            v_c, v_cb = vr[cur]
            if j == 0:
                rmsnorm(ht_t, ht_b, nt, gX_t[:, 0:1024], gX_b, xn_t, xn_b, sm[0:nt, 0:4], smb)
                to_featmajor(xn_t, xn_b, nt, D, xkT, xkT_b)
                wv, wb = load_w(a_w_kv, 8, 0, 512)
                pt, pb = psum()
                for kh in range(4):
                    for k in range(8):
                        op("pe", lambda e, kh=kh, k=k: e.matmul(pt[0:64, kh * nt:(kh + 1) * nt], lhsT=wv[:, k, kh * 64:(kh + 1) * 64], rhs=xkT[:, k, 0:nt],
                                                                 start=(k == 0), stop=(k == 7)), reads=[wb, xkT_b], writes=[pb])
                for kh in range(4):
                    op("act", lambda e, kh=kh: e.activation(out=kT_c[:, kh, 0:nt], in_=pt[0:64, kh * nt:(kh + 1) * nt], func=ACT.Identity,
                                                            bias=lc_t[0:64, 96 + kh:97 + kh]), reads=[pb, lc_b], writes=[kT_cb])
                pt2, pb2 = psum()
                for k in range(8):
                    op("pe", lambda e, k=k: e.matmul(pt2[0:nt, :], lhsT=xkT[:, k, 0:nt], rhs=wv[:, k, :], start=(k == 0), stop=(k == 7)),
                       reads=[wb, xkT_b], writes=[pb2])
                kvt_t, kvt_b = P[5]
                op("dve", lambda e: e.tensor_tensor(out=kvt_t[0:nt, 0:512], in0=pt2[0:nt, :], in1=gX_t[0:nt, 1024:1536], op=ALU.add),
                   reads=[pb2, gX_b], writes=[kvt_b])
                op("pool", lambda e: e.tensor_copy(out=v_c[0:nt, :], in_=kvt_t[0:nt, 256:512]), reads=[kvt_b], writes=[v_cb])
                fw.dma("sp", kT_d[t.kvslot][:, :, 0:nt], kT_c[:, :, 0:nt], reads=[kT_cb], writes=[t.kTd_b], owner=kT_cb)
                fw.dma("sp", v_d[t.kvslot][0:nt, :], v_c[0:nt, :], reads=[v_cb], writes=[t.vd_b], owner=v_cb)
                if t.kind == "p" and t.idx == NPT - 1:
                    fw.dma("sp", o_pk, kvt_t[0:128, 0:256], reads=[kvt_b], owner=kvt_b)
                    fw.dma("sp", o_pv, kvt_t[0:128, 256:512], reads=[kvt_b], owner=kvt_b)
                    outs_b.append(kvt_b)
                if t.kind == "s":
                    fw.dma("sp", o_sk[t.idx][128 - TS:128, :], kvt_t[0:TS, 0:256], reads=[kvt_b], owner=kvt_b)
                    fw.dma("sp", o_sv[t.idx][128 - TS:128, :], kvt_t[0:TS, 256:512], reads=[kvt_b], owner=kvt_b)
                    outs_b.append(kvt_b)
            else:
                fw.dma("sp", kT_c[:, :, 0:nt], kT_d[t.kvslot][:, :, 0:nt], reads=[t.kTd_b], writes=[kT_cb], owner=kT_cb)
                fw.dma("sp", v_c[0:nt, :], v_d[t.kvslot][0:nt, :], reads=[t.vd_b], writes=[v_cb], owner=v_cb)
            yield
            rmsnorm(ht_t, ht_b, nt, gmix_t, gmix_b, xn_t, xn_b, sm[0:nt, 0:4], smb)
            to_featmajor(xn_t, xn_b, nt, D, xnT_t, xnT_b)
            QT_t, QT_b = P[0]
            QT = QT_t[0:64, :].rearrange("p (h n) -> p h n", h=16)
            for blk in range(2):
                wv, wb = load_w(a_w_q[j], 8, blk * 512, 512)
                for h4 in range(2):
                    pt, pb = psum()
                    for hh in range(4):
                        for k in range(8):
                            op("pe", lambda e, hh=hh, k=k: e.matmul(pt[0:64, hh * nt:(hh + 1) * nt], lhsT=wv[:, k, (h4 * 4 + hh) * 64:(h4 * 4 + hh + 1) * 64],
                                                                     rhs=xnT_t[:, k, 0:nt], start=(k == 0), stop=(k == 7)), reads=[wb, xnT_b], writes=[pb])
                    for hh in range(4):
                        hq = blk * 8 + h4 * 4 + hh
                        op("act", lambda e, hh=hh, hq=hq: e.activation(out=QT[:, hq, 0:nt], in_=pt[0:64, hh * nt:(hh + 1) * nt], func=ACT.Identity,
                                                                        scale=0.125, bias=lc_t[0:64, 112 + hq:113 + hq]), reads=[pb, lc_b], writes=[QT_b])
                    yield
            oT_t, oT_b = P[1]
            oT = oT_t[0:64, :].rearrange("p (h n) -> p h n", h=16)
            NK = sum(n for _, n in blocks)
            if t.kind == "p":
                mask = amask_t[0:nt, 0:256] if t.idx > 0 else amask_t[0:nt, 256:384]
            else:
                mask = None
            S_par = [P[2], P[4]]
            PT_par = [P[3], P[5]]
            pst = {}

            def stageA(hq):
                kvh = hq // 4
                S_t, S_b = S_par[hq % 2]
                pt, pb = psum()
                off = 0
                for (slot, n) in blocks:
                    op("pe", lambda e, slot=slot, n=n, off=off: e.matmul(pt[0:nt, off:off + n], lhsT=QT[:, hq, 0:nt], rhs=kTr[slot][0][:, kvh, 0:n],
                                                                          start=True, stop=True), reads=[QT_b, kTr[slot][1]], writes=[pb])
                    off += n
                Ss = S_t[0:nt, 0:NK]
                st_ = sm[0:nt, 8 + 8 * (hq % 2):16 + 8 * (hq % 2)]
                if mask is not None:
                    op("dve", lambda e: e.tensor_tensor(out=Ss, in0=pt[0:nt, 0:NK], in1=mask, op=ALU.add), reads=[pb, amask_b], writes=[S_b])
                else:
                    op("dve", lambda e: e.tensor_copy(out=Ss, in_=pt[0:nt, 0:NK]), reads=[pb], writes=[S_b])
                op("dve", lambda e: e.reduce_max(out=st_[:, 0:1], in_=Ss, axis=AX.X), reads=[S_b], writes=[smb])
                op("dve", lambda e: e.tensor_scalar_mul(out=st_[:, 1:2], in0=st_[:, 0:1], scalar1=-1.0), reads=[smb], writes=[smb])
                op("dve", lambda e: e.memset(st_[:, 2:3], 0.0), writes=[smb])
                op("act", lambda e: e.activation(out=Ss, in_=Ss, func=ACT.Exp, bias=st_[:, 1:2], accum_out=st_[:, 2:3]),
                   reads=[S_b, smb], writes=[S_b, smb])
                op("act", lambda e: e.activation(out=st_[:, 3:4], in_=lc_t[0:nt, 288 + hq:289 + hq], func=ACT.Exp, bias=st_[:, 1:2]),
                   reads=[lc_b, smb], writes=[smb])
                op("dve", lambda e: e.tensor_tensor(out=st_[:, 4:5], in0=st_[:, 2:3], in1=st_[:, 3:4], op=ALU.add), reads=[smb], writes=[smb])
                op("dve", lambda e: e.reciprocal(out=st_[:, 5:6], in_=st_[:, 4:5]), reads=[smb], writes=[smb])
                op("dve", lambda e: e.tensor_scalar_mul(out=Ss, in0=Ss, scalar1=st_[:, 5:6]), reads=[S_b, smb], writes=[S_b])

            def stageB(hq):
                kvh = hq // 4
                S_t, S_b = S_par[hq % 2]
                PT_t, PT_b = PT_par[hq % 2]
                pt2, pb2 = psum()
                off = 0
                for bi, (slot, n) in enumerate(blocks):
                    op("pe", lambda e, n=n, off=off, bi=bi: e.transpose(out=pt2[0:n, bi * 128:bi * 128 + nt],
                                                                        in_=S_t[0:nt, off:off + n],
                                                                        identity=ident_t[0:nt, 0:nt]), reads=[S_b, ident_b], writes=[pb2])
                    off += n
                PTs = PT_t[:, 0:256]
                for bi, (slot, n) in enumerate(blocks):
                    op("act", lambda e, n=n, bi=bi: e.copy(out=PTs[0:n, bi * 128:bi * 128 + nt], in_=pt2[0:n, bi * 128:bi * 128 + nt]),
                       reads=[pb2], writes=[PT_b])
                if hq % 4 == 0:
                    pst["po"], pst["pob"] = psum()
                po, pob = pst["po"], pst["pob"]
                for bi, (slot, n) in enumerate(blocks):
                    op("pe", lambda e, n=n, bi=bi, slot=slot: e.matmul(po[0:64, (hq % 4) * nt:(hq % 4 + 1) * nt], lhsT=vr[slot][0][0:n, kvh * 64:(kvh + 1) * 64],
                                                                        rhs=PTs[0:n, bi * 128:bi * 128 + nt], start=(bi == 0), stop=(bi == len(blocks) - 1)),
                       reads=[vr[slot][1], PT_b], writes=[pob])
                if hq % 4 == 3:
                    op("act", lambda e: e.copy(out=oT[:, hq - 3:hq + 1, 0:nt], in_=po[0:64, 0:4 * nt].rearrange("p (h n) -> p h n", h=4)),
                       reads=[pob], writes=[oT_b])

            stageA(0)
            yield
            for hq in range(16):
                if hq + 1 < 16:
                    stageA(hq + 1)
                    yield
                stageB(hq)
                yield
            op("dve", lambda e: e.tensor_tensor(out=ht_t[0:nt, :], in0=ht_t[0:nt, :], in1=bo_t[0:nt, :], op=ALU.add),
               reads=[ht_b, bo_b], writes=[ht_b])
            wo = a_w_o[j].rearrange("(h d) n -> d h n", d=64)
            for blk in range(4):
                wt, wb = wslot()
                wv = wt[0:64, 0:16 * 256].rearrange("p (h n) -> p h n", h=16)
                fw.dma("sp", wv, wo[:, :, blk * 256:(blk + 1) * 256], writes=[wb], owner=wb)
                pt, pb = psum()
                for hq in range(16):
                    op("pe", lambda e, hq=hq: e.matmul(pt[0:nt, 0:256], lhsT=oT[:, hq, 0:nt], rhs=wv[:, hq, :], start=(hq == 0), stop=(hq == 15)),
                       reads=[wb, oT_b], writes=[pb])
                op("dve", lambda e: e.tensor_tensor(out=ht_t[0:nt, blk * 256:(blk + 1) * 256], in0=ht_t[0:nt, blk * 256:(blk + 1) * 256],
                                                    in1=pt[0:nt, 0:256], op=ALU.add), reads=[ht_b, pb], writes=[ht_b])
                yield
            fw.dma("sp", h_ap(ci + 1, t), ht_t[0:nt, :], reads=[ht_b], writes=[t.hb[ci + 1]], owner=ht_b)

        def peer_a(l, t, ci):
            nt = t.nt
            fw.dma("sp", htp_t[0:nt, :], h_ap(ci, t), reads=hbl(t, ci), writes=[htp_b], owner=htp_b)
            rmsnorm(htp_t, htp_b, nt, gffn_t, gffn_b, xnp_t, xnp_b, sm[0:nt, 0:4], smb)
            to_featmajor(xnp_t, xnp_b, nt, D, xnT_t, xnT_b)
            fw.mark(17)
            qT_t, qT_b = P[0]
            qT = qT_t[:].rearrange("p (c n) -> p c n", c=16)
            S_t, S_b = P[1]
            Sw_t, Sw_b = P[2]
            S3 = S_t[0:nt, :].rearrange("p (c n) -> p c n", c=16)
            Sw3 = Sw_t[0:nt, :].rearrange("p (c n) -> p c n", c=16)
            for blk in range(4):
                wv, wb = load_w(p_w_q[l], 8, blk * 512, 512)
                pt, pb = psum()
                for cc in range(4):
                    for k in range(8):
                        op("pe", lambda e, cc=cc, k=k: e.matmul(pt[:, cc * nt:(cc + 1) * nt], lhsT=wv[:, k, cc * 128:(cc + 1) * 128], rhs=xnT_t[:, k, 0:nt],
                                                                 start=(k == 0), stop=(k == 7)), reads=[wb, xnT_b], writes=[pb])
                op("act", lambda e: e.copy(out=qT[:, blk * 4:blk * 4 + 4, 0:nt], in_=pt[:, 0:4 * nt].rearrange("p (c n) -> p c n", c=4)),
                   reads=[pb], writes=[qT_b])
            fw.mark(18)
            for blk in range(4):
                pt, pb = psum()
                for cc in range(4):
                    c = blk * 4 + cc
                    op("pe", lambda e, cc=cc, c=c: e.matmul(pt[0:nt, cc * 128:(cc + 1) * 128], lhsT=qT[:, c, 0:nt], rhs=sk_t[:, c % 2, :], start=True, stop=True),
                       reads=[qT_b, sk_b], writes=[pb])
                if os.environ.get('K_SKIPC') != '1':
                    op("act", lambda e: e.copy(out=S_t[0:nt, blk * 512:(blk + 1) * 512], in_=pt[0:nt, :]), reads=[pb], writes=[S_b])
                if os.environ.get('K_SKIPC') not in ('1', '2'):
                    op("dve", lambda e: e.tensor_copy(out=Sw_t[0:nt, blk * 512:(blk + 1) * 512], in_=S_t[0:nt, blk * 512:(blk + 1) * 512]), reads=[S_b], writes=[Sw_b])
            fw.mark(20)
            R = [smb]
            vals = sm[0:nt, 256:512].rearrange("p (c k) -> p c k", c=16)
            idxf = sm[0:nt, 512:768].rearrange("p (c k) -> p c k", c=16)
            vs = sm[0:nt, 768:896].rearrange("p (h k) -> p h k", h=8)
            posf = sm[0:nt, 896:1024].rearrange("p (h k) -> p h k", h=8)
            idxu = idx_t[0:nt, 0:256].rearrange("p (c k) -> p c k", c=16)
            posu = idx_t[0:nt, 256:384].rearrange("p (h k) -> p h k", h=8)
            for c in range(16):
                op("dve", lambda e, c=c: e.max(out=vals[:, c, 0:8], in_=Sw3[:, c, :]), reads=[Sw_b], writes=R)
                op("dve", lambda e, c=c: e.match_replace(out=Sw3[:, c, :], in_to_replace=vals[:, c, 0:8], in_values=Sw3[:, c, :], imm_value=-1e30),
                   reads=[Sw_b] + R, writes=[Sw_b])
                op("dve", lambda e, c=c: e.max(out=vals[:, c, 8:16], in_=Sw3[:, c, :]), reads=[Sw_b], writes=R)
                op("dve", lambda e, c=c: e.max_index(out=idxu[:, c, 0:8], in_max=vals[:, c, 0:8], in_values=S3[:, c, :]), reads=[S_b] + R, writes=[idx_b])
                op("dve", lambda e, c=c: e.max_index(out=idxu[:, c, 8:16], in_max=vals[:, c, 8:16], in_values=S3[:, c, :]), reads=[S_b] + R, writes=[idx_b])
            op("dve", lambda e: e.tensor_copy(out=idxf, in_=idxu), reads=[idx_b], writes=R)
            vals4 = sm[0:nt, 256:512].rearrange("p (h s k) -> p h s k", h=8, s=2)
            idxf4 = sm[0:nt, 512:768].rearrange("p (h s k) -> p h s k", h=8, s=2)
            cand_t, cand_b = P[3]
            cw_t, cw_b = P[4]
            T1_t, T1_b = P[5]
            cand = cand_t[0:nt, :].rearrange("p (h a b) -> p h a b", h=8, a=16)
            cand3 = cand_t[0:nt, :].rearrange("p (h n) -> p h n", h=8)
            cw3 = cw_t[0:nt, :].rearrange("p (h n) -> p h n", h=8)
            for h in range(8):
                op("dve", lambda e, h=h: e.tensor_tensor(out=cand[:, h], in0=bc(vals4[:, h, 0, :].unsqueeze(2), [nt, 16, 16]),
                                                         in1=bc(vals4[:, h, 1, :].unsqueeze(1), [nt, 16, 16]), op=ALU.add), reads=R, writes=[cand_b])
            op("dve", lambda e: e.tensor_copy(out=cw_t[0:nt, :], in_=cand_t[0:nt, :]), reads=[cand_b], writes=[cw_b])
            for h in range(8):
                op("dve", lambda e, h=h: e.max(out=vs[:, h, 0:8], in_=cw3[:, h, :]), reads=[cw_b], writes=R)
                op("dve", lambda e, h=h: e.match_replace(out=cw3[:, h, :], in_to_replace=vs[:, h, 0:8], in_values=cw3[:, h, :], imm_value=-1e30),
                   reads=[cw_b] + R, writes=[cw_b])
                op("dve", lambda e, h=h: e.max(out=vs[:, h, 8:16], in_=cw3[:, h, :]), reads=[cw_b], writes=R)
                op("dve", lambda e, h=h: e.max_index(out=posu[:, h, 0:8], in_max=vs[:, h, 0:8], in_values=cand3[:, h, :]), reads=[cand_b] + R, writes=[idx_b])
                op("dve", lambda e, h=h: e.max_index(out=posu[:, h, 8:16], in_max=vs[:, h, 8:16], in_values=cand3[:, h, :]), reads=[cand_b] + R, writes=[idx_b])
            op("dve", lambda e: e.tensor_copy(out=posf, in_=posu), reads=[idx_b], writes=R)
            fw.mark(21)
            T15 = cw_t[0:nt, 0:128 * 15].rearrange("p (s m) -> p s m", m=15)
            posf2 = sm[0:nt, 896:1024]
            af = psm_t[0:nt, 0:128]
            bf = psm_t[0:nt, 128:256]
            e1 = psm_t[0:nt, 256:384]
            e2 = psm_t[0:nt, 384:512]
            gate = psm_t[0:nt, 512:640]
            actv = psm_t[0:nt, 640:768]
            wgt = psm_t[0:nt, 768:896]
            tmp1 = psm_t[0:nt, 896:1024]
            tmp2 = psm_t[0:nt, 1024:1152]
            eidf = psm_t[0:nt, 1152:1280]
            Rb = [psm_b]
            op("dve", lambda e: e.tensor_tensor(out=T15, in0=bc(posf2.unsqueeze(2), [nt, 128, 15]), in1=bc(thr_t[0:nt, :].unsqueeze(1), [nt, 128, 15]), op=ALU.is_ge),
               reads=R + [thr_b], writes=[cw_b])
            op("dve", lambda e: e.reduce_sum(out=af, in_=T15, axis=AX.X), reads=[cw_b], writes=Rb)
            op("dve", lambda e: e.scalar_tensor_tensor(out=bf, in0=af, scalar=-16.0, in1=posf2, op0=ALU.mult, op1=ALU.add), reads=Rb + R, writes=Rb)
            T1 = T1_t[0:nt, :].rearrange("p (h k a) -> p h k a", h=8, k=16)
            for (src, which, dst) in ((af, 0, e1), (bf, 1, e2)):
                src3 = src.rearrange("p (h k) -> p h k", h=8)
                dst3 = dst.rearrange("p (h k) -> p h k", h=8)
                for h in range(8):
                    op("dve", lambda e, h=h, src3=src3: e.tensor_tensor(out=T1[:, h], in0=bc(src3[:, h, :].unsqueeze(2), [nt, 16, 16]),
                                                                        in1=bc(iota_t[0:nt, :].unsqueeze(1), [nt, 16, 16]), op=ALU.is_equal),
                       reads=Rb + [iota_b], writes=[T1_b])
                    op("dve", lambda e, h=h, which=which: e.tensor_tensor(out=T1[:, h], in0=T1[:, h], in1=bc(idxf4[:, h, which, :].unsqueeze(1), [nt, 16, 16]), op=ALU.mult),
                       reads=[T1_b] + R, writes=[T1_b])
                op("dve", lambda e, dst=dst: e.reduce_sum(out=dst, in_=T1_t[0:nt, :].rearrange("p (s a) -> p s a", a=16), axis=AX.X), reads=[T1_b], writes=Rb)
            op("dve", lambda e: e.scalar_tensor_tensor(out=eidf, in0=e1, scalar=128.0, in1=e2, op0=ALU.mult, op1=ALU.add), reads=Rb, writes=Rb)
            op("dve", lambda e: e.tensor_scalar_add(out=eidf, in0=eidf, scalar1=float(l * 16384)), reads=Rb, writes=Rb)
            op("dve", lambda e: e.tensor_copy(out=eid_t[0:nt, :], in_=eidf), reads=Rb, writes=[eid_b])
            fw.mark(22)
            gate3 = gate.rearrange("p (h k) -> p h k", h=8)
            op("dve", lambda e: e.tensor_tensor(out=gate3, in0=vs, in1=bc(vs[:, :, 0:1], [nt, 8, 16]), op=ALU.subtract), reads=R, writes=Rb)
            op("act", lambda e: e.activation(out=gate, in_=gate, func=ACT.Exp), reads=Rb, writes=Rb)
            op("dve", lambda e: e.reduce_sum(out=tmp1[:, 0:8], in_=gate3, axis=AX.X), reads=Rb, writes=Rb)
            op("dve", lambda e: e.reciprocal(out=tmp1[:, 8:16], in_=tmp1[:, 0:8]), reads=Rb, writes=Rb)
            op("dve", lambda e: e.tensor_tensor(out=gate3, in0=gate3, in1=bc(tmp1[:, 8:16].unsqueeze(2), [nt, 8, 16]), op=ALU.mult), reads=Rb, writes=Rb)
            fw.mark(23)
        def peer_b(l, t, ci):
            nt = t.nt
            Rb = [psm_b]
            gate = psm_t[0:nt, 512:640]
            actv = psm_t[0:nt, 640:768]
            wgt = psm_t[0:nt, 768:896]
            tmp1 = psm_t[0:nt, 896:1024]
            g_i = [0]

            def gather(tab, jj):
                gt, gb = G[g_i[0] % NG]
                g_i[0] += 1
                fw.dma("pool", None, None, reads=[eid_b], writes=[gb], owner=gb,
                       fn=lambda e: e.indirect_dma_start(out=gt[0:nt, :], out_offset=None, in_=tab,
                                                         in_offset=bass.IndirectOffsetOnAxis(ap=eid_t[0:nt, jj:jj + 1], axis=0)))
                return gt, gb
            op("dve", lambda e: e.memset(actv, 0.0), writes=Rb)
            for jj in range(128):
                gt, gb = gather(p_u, jj)
                op("dve", lambda e, gt=gt, jj=jj: e.scalar_tensor_tensor(out=gt[0:nt, :], in0=gt[0:nt, :], scalar=1.0, in1=xnp_t[0:nt, :],
                                                                           op0=ALU.mult, op1=ALU.mult, accum_out=actv[:, jj:jj + 1]),
                   reads=[gb, xnp_b], writes=[gb] + Rb)
                yield
            fw.mark(24)
            op("dve", lambda e: e.tensor_tensor(out=tmp1, in0=actv, in1=actv, op=ALU.mult), reads=Rb, writes=Rb)
            op("dve", lambda e: e.tensor_scalar(out=tmp1, in0=tmp1, scalar1=0.044715, scalar2=1.0, op0=ALU.mult, op1=ALU.add), reads=Rb, writes=Rb)
            op("dve", lambda e: e.tensor_tensor(out=tmp1, in0=tmp1, in1=actv, op=ALU.mult), reads=Rb, writes=Rb)
            op("act", lambda e: e.activation(out=tmp1, in_=tmp1, func=ACT.Tanh, scale=0.7978845608028654), reads=Rb, writes=Rb)
            op("dve", lambda e: e.tensor_scalar(out=tmp1, in0=tmp1, scalar1=1.0, scalar2=0.5, op0=ALU.add, op1=ALU.mult), reads=Rb, writes=Rb)
            op("dve", lambda e: e.tensor_tensor(out=tmp1, in0=tmp1, in1=actv, op=ALU.mult), reads=Rb, writes=Rb)
            op("dve", lambda e: e.tensor_tensor(out=wgt, in0=tmp1, in1=gate, op=ALU.mult), reads=Rb, writes=Rb)
            for jj in range(128):
                gt, gb = gather(p_v, jj)
                op("dve", lambda e, gt=gt, jj=jj: e.scalar_tensor_tensor(out=htp_t[0:nt, :], in0=gt[0:nt, :], scalar=wgt[:, jj:jj + 1], in1=htp_t[0:nt, :],
                                                                           op0=ALU.mult, op1=ALU.add), reads=[gb, htp_b] + Rb, writes=[htp_b])
                yield
            fw.dma("sp", h_ap(ci + 1, t), htp_t[0:nt, :], reads=[htp_b], writes=hbl(t, ci + 1), owner=htp_b)
            if l == 3:
                rmsnorm(htp_t, htp_b, nt, gX_t[:, 0:1024], gX_b, xnp_t, xnp_b, psm_t[0:nt, 1280:1284], psm_b)
                dst = y_p[t.row0:t.row0 + nt, :] if t.kind == "p" else (y_s[0:nt, :] if t.kind == "sp" else y_s[t.idx * TS:(t.idx + 1) * TS, :])
                fw.dma("sp", dst, xnp_t[0:nt, :], reads=[xnp_b], owner=xnp_b)
                outs_b.append(xnp_b)

        outs_b = []
        pending = [None]
        RATIO = {(True, 'p'): float(os.environ.get('K_RM', '2.4')), (True, 's'): float(os.environ.get('K_RMS', '3.8')),
                 (False, 'p'): float(os.environ.get('K_RA', '6.1')), (False, 's'): float(os.environ.get('K_RAS', '6.1'))}

        def drain2(g1, g2, r=1.0):
            n1 = n2 = 0
            acc = 0.0
            while g1 is not None or g2 is not None:
                if g1 is not None:
                    try:
                        next(g1)
                        n1 += 1
                    except StopIteration:
                        g1 = None
                    acc += r
                else:
                    acc += 1.0
                while g2 is not None and acc >= 1.0:
                    acc -= 1.0
                    try:
                        next(g2)
                        n2 += 1
                    except StopIteration:
                        g2 = None
                if g2 is None:
                    acc = 0.0
            if os.environ.get('K_CNT'):
                print("drain2 yields mixer=%d peer=%d" % (n1, n2))

        for l in range(nlayers):
            load_bc(gmix_t[:], gmix_b, norm_mix[l])
            load_bc(gffn_t[:], gffn_b, norm_ffn[l])
            fw.dma("sp", sk_t[:, 0, :], p_k1T[l], writes=[sk_b], owner=sk_b)
            fw.dma("sp", sk_t[:, 1, :], p_k2T[l], writes=[sk_b], owner=sk_b)
            if l < 2:
                load_bc(gX_t[:], gX_b, m_norm[l])
                load_bc(lc_t[:, 0:32], lc_b, m_dt_bias[l])
                load_bc(lc_t[:, 32:64], lc_b, m_a_log[l])
                load_bc(lc_t[:, 64:96], lc_b, m_d_skip[l])
                fw.dma("sp", lc_t[:, 128:256], m_conv_w[l].rearrange("p c k -> p (c k)"), writes=[lc_b], owner=lc_b)
                fw.dma("sp", lc_t[:, 256:288], m_conv_b[l], writes=[lc_b], owner=lc_b)
                op("act", lambda e: e.activation(out=lc_t[:, 32:64], in_=lc_t[:, 32:64], func=ACT.Exp), reads=[lc_b], writes=[lc_b])
                op("dve", lambda e: e.tensor_scalar_mul(out=lc_t[:, 32:64], in0=lc_t[:, 32:64], scalar1=-1.0), reads=[lc_b], writes=[lc_b])
            else:
                j = l - 2
                if j == 0:
                    load_bc(gX_t[:, 0:1024], gX_b, norm_kv[0])
                    load_bc(gX_t[:, 1024:1536], gX_b, a_b_kv[0])
                    fw.dma("sp", lc_t[0:64, 96:100], a_b_kT, writes=[lc_b], owner=lc_b)
                load_bc(bo_t[:], bo_b, a_b_o[j])
                fw.dma("sp", lc_t[0:64, 112:128], a_b_qT[j], writes=[lc_b], owner=lc_b)
                op("dve", lambda e: e.tensor_scalar_mul(out=lc_t[0:64, 112:128], in0=lc_t[0:64, 112:128], scalar1=0.125), reads=[lc_b], writes=[lc_b])
                load_bc(lc_t[:, 288:304], lc_b, a_sinks[j])
            if l == 3:
                load_bc(gX_t[:, 0:1024], gX_b, norm_final[0])
            try:
                for t in tiles:
                    ci = 2 * l
                    mix = mamba_tile(l, t, ci) if l < 2 else attn_tile(l, t, ci)
                    drain2(mix, pending[0], RATIO[(l < 2, t.kind)])
                    pending[0] = None
                    pt_ = t
                    if t.kind == "s":
                        if t is not stiles[-1]:
                            continue
                        pt_ = speer
                    if not os.environ.get('K_NOPEER'):
                        peer_a(l, pt_, ci + 1)
                        pending[0] = peer_b(l, pt_, ci + 1)
                        if os.environ.get('K_NOILV') or l >= ILV_MAXL:
                            drain2(pending[0], None)
                            pending[0] = None
            except Stop:
                break
        if pending[0] is not None:
            drain2(pending[0], None)
        fw.finish(fw.allbufs, "sp")
        fw.finish(fw.allbufs, "act")
        print("program: ninst=%d nwait=%d sems=%d" % (fw.ninst, fw.nwait, len(fw.sems)))
    return nc


_CACHE = {}


def make_in_maps(inp):
    f = lambda a: np.ascontiguousarray(np.asarray(a, dtype=np.float32))
    shared = {
        "norm_mix": f(inp["norm_mix"]), "norm_ffn": f(inp["norm_ffn"]),
        "norm_kv": f(inp["norm_kv"]).reshape(1, D), "norm_final": f(inp["norm_final"]).reshape(1, D),
        "m_w_in": f(inp["m_w_in"]),
        "m_conv_w": f(np.asarray(inp["m_conv_w"]).reshape(2, 4, 32, 128).transpose(0, 3, 2, 1)),
        "m_conv_b": f(np.asarray(inp["m_conv_b"]).reshape(2, 32, 128).transpose(0, 2, 1)),
        "m_dt_bias": f(inp["m_dt_bias"]), "m_a_log": f(inp["m_a_log"]), "m_d_skip": f(inp["m_d_skip"]),
        "m_norm": f(inp["m_norm"]), "m_w_out": f(inp["m_w_out"]),
        "a_w_kv": f(inp["a_w_kv"]), "a_b_kv": f(inp["a_b_kv"]).reshape(1, 512),
        "a_b_kT": f(np.asarray(inp["a_b_kv"])[:256].reshape(4, 64).T),
        "a_w_q": f(inp["a_w_q"]),
        "a_b_qT": f(np.asarray(inp["a_b_q"]).reshape(2, 16, 64).transpose(0, 2, 1)),
        "a_sinks": f(inp["a_sinks"]), "a_w_o": f(inp["a_w_o"]), "a_b_o": f(inp["a_b_o"]),
        "p_w_q": f(inp["p_w_q"]),
        "p_k1T": f(np.asarray(inp["p_sub_k1"]).transpose(0, 2, 1)),
        "p_k2T": f(np.asarray(inp["p_sub_k2"]).transpose(0, 2, 1)),
        "p_u": f(inp["p_u"]).reshape(4 * 16384, D), "p_v": f(inp["p_v"]).reshape(4 * 16384, D),
    }
    xp = np.asarray(inp["x_prompt"], dtype=np.float32)
    xs = np.asarray(inp["x_sample"], dtype=np.float32)
    ssm = np.asarray(inp["state_ssm"], dtype=np.float32)
    conv = np.asarray(inp["state_conv"], dtype=np.float32)
    ckw = np.asarray(inp["cache_k_win"], dtype=np.float32)
    cvw = np.asarray(inp["cache_v_win"], dtype=np.float32)
    maps = []
    for c in range(8):
        m = dict(shared)
        m["x_p"] = f(xp[c])
        m["x_s"] = f(xs[4 * c:4 * c + 4].reshape(NSS * TS, D))
        m["st_ssm"] = f(ssm[:, 4 * c:4 * c + 4].reshape(2, NSS, 2048, 128))
        m["st_conv"] = f(conv[:, 4 * c:4 * c + 4])
        m["ck"] = f(ckw[4 * c:4 * c + 4].reshape(NSS, 128, 256))
        m["cv"] = f(cvw[4 * c:4 * c + 4].reshape(NSS, 128, 256))
        maps.append(m)
    return maps


def assemble(results):
    r = results
    y_prompt = np.stack([r[c]["y_p"] for c in range(8)], 0)
    y_sample = np.concatenate([r[c]["y_s"].reshape(NSS, TS, D) for c in range(8)], 0)
    pr_ssm = np.stack([r[c]["o_pssm"].reshape(2, 32, 64, 128) for c in range(8)], 1)
    pr_conv = np.stack([r[c]["o_pconv"] for c in range(8)], 1)
    pr_k = np.stack([r[c]["o_pk"].reshape(128, 4, 64) for c in range(8)], 0)
    pr_v = np.stack([r[c]["o_pv"].reshape(128, 4, 64) for c in range(8)], 0)
    sm_ssm = np.concatenate([r[c]["o_sssm"].reshape(2, NSS, 32, 64, 128) for c in range(8)], 1)
    sm_conv = np.concatenate([r[c]["o_sconv"] for c in range(8)], 1)
    sm_k = np.concatenate([r[c]["o_sk"].reshape(NSS, 128, 4, 64) for c in range(8)], 0)
    sm_v = np.concatenate([r[c]["o_sv"].reshape(NSS, 128, 4, 64) for c in range(8)], 0)
    outs = (y_prompt, y_sample, pr_ssm, pr_conv, pr_k, pr_v, sm_ssm, sm_conv, sm_k, sm_v)
    return tuple(np.ascontiguousarray(o, dtype=np.float32) for o in outs)


def kernel(**inputs):
    if "nc" not in _CACHE:
        _CACHE["nc"] = build_program()
    nc = _CACHE["nc"]
    maps = make_in_maps(inputs)
    res = run_bass_kernel_spmd(nc, maps, core_ids=list(range(8)))
    return assemble(res.results)
```
